# Optimizing a Trainium2 kernel written in Bass

```python
import math
import jax, jax.numpy as jnp
from jax import lax
import numpy as np

D_MODEL = 2048
BATCH = 4
SEQ = 4096
DEPTH = 1

CHUNK = 64
D_MIX = 2 * D_MODEL
D_SSD = D_MIX // 2
D_LRU = D_MIX - D_SSD
SSD_HEAD_DIM = 64
SSD_HEADS = D_SSD // SSD_HEAD_DIM
SSD_GROUPS = 8
SSD_STATE = 128
D_XBC = D_SSD + 2 * SSD_GROUPS * SSD_STATE
LRU_HEADS = 16
LRU_BLOCK = D_LRU // LRU_HEADS
LRU_C = 8.0
CONV_WIDTH = 4
D_FF = 4 * D_MODEL
D_IN = D_SSD + D_XBC + SSD_HEADS + 2 * D_LRU
EPS = 1e-6

kernel_name = "hybrid_ssd_rglru_parallel_block"


def rms_norm(x, g):
    xf = x.astype(jnp.float32)
    y = xf * lax.rsqrt(jnp.mean(xf * xf, axis=-1, keepdims=True) + EPS)
    return (y * g.astype(jnp.float32)).astype(x.dtype)


def causal_dwconv(x, w, b):
    c = x.shape[-1]
    y = lax.conv_general_dilated(
        x, w[:, None, :].astype(x.dtype), window_strides=(1,),
        padding=[(w.shape[0] - 1, 0)],
        dimension_numbers=("NWC", "WIO", "NWC"), feature_group_count=c)
    return y + b.astype(x.dtype)


def ssd_scan(xh, dt, a, bm, cm, d_skip):
    b_, l_, h_, p_ = xh.shape
    g_, n_ = bm.shape[2], bm.shape[3]
    k_ = h_ // g_
    c_ = l_ // CHUNK
    x6 = (xh * dt[..., None]).reshape(b_, c_, CHUNK, g_, k_, p_)
    adt = (dt * a).reshape(b_, c_, CHUNK, g_, k_)
    acs = jnp.cumsum(adt, axis=2)
    bc = bm.reshape(b_, c_, CHUNK, g_, n_)
    cc = cm.reshape(b_, c_, CHUNK, g_, n_)
    causal = jnp.tril(jnp.ones((CHUNK, CHUNK), dtype=bool))[None, None, :, :, None, None]
    seg = acs[:, :, :, None] - acs[:, :, None, :]
    decay = jnp.exp(jnp.where(causal, seg, -jnp.inf))
    scores = jnp.einsum("bclgn,bcsgn->bclsg", cc, bc)
    y_diag = jnp.einsum("bclsg,bclsgk,bcsgkp->bclgkp", scores, decay, x6)
    decay_to_end = jnp.exp(acs[:, :, -1:] - acs)
    states = jnp.einsum("bcsgn,bcsgk,bcsgkp->bcgkpn", bc, decay_to_end, x6)
    chunk_decay = jnp.exp(acs[:, :, -1])

    def step(h, inp):
        s, dcy = inp
        return h * dcy[..., None, None] + s, h

    h0 = jnp.zeros((b_, g_, k_, p_, n_), dtype=xh.dtype)
    _, prev = lax.scan(step, h0, (jnp.moveaxis(states, 1, 0), jnp.moveaxis(chunk_decay, 1, 0)))
    prev = jnp.moveaxis(prev, 0, 1)
    y_off = jnp.einsum("bclgn,bcgkpn,bclgk->bclgkp", cc, prev, jnp.exp(acs))
    y = (y_diag + y_off).reshape(b_, l_, h_, p_)
    return y + xh * d_skip[:, None]


def rg_lru(xl, w_a, b_a, w_x, b_x, lam):
    b_, l_, _ = xl.shape
    xb = xl.reshape(b_, l_, LRU_HEADS, LRU_BLOCK)
    r = jax.nn.sigmoid(jnp.einsum("blhi,hij->blhj", xb, w_a) + b_a).reshape(b_, l_, D_LRU)
    i = jax.nn.sigmoid(jnp.einsum("blhi,hij->blhj", xb, w_x) + b_x).reshape(b_, l_, D_LRU)
    log_a = -LRU_C * r * jax.nn.softplus(-lam)
    a = jnp.exp(log_a)
    u = jnp.sqrt(-jnp.expm1(2.0 * log_a)) * (i * xl)

    def combine(e1, e2):
        a1, b1 = e1
        a2, b2 = e2
        return a1 * a2, a2 * b1 + b2

    _, h = lax.associative_scan(combine, (a, u), axis=1)
    return h


def setup_inputs(seed: int = 0) -> dict:
    key = jax.random.key(seed)
    ks = jax.random.split(key, 32)
    f32 = jnp.float32
    nrm = lambda k, s, sc: jax.random.normal(k, s, f32) * sc
    gain = lambda k, s: 1.0 + 0.02 * jax.random.normal(k, s, f32)
    L = DEPTH
    dt0 = jnp.exp(jax.random.uniform(ks[10], (L, SSD_HEADS), f32, math.log(1e-3), math.log(1e-1)))
    a0 = jax.random.uniform(ks[14], (L, D_LRU), f32, 0.9, 0.999)
    a_base = jnp.exp(jnp.log(a0) / LRU_C)
    return {
        "x": jax.random.normal(ks[0], (BATCH, SEQ, D_MODEL), f32),
        "pre_mix_norm": gain(ks[1], (L, D_MODEL)),
        "w_in": nrm(ks[2], (L, D_MODEL, D_IN), D_MODEL ** -0.5),
        "ssd_conv_w": nrm(ks[3], (L, CONV_WIDTH, D_XBC), CONV_WIDTH ** -0.5),
        "ssd_conv_b": nrm(ks[4], (L, D_XBC), 0.01),
        "ssd_dt_bias": dt0 + jnp.log(-jnp.expm1(-dt0)),
        "ssd_a_log": jnp.log(jax.random.uniform(ks[11], (L, SSD_HEADS), f32, 1.0, 16.0)),
        "ssd_d": gain(ks[12], (L, SSD_HEADS)),
        "ssd_norm": gain(ks[13], (L, D_SSD)),
        "lru_conv_w": nrm(ks[5], (L, CONV_WIDTH, D_LRU), CONV_WIDTH ** -0.5),
        "lru_conv_b": nrm(ks[6], (L, D_LRU), 0.01),
        "lru_w_a": nrm(ks[7], (L, LRU_HEADS, LRU_BLOCK, LRU_BLOCK), LRU_BLOCK ** -0.5),
        "lru_b_a": nrm(ks[8], (L, LRU_HEADS, LRU_BLOCK), 0.01),
        "lru_w_x": nrm(ks[9], (L, LRU_HEADS, LRU_BLOCK, LRU_BLOCK), LRU_BLOCK ** -0.5),
        "lru_b_x": nrm(ks[15], (L, LRU_HEADS, LRU_BLOCK), 0.01),
        "lru_lambda": jnp.log(a_base) - jnp.log1p(-a_base),
        "lru_norm": gain(ks[16], (L, D_LRU)),
        "w_out": nrm(ks[17], (L, D_MIX, D_MODEL), D_MIX ** -0.5),
        "post_mix_norm": gain(ks[18], (L, D_MODEL)),
        "pre_mlp_norm": gain(ks[19], (L, D_MODEL)),
        "w_mlp_in": nrm(ks[20], (L, D_MODEL, D_FF), D_MODEL ** -0.5),
        "w_mlp_out": nrm(ks[21], (L, D_FF, D_MODEL), D_FF ** -0.5),
        "post_mlp_norm": gain(ks[22], (L, D_MODEL)),
    }


def reference(x, pre_mix_norm, w_in, ssd_conv_w, ssd_conv_b, ssd_dt_bias, ssd_a_log, ssd_d,
              ssd_norm, lru_conv_w, lru_conv_b, lru_w_a, lru_b_a, lru_w_x, lru_b_x, lru_lambda,
              lru_norm, w_out, post_mix_norm, pre_mlp_norm, w_mlp_in, w_mlp_out, post_mlp_norm):
    f32 = jnp.float32
    b_, l_, _ = x.shape
    split_at = np.cumsum([D_SSD, D_XBC, SSD_HEADS, D_LRU]).tolist()
    for li in range(DEPTH):
        h = rms_norm(x, pre_mix_norm[li])
        proj = h @ w_in[li].astype(h.dtype)
        z, xbc, dt_raw, gate_lru, x_lru = jnp.split(proj, split_at, axis=-1)

        xbc = jax.nn.silu(causal_dwconv(xbc, ssd_conv_w[li], ssd_conv_b[li])).astype(f32)
        xs, bm, cm = jnp.split(xbc, [D_SSD, D_SSD + SSD_GROUPS * SSD_STATE], axis=-1)
        dt = jax.nn.softplus(dt_raw.astype(f32) + ssd_dt_bias[li].astype(f32))
        a = -jnp.exp(ssd_a_log[li].astype(f32))
        y_ssd = ssd_scan(xs.reshape(b_, l_, SSD_HEADS, SSD_HEAD_DIM), dt, a,
                         bm.reshape(b_, l_, SSD_GROUPS, SSD_STATE),
                         cm.reshape(b_, l_, SSD_GROUPS, SSD_STATE), ssd_d[li].astype(f32))
        y_ssd = y_ssd.reshape(b_, l_, D_SSD) * jax.nn.silu(z.astype(f32))
        yg = y_ssd.reshape(b_, l_, SSD_GROUPS, D_SSD // SSD_GROUPS)
        yg = yg * lax.rsqrt(jnp.mean(yg * yg, axis=-1, keepdims=True) + EPS)
        y_ssd = (yg.reshape(b_, l_, D_SSD) * ssd_norm[li].astype(f32)).astype(x.dtype)

        xl = causal_dwconv(x_lru, lru_conv_w[li], lru_conv_b[li]).astype(f32)
        hl = rg_lru(xl, lru_w_a[li].astype(f32), lru_b_a[li].astype(f32),
                    lru_w_x[li].astype(f32), lru_b_x[li].astype(f32), lru_lambda[li].astype(f32))
        y_lru = rms_norm(hl * jax.nn.gelu(gate_lru.astype(f32)), lru_norm[li]).astype(x.dtype)

        mix = jnp.concatenate([y_ssd, y_lru], axis=-1) @ w_out[li].astype(x.dtype)
        x = x + rms_norm(mix, post_mix_norm[li])

        hm = rms_norm(x, pre_mlp_norm[li]) @ w_mlp_in[li].astype(x.dtype)
        hm = jnp.square(jax.nn.relu(hm)) @ w_mlp_out[li].astype(x.dtype)
        x = x + rms_norm(hm, post_mlp_norm[li])
    return x
```

```python
import contextlib
import numpy as np
import concourse.bass as bass
import concourse.mybir as mybir
from concourse.bass_utils import run_bass_kernel_spmd

F32 = mybir.dt.float32
BF16 = mybir.dt.bfloat16
AF = mybir.ActivationFunctionType
ALU = mybir.AluOpType
EPS = 1e-6


class Cfg:
    def __init__(self, D=2048, H=32, G=8, DL=2048, DFF=8192, TT=1024, NMAIN=2, NPRE=2, FC=1024):
        self.D, self.H, self.G, self.DL, self.DFF = D, H, G, DL, DFF
        self.TT, self.NMAIN, self.NPRE, self.FC = TT, NMAIN, NPRE, FC
        self.P, self.N = 64, 128
        self.DS = H * 64
        self.HG = H // G
        assert self.HG == 4
        self.KD = D // 128
        self.KS = self.DS // 128
        self.DXBC = self.DS + 2 * G * 128
        self.KX = self.DXBC // 128
        self.DIN = self.DS + self.DXBC + H + 2 * DL
        self.NBLK = DL // 128
        self.DMIX = self.DS + DL
        self.KM = self.DMIX // 128
        self.KF = DFF // 128
        self.NQ = TT // 128
        self.HW = min(512, TT)
        self.NH = TT // self.HW
        self.oxs = self.DS
        self.oB = 2 * self.DS
        self.oC = 2 * self.DS + G * 128
        self.odt = self.DS + self.DXBC
        self.ogate = self.odt + H
        self.oxl = self.ogate + DL


class Sched:
    ENGS = ("pe", "act", "dve", "pool", "sp")

    def __init__(self, nc):
        self.nc = nc
        self.ops = []
        self.last_w = {}
        self.readers = {}
        self.dma_cnt = {}

    def op(self, eng, fn, reads=(), writes=(), dma_key=None):
        idx = len(self.ops)
        deps = set()
        for t in reads:
            w = self.last_w.get(t)
            if w is not None:
                deps.add(w)
        for t in writes:
            w = self.last_w.get(t)
            if w is not None:
                deps.add(w)
            for r in self.readers.get(t, {}).values():
                deps.update(r)
        deps.discard(idx)
        o = dict(eng=eng, fn=fn, deps=deps, dma=dma_key is not None, key=dma_key, ms=False)
        if dma_key is not None:
            n = self.dma_cnt.get(dma_key, 0) + 1
            self.dma_cnt[dma_key] = n
            o["dval"] = 16 * n
        self.ops.append(o)
        for t in reads:
            d = self.readers.setdefault(t, {})
            if dma_key is not None:
                d.setdefault("dma", []).append(idx)
            else:
                d[eng] = [idx]
        for t in writes:
            self.last_w[t] = idx
            self.readers[t] = {}
        return idx

    def emit(self):
        nc, ops = self.nc, self.ops
        for o in ops:
            for d in o["deps"]:
                p = ops[d]
                if p["dma"]:
                    continue
                if p["eng"] == "pe" and o["eng"] == "pe" and not o["dma"]:
                    continue
                p["ms"] = True
        KSEM = 8
        cnt = {e: 0 for e in self.ENGS}
        for o in ops:
            if o["ms"]:
                o["msi"] = cnt[o["eng"]]
                cnt[o["eng"]] += 1
        self.ms_counts = cnt
        with contextlib.ExitStack() as es:
            esem = {e: [es.enter_context(nc.semaphore("S_%s%d" % (e, i))) for i in range(KSEM)] for e in self.ENGS if cnt[e] > 0}
            dsem = {}
            for i, (k, n) in enumerate(self.dma_cnt.items()):
                r = 1 if k in ("setup", "setup_p") else max(1, (n * 16 + 479) // 480)
                dsem[k] = [es.enter_context(nc.semaphore("D_%d_%d" % (i, j))) for j in range(r)]
            block = es.enter_context(nc.Block())

            def semval(p):
                if p["dma"]:
                    lst = dsem[p["key"]]
                    if p["key"] in ("setup", "setup_p"):
                        return lst[0], 16 * self.dma_cnt[p["key"]]
                    n = p["dval"] // 16 - 1
                    return lst[n % len(lst)], 16 * (n // len(lst) + 1)
                i = p["msi"]
                return esem[p["eng"]][i % KSEM], i // KSEM + 1

            def run(engname, eng):
                waited = {}
                for o in ops:
                    if o["eng"] != engname:
                        continue
                    need = {}
                    for d in o["deps"]:
                        p = ops[d]
                        if (not p["dma"]) and p["eng"] == "pe" and engname == "pe" and not o["dma"]:
                            continue
                        s, v = semval(p)
                        if need.get(id(s), (None, 0))[1] < v:
                            need[id(s)] = (s, v)
                    for s, v in need.values():
                        if waited.get(id(s), 0) < v:
                            eng.wait_ge(s, v)
                            waited[id(s)] = v
                    ins = o["fn"](eng)
                    if o["dma"]:
                        ins.then_inc(semval(o)[0], 16)
                    elif o["ms"]:
                        ins.then_inc(semval(o)[0], 1)

            block.tensor(lambda e: run("pe", e))
            block.scalar(lambda e: run("act", e))
            block.vector(lambda e: run("dve", e))
            block.gpsimd(lambda e: run("pool", e))
            block.sync(lambda e: run("sp", e))


def build(cfg):
    c = cfg
    D, H, G, TT, KD, KS, KX, KM, NQ, NH, HW, NBLK = c.D, c.H, c.G, c.TT, c.KD, c.KS, c.KX, c.KM, c.NQ, c.NH, c.HW, c.NBLK
    nc = bass.Bass("TRN2", target_bir_lowering=False)
    din = lambda name, shape: nc.dram_tensor(name, list(shape), F32, kind="ExternalInput").ap()
    xm = din("xm", [c.NMAIN * TT, D])
    xp = din("xp", [max(c.NPRE, 1) * TT, D])
    flag_d = din("flag", [128, 1])
    w_in = din("w_in", [D, c.DIN])
    w_out = din("w_out", [c.DMIX, D])
    w1 = din("w1", [D, c.DFF])
    w2 = din("w2", [c.DFF, D])
    lwa_d = din("lwa", [NBLK, 128, 128])
    lwx_d = din("lwx", [NBLK, 128, 128])
    vec_d = {}
    vec_shapes = dict(g_pre=[128, KD], g_post=[128, KD], g_mlp=[128, KD], g_pmlp=[128, KD],
                      cws=[128, KX * 4], cbs=[128, KX], cwl=[128, NBLK * 4], cbl=[128, NBLK],
                      lba=[128, NBLK], lbx=[128, NBLK], lam=[128, NBLK], lnorm=[128, NBLK],
                      snorm=[128, KS], dtb=[128, H], alog=[128, H], dsk=[128, H])
    for k, shp in vec_shapes.items():
        vec_d[k] = din(k, shp)
    out = nc.dram_tensor("out", [c.NMAIN * TT, D], F32, kind="ExternalOutput").ap()
    msT = nc.dram_tensor("msT", [KD, 128, TT], F32).ap()
    x1s = nc.dram_tensor("x1s", [KD, 128, TT], F32).ap()

    es = contextlib.ExitStack()
    with es:
        sb = lambda name, shape, dt: es.enter_context(nc.sbuf_tensor(name, list(shape), dt))
        S = Sched(nc)
        if getattr(cfg, "PAD", 0):
            sb("pad", [128, cfg.PAD // 4], F32)
        def ACT(out_, in_, func, r, w, bias=None, scale=None, accum=None):
            kw = {}
            if bias is not None:
                kw["bias"] = bias
            if scale is not None:
                kw["scale"] = scale
            if accum is not None:
                kw["accum_out"] = accum
            S.op("act", lambda e: e.activation(out=out_, in_=in_, func=func, **kw), r, w)

        def TTn(eng, out_, in0, in1, op, r, w):
            S.op(eng, lambda e: e.tensor_tensor(out=out_, in0=in0, in1=in1, op=op), r, w)

        def TS(eng, out_, in0, s1, s2, op0, op1, r, w):
            if s2 is None:
                S.op(eng, lambda e: e.tensor_scalar(out=out_, in0=in0, scalar1=s1, scalar2=None, op0=op0), r, w)
            else:
                S.op(eng, lambda e: e.tensor_scalar(out=out_, in0=in0, scalar1=s1, scalar2=s2, op0=op0, op1=op1), r, w)

        def STT(out_, in0, scalar, in1, op0, op1, r, w):
            S.op("dve", lambda e: e.scalar_tensor_tensor(out=out_, in0=in0, scalar=scalar, in1=in1, op0=op0, op1=op1), r, w)

        def CP(eng, out_, in_, r, w):
            if eng == "act":
                S.op("act", lambda e: e.activation(out=out_, in_=in_, func=AF.Copy), r, w)
            else:
                S.op(eng, lambda e: e.tensor_copy(out=out_, in_=in_), r, w)

        def MM(out_, lhsT, rhs, start, stop, r, w):
            S.op("pe", lambda e: e.matmul(out_, lhsT=lhsT, rhs=rhs, start=start, stop=stop), r, w)

        def TR(out_, in_, ident, r, w):
            S.op("pe", lambda e: e.transpose(out_, in_, ident), r, w)

        def DMA(q, out_, in_, r, w, key):
            S.op(q, lambda e: e.dma_start(out=out_, in_=in_), r, w, dma_key=key)

        def bc(ap, axis, shape):
            return ap.unsqueeze(axis).to_broadcast(list(shape))

        identb = sb("identb", [128, 128], BF16)
        identf = sb("identf", [128, 128], F32)
        onesb = sb("onesb", [128, 128], BF16)
        trib = sb("trib", [128, 128], BF16)
        negm = sb("negm", [128, 128], F32)
        vec = {k: sb("v_" + k, shp, F32) for k, shp in vec_shapes.items()}
        flag = sb("flag_sb", [128, 1], F32)
        clru = sb("clru", [128, NBLK], F32)
        a_bc = sb("a_bc", [128, H], F32)
        wdt = sb("wdt", [128, KD, H], BF16)
        lwa = sb("lwa_sb", [128, NBLK, 128], BF16)
        lwx = sb("lwx_sb", [128, NBLK, 128], BF16)
        prev = sb("prev", [128, G * 256], F32)
        prevb = sb("prevb", [128, NQ, 256], BF16)
        hcar = sb("hcar", [128, NBLK], F32)
        car_s = sb("car_s", [128, KX, 3], F32)
        car_l = sb("car_l", [128, NBLK, 3], F32)
        QH = NQ * H
        dtf = {k: sb("dt_" + k, [128, NQ, H], F32) for k in ("dt", "adt", "acs", "ea", "cdb", "dtde", "t0", "t1", "t2")}
        adt_hi = sb("adt_hi", [128, NQ, H], BF16)
        adt_lo = sb("adt_lo", [128, NQ, H], BF16)
        ssq_x = sb("ssq_x", [128, NQ], F32)
        RAK = max(KD, (KM + 1) // 2)
        RA = sb("RA", [128, RAK * TT], F32)
        vT = RA[:].bitcast(BF16).rearrange("p (k t) -> p k t", t=TT)
        accT = RA[:].rearrange("p (k t) -> p k t", t=TT)
        hT = sb("hT", [128, KD, TT], BF16)
        OVW = max(c.FC // 128 * TT // 2, D + D // 2)
        ovl = sb("ovl", [128, OVW], F32)
        xt = ovl[:, 0:D]
        xn = ovl[:, D:D + D // 2].bitcast(BF16)
        actT = ovl[:, 0:c.FC // 128 * TT // 2].bitcast(BF16).rearrange("p (k t) -> p k t", t=TT)
        NW = 3
        wbuf = [sb("wbuf%d" % i, [128, KD, 128], BF16) for i in range(NW)]
        NF = 8
        f4 = [sb("f4_%d" % i, [128, TT + 4], F32) for i in range(NF)]
        h2 = [sb("h2_%d" % i, [128, TT], BF16) for i in range(3)]
        sm = {k: sb("sm_" + k, shp, dt) for k, (shp, dt) in dict(
            xs=([128, 256], F32), x6=([128, 256], BF16), x6e=([128, 256], BF16), bt=([128, 128], BF16),
            xhi=([128, 4, 128], BF16), xlo=([128, 4, 128], BF16), t=([128, 4, 128], F32),
            mt=([128, 4, 128], BF16), y=([128, 256], F32)).items()}
        ps = lambda name: es.enter_context(nc.psum_tensor(name, [128, 1024], F32))
        big = [ps("big0"), ps("big1")]
        aux0, aux1 = ps("aux0"), ps("aux1")
        bigi = [0]

        def nextbig():
            i = bigi[0] % 2
            bigi[0] += 1
            return big[i], [("big%d" % i, b) for b in range((TT + 511) // 512)]

        wi = [0]

        def nextw():
            i = wi[0] % NW
            wi[0] += 1
            return wbuf[i], "wbuf%d" % i

        for k in vec_shapes:
            DMA("sp", vec[k][:], vec_d[k][:], [], ["v_" + k], "setup")
        DMA("sp", flag[:], flag_d[:], [], ["flag"], "setup")
        DMA("pool", wdt[:], w_in[:, c.odt:c.odt + H].rearrange("(k p) h -> p k h", p=128), [], ["wdt"], "setup_p")
        DMA("pool", lwa[:], lwa_d.rearrange("b i j -> i b j"), [], ["lwa"], "setup_p")
        DMA("pool", lwx[:], lwx_d.rearrange("b i j -> i b j"), [], ["lwx"], "setup_p")

        def mask_const(t, tok, fill, pat, cm, op):
            S.op("pool", lambda e: e.affine_select(out=t[:], in_=t[:], pattern=pat, compare_op=op, fill=fill,
                                                   base=0, channel_multiplier=cm), [tok], [tok])

        S.op("pool", lambda e: e.memset(identb[:], 1.0), [], ["identb"])
        mask_const(identb, "identb", 0.0, [[-1, 128]], 1, ALU.is_equal)
        S.op("pool", lambda e: e.memset(identf[:], 1.0), [], ["identf"])
        mask_const(identf, "identf", 0.0, [[-1, 128]], 1, ALU.is_equal)
        S.op("pool", lambda e: e.memset(onesb[:], 1.0), [], ["onesb"])
        S.op("pool", lambda e: e.memset(trib[:], 1.0), [], ["trib"])
        mask_const(trib, "trib", 0.0, [[1, 128]], -1, ALU.is_ge)
        S.op("pool", lambda e: e.memset(negm[:], 0.0), [], ["negm"])
        mask_const(negm, "negm", -30000.0, [[1, 128]], -1, ALU.is_ge)
        S.op("pool", lambda e: e.memset(prev[:], 0.0), [], ["prev"])
        S.op("pool", lambda e: e.memset(hcar[:], 0.0), [], ["hcar"])
        S.op("pool", lambda e: e.memset(car_s[:], 0.0), [], ["car_s"])
        S.op("pool", lambda e: e.memset(car_l[:], 0.0), [], ["car_l"])

        def log1p_small(out_, e_, tmpw, tmpq, tok_o, tok_e, tok_w, tok_q):
            TS("dve", tmpw, e_, 2.0, None, ALU.add, None, [tok_e], [tok_w])
            S.op("dve", lambda e: e.reciprocal(out=tmpw, in_=tmpw), [tok_w], [tok_w])
            TTn("dve", tmpw, tmpw, e_, ALU.mult, [tok_w, tok_e], [tok_w])
            TTn("dve", out_, tmpw, tmpw, ALU.mult, [tok_w], [tok_o])
            TS("dve", tmpq, out_, 1.0 / 11.0, None, ALU.mult, None, [tok_o], [tok_q])
            for cst in (1.0 / 9.0, 1.0 / 7.0, 1.0 / 5.0, 1.0 / 3.0):
                STT(tmpq, tmpq, cst, out_, ALU.add, ALU.mult, [tok_q, tok_o], [tok_q])
            TS("dve", tmpq, tmpq, 1.0, None, ALU.add, None, [tok_q], [tok_q])
            TTn("dve", tmpq, tmpq, tmpw, ALU.mult, [tok_q, tok_w], [tok_q])
            TS("dve", out_, tmpq, 2.0, None, ALU.mult, None, [tok_q], [tok_o])

        def softplus(out_, x_, ta, tb, tcc, tok_o, tok_x, tok_a, tok_b, tok_c):
            TS("dve", ta, x_, -1.0, None, ALU.mult, None, [tok_x], [tok_a])
            TTn("dve", ta, ta, x_, ALU.max, [tok_a, tok_x], [tok_a])
            ACT(ta, ta, AF.Exp, [tok_a], [tok_a], scale=-1.0)
            log1p_small(tb, ta, tcc, out_, tok_b, tok_a, tok_c, tok_o)
            TS("dve", ta, x_, 0.0, None, ALU.max, None, [tok_x], [tok_a])
            TTn("dve", out_, ta, tb, ALU.add, [tok_a, tok_b], [tok_o])

        t0s = dtf["t0"][:, 0, 0:NBLK] if NBLK <= H else None
        assert NBLK <= H
        t1s, t2s, t3s, t4s = (dtf[k][:, 0, 0:NBLK] for k in ("t1", "t2", "dt", "adt"))
        TS("dve", t0s, vec["lam"][:], -1.0, None, ALU.mult, None, ["v_lam"], ["dt_t0"])
        softplus(t1s, t0s, t2s, t3s, t4s, "dt_t1", "dt_t0", "dt_t2", "dt_dt", "dt_adt")
        TS("dve", clru[:], t1s, -8.0, None, ALU.mult, None, ["dt_t1"], ["clru"])
        ACT(a_bc[:], vec["alog"][:], AF.Exp, ["v_alog"], ["a_bc"])
        TS("dve", a_bc[:], a_bc[:], -1.0, None, ALU.mult, None, ["a_bc"], ["a_bc"])

        def load_w_chunk(src_ap, nk=None):
            wb, tok = nextw()
            nk = KD if nk is None else nk
            DMA("pool", wb[:, 0:nk, :], src_ap.rearrange("(k p) m -> p k m", p=128), [], [tok], tok)
            return wb, tok

        def proj_chunk(col0):
            wb, wtok = load_w_chunk(w_in[:, col0:col0 + 128])
            bt, btok = nextbig()
            for hf in range(NH):
                for k in range(KD):
                    MM(bt[:, hf * HW:(hf + 1) * HW], wb[:, k, :], hT[:, k, hf * HW:(hf + 1) * HW], k == 0, k == KD - 1,
                       [wtok, ("hT", hf)], [btok[hf * HW // 512]])
            return bt, btok

        def conv_chunk(bt, btok, car, cidx, cwn, cbn, xpad, xptok, acc, acctok, car_tok):
            cw, cb = vec[cwn], vec[cbn]
            CP("pool", xpad[:, 0:3], car[:, cidx, :], [car_tok], [xptok])
            CP("act", xpad[:, 3:3 + TT], bt[:, 0:TT], btok + [xptok], [xptok])
            CP("pool", car[:, cidx, :], xpad[:, TT:TT + 3], [xptok], [car_tok])
            ACT(acc[:, 0:TT], xpad[:, 0:TT], AF.Identity, [xptok, "v_" + cwn, "v_" + cbn], [acctok],
                bias=cb[:, cidx:cidx + 1], scale=cw[:, 4 * cidx:4 * cidx + 1])
            for k in (1, 2, 3):
                STT(acc[:, 0:TT], xpad[:, k:k + TT], cw[:, 4 * cidx + k:4 * cidx + k + 1], acc[:, 0:TT], ALU.mult, ALU.add,
                    [xptok, acctok, "v_" + cwn], [acctok])

        def rstd_from(ps_ap, pstok, n, dst, dtok):
            ACT(dst, ps_ap, AF.Sqrt, pstok + ["eps"], [dtok], bias=eps_t[:, 0:1], scale=1.0 / n)
            S.op("dve", lambda e: e.reciprocal(out=dst, in_=dst), [dtok], [dtok])

        auxt = lambda nm: [(nm, b) for b in range((TT + 511) // 512)]
        eps_t = sb("eps_t", [128, 1], F32)
        S.op("pool", lambda e: e.memset(eps_t[:], EPS), [], ["eps"])

        acttoks = [("act", f) for f in range(c.FC // 128)]

        def phaseA(xsrc, t0, main):
            for j in range(NQ):
                DMA("sp", xt, xsrc[t0 + j * 128:t0 + (j + 1) * 128, :], [], ["ovl"] + acttoks, "ovl")
                ACT(xn, xt, AF.Square, ["ovl"], ["ovlb", "ssq%d" % j], accum=ssq_x[:, j:j + 1])
                ACT(ssq_x[:, j:j + 1], ssq_x[:, j:j + 1], AF.Sqrt, ["ssq%d" % j, "eps"], ["ssq%d" % j], bias=eps_t[:, 0:1], scale=1.0 / D)
                S.op("dve", lambda e, j=j: e.reciprocal(out=ssq_x[:, j:j + 1], in_=ssq_x[:, j:j + 1]), ["ssq%d" % j], ["ssq%d" % j])
                TS("dve", xn, xt, ssq_x[:, j:j + 1], None, ALU.mult, None, ["ovl", "ovlb", "ssq%d" % j], ["ovlb"])
                pt, ptok = nextbig()
                ptb = pt[:].bitcast(BF16)
                for k in range(KD):
                    TR(ptb[:, k * 128:(k + 1) * 128], xn[:, k * 128:(k + 1) * 128], identb[:], ["ovlb", "identb"], [ptok[(k * 64) // 512]])
                TTn("dve", hT[:, :, j * 128:(j + 1) * 128], ptb[:, 0:KD * 128].rearrange("p (k m) -> p k m", m=128),
                    bc(vec["g_pre"][:], 2, [128, KD, 128]), ALU.mult, ptok + ["v_g_pre"], [("hT", (j * 128) // HW)])
            stop(1)
            a0v = aux0[:, 0:QH].rearrange("p (q h) -> p q h", h=H)
            a1v = aux0[:, 512:512 + QH].rearrange("p (q h) -> p q h", h=H)
            for q in range(NQ):
                for k in range(KD):
                    MM(a0v[:, q, :], hT[:, k, q * 128:(q + 1) * 128], wdt[:, k, :], k == 0, k == KD - 1,
                       [("hT", (q * 128) // HW), "wdt"], [("aux0", 0)])
            TTn("dve", dtf["t0"][:], a0v, bc(vec["dtb"][:], 1, [128, NQ, H]), ALU.add, [("aux0", 0), "v_dtb"], ["dt_t0"])
            softplus(dtf["dt"][:], dtf["t0"][:], dtf["t1"][:], dtf["t2"][:], dtf["adt"][:], "dt_dt", "dt_t0", "dt_t1", "dt_t2", "dt_adt")
            TTn("dve", dtf["adt"][:], dtf["dt"][:], bc(a_bc[:], 1, [128, NQ, H]), ALU.mult, ["dt_dt", "a_bc"], ["dt_adt"])
            CP("dve", adt_hi[:], dtf["adt"][:], ["dt_adt"], ["adt_hi"])
            TTn("dve", adt_lo[:], dtf["adt"][:], adt_hi[:], ALU.subtract, ["dt_adt", "adt_hi"], ["adt_lo"])
            for q in range(NQ):
                MM(a0v[:, q, :], trib[:], adt_hi[:, q, :], True, False, ["trib", "adt_hi"], [("aux0", 0)])
                MM(a0v[:, q, :], trib[:], adt_lo[:, q, :], False, True, ["trib", "adt_lo"], [("aux0", 0)])
                MM(a1v[:, q, :], onesb[:], adt_hi[:, q, :], True, False, ["onesb", "adt_hi"], [("aux0", 1)])
                MM(a1v[:, q, :], onesb[:], adt_lo[:, q, :], False, True, ["onesb", "adt_lo"], [("aux0", 1)])
            CP("act", dtf["acs"][:], a0v, [("aux0", 0)], ["dt_acs"])
            if main and not getattr(cfg, "NOEA", 0):
                ACT(dtf["ea"][:], dtf["acs"][:], AF.Exp, ["dt_acs"], ["dt_ea"])
            ACT(dtf["cdb"][:], a1v, AF.Exp, [("aux0", 1)], ["dt_cdb"])
            TTn("dve", dtf["t0"][:], a1v, dtf["acs"][:], ALU.subtract, [("aux0", 1), "dt_acs"], ["dt_t0"])
            ACT(dtf["t0"][:], dtf["t0"][:], AF.Exp, ["dt_t0"], ["dt_t0"])
            TTn("dve", dtf["dtde"][:], dtf["t0"][:], dtf["dt"][:], ALU.mult, ["dt_t0", "dt_dt"], ["dt_dtde"])

            stop(2)
            xpad, xl, rr, ii, a2, hl, ge = (f4[i] for i in range(7))
            xlb, sqb = h2[0], h2[1]
            for blk in range(NBLK):
                bt, btok = proj_chunk(c.oxl + blk * 128)
                conv_chunk(bt, btok, car_l, blk, "cwl", "cbl", xpad, "f4_0", xl, "f4_1", "car_l")
                CP("pool", xlb[:], xl[:, 0:TT], ["f4_1"], ["h2_0"])
                for (wsb, wtok, bn, dst, dtok) in ((lwa, "lwa", "lba", rr, "f4_2"), (lwx, "lwx", "lbx", ii, "f4_3")):
                    gt, gtok = nextbig()
                    for hf in range(NH):
                        MM(gt[:, hf * HW:(hf + 1) * HW], wsb[:, blk, :], xlb[:, hf * HW:(hf + 1) * HW], True, True, [wtok, "h2_0"], [gtok[hf * HW // 512]])
                    ACT(dst[:, 0:TT], gt[:, 0:TT], AF.Sigmoid, gtok + ["v_" + bn], [dtok], bias=vec[bn][:, blk:blk + 1])
                ACT(rr[:, 0:TT], rr[:, 0:TT], AF.Exp, ["f4_2", "clru"], ["f4_2"], scale=clru[:, blk:blk + 1])
                ACT(a2[:, 0:TT], rr[:, 0:TT], AF.Square, ["f4_2"], ["f4_4"])
                ACT(a2[:, 0:TT], a2[:, 0:TT], AF.Sqrt, ["f4_4"], ["f4_4"], bias=1.0, scale=-1.0)
                TTn("pool", ii[:, 0:TT], ii[:, 0:TT], xl[:, 0:TT], ALU.mult, ["f4_3", "f4_1"], ["f4_3"])
                TTn("pool", ii[:, 0:TT], ii[:, 0:TT], a2[:, 0:TT], ALU.mult, ["f4_3", "f4_4"], ["f4_3"])
                S.op("dve", lambda e, blk=blk: e.tensor_tensor_scan(out=hl[:, 0:TT], data0=rr[:, 0:TT], data1=ii[:, 0:TT],
                                                                     initial=hcar[:, blk:blk + 1], op0=ALU.mult, op1=ALU.add),
                     ["f4_2", "f4_3", "hcar"], ["f4_5"])
                CP("pool", hcar[:, blk:blk + 1], hl[:, TT - 1:TT], ["f4_5"], ["hcar"])
                if main:
                    gt, gtok = proj_chunk(c.ogate + blk * 128)
                    gx = f4[7]
                    CP("act", gx[:, 0:TT], gt[:, 0:TT], gtok, ["f4_7"])
                    TTn("pool", ge[:, 0:TT], gx[:, 0:TT], gx[:, 0:TT], ALU.mult, ["f4_7"], ["f4_6"])
                    TS("pool", ge[:, 0:TT], ge[:, 0:TT], 0.044715, 1.0, ALU.mult, ALU.add, ["f4_6"], ["f4_6"])
                    TTn("pool", ge[:, 0:TT], ge[:, 0:TT], gx[:, 0:TT], ALU.mult, ["f4_6", "f4_7"], ["f4_6"])
                    ACT(ge[:, 0:TT], ge[:, 0:TT], AF.Sigmoid, ["f4_6"], ["f4_6"], scale=1.5957691216057308)
                    TTn("pool", ge[:, 0:TT], ge[:, 0:TT], gx[:, 0:TT], ALU.mult, ["f4_6", "f4_7"], ["f4_6"])
                    TTn("dve", ge[:, 0:TT], ge[:, 0:TT], hl[:, 0:TT], ALU.mult, ["f4_6", "f4_5"], ["f4_6"])
                    ACT(sqb[:], ge[:, 0:TT], AF.Square, ["f4_6"], ["h2_1"])
                    CP("pool", vT[:, KS + blk, :], ge[:, 0:TT], ["f4_6"], [("vT", KS + blk)])
                    for hf in range(NH):
                        MM(aux1[:, hf * HW:(hf + 1) * HW], onesb[:], sqb[:, hf * HW:(hf + 1) * HW], blk == 0, blk == NBLK - 1,
                           ["onesb", "h2_1"], [("aux1", hf * HW // 512)])
            if main:
                rl = f4[7]
                rstd_from(aux1[:, 0:TT], auxt("aux1"), c.DL, rl[:, 0:TT], "f4_7")
                for blk in range(NBLK):
                    STT(vT[:, KS + blk, :], vT[:, KS + blk, :], vec["lnorm"][:, blk:blk + 1], rl[:, 0:TT], ALU.mult, ALU.mult,
                        [("vT", KS + blk), "f4_7", "v_lnorm"], [("vT", KS + blk)])

            stop(3)
            xsT = (f4[2], f4[3])
            zs = (f4[4], f4[5])
            yT = (f4[6], f4[7])
            BT, CT = h2[0], h2[1]
            sqg = h2[2]
            ps_acsb = aux0[:, 0:512]
            ps_xs = aux0[:, 512:768]
            ps_y = aux0[:, 768:1024]
            ps_st = aux1[:, 0:256]
            ps_yo = aux1[:, 256:512]
            ps_sc = aux1[:, 512:640]
            ps_yT = aux1[:, 640:896]
            ps_bt = aux1[:, 896:960].bitcast(BF16)
            for g in range(G):
                hs = slice(g * 4, g * 4 + 4)
                pg = prev[:, g * 256:(g + 1) * 256]
                for i in range(2):
                    cidx = g * 2 + i
                    bt, btok = proj_chunk(c.oxs + cidx * 128)
                    conv_chunk(bt, btok, car_s, cidx, "cws", "cbs", f4[0], "f4_0", f4[1], "f4_1", "car_s")
                    ACT(xsT[i][:, 0:TT], f4[1][:, 0:TT], AF.Silu, ["f4_1"], ["f4_%d" % (2 + i)])
                for (cidx, dst, dtok) in ((KS + g, BT, "h2_0"), (KS + G + g, CT, "h2_1")):
                    bt, btok = proj_chunk(c.oxs + cidx * 128)
                    conv_chunk(bt, btok, car_s, cidx, "cws", "cbs", f4[0], "f4_0", f4[1], "f4_1", "car_s")
                    ACT(dst[:], f4[1][:, 0:TT], AF.Silu, ["f4_1"], [dtok])
                if main:
                    for i in range(2):
                        bt, btok = proj_chunk(g * 256 + i * 128)
                        ACT(zs[i][:, 0:TT], bt[:, 0:TT], AF.Silu, btok, ["f4_%d" % (4 + i)])
                stop(10)
                for q in range(NQ):
                    qs = slice(q * 128, (q + 1) * 128)
                    for i in range(2):
                        TR(ps_xs[:, i * 128:(i + 1) * 128], xsT[i][:, qs], identf[:], ["f4_%d" % (2 + i), "identf"], [("aux0", 1)])
                    CP("act", sm["xs"][:], ps_xs, [("aux0", 1)], ["sm_xs"])
                    xs3 = sm["xs"][:].rearrange("p (h d) -> p h d", d=64)
                    TTn("pool", sm["x6e"][:].rearrange("p (h d) -> p h d", d=64), xs3, bc(dtf["dtde"][:, q, hs], 2, [128, 4, 64]),
                        ALU.mult, ["sm_xs", "dt_dtde"], ["sm_x6e"])
                    TR(ps_bt, BT[:, qs], identb[:], ["h2_0", "identb"], [("aux1", 1)])
                    CP("act", sm["bt"][:], ps_bt, [("aux1", 1)], ["sm_bt"])
                    MM(ps_st, sm["bt"][:], sm["x6e"][:], True, True, ["sm_bt", "sm_x6e"], [("aux1", 0)])
                    if main:
                        CP("pool", prevb[:, q, :], pg, [("prev", g)], [("prevb", q)])
                    TTn("dve", pg.rearrange("p (h d) -> p h d", d=64), pg.rearrange("p (h d) -> p h d", d=64),
                        bc(dtf["cdb"][:, q, hs], 2, [128, 4, 64]), ALU.mult, [("prev", g), "dt_cdb"], [("prev", g)])
                    TTn("dve", pg, pg, ps_st, ALU.add, [("prev", g), ("aux1", 0)], [("prev", g)])
                    if not main:
                        continue
                    TTn("pool", sm["x6"][:].rearrange("p (h d) -> p h d", d=64), xs3, bc(dtf["dt"][:, q, hs], 2, [128, 4, 64]),
                        ALU.mult, ["sm_xs", "dt_dt"], ["sm_x6"])
                    TTn("pool", sm["xhi"][:], bc(adt_hi[:, q, hs], 2, [128, 4, 128]), bc(trib[:], 1, [128, 4, 128]), ALU.mult,
                        ["adt_hi", "trib"], ["sm_xhi"])
                    TTn("pool", sm["xlo"][:], bc(adt_lo[:, q, hs], 2, [128, 4, 128]), bc(trib[:], 1, [128, 4, 128]), ALU.mult,
                        ["adt_lo", "trib"], ["sm_xlo"])
                    MM(ps_acsb, onesb[:], sm["xhi"][:].rearrange("p h l -> p (h l)"), True, False, ["onesb", "sm_xhi"], [("aux0", 0)])
                    MM(ps_acsb, onesb[:], sm["xlo"][:].rearrange("p h l -> p (h l)"), False, True, ["onesb", "sm_xlo"], [("aux0", 0)])
                    MM(ps_sc, BT[:, qs], CT[:, qs], True, True, ["h2_0", "h2_1"], [("aux1", 1)])
                    TTn("dve", sm["t"][:], ps_acsb.rearrange("p (h l) -> p h l", l=128), bc(dtf["acs"][:, q, hs], 2, [128, 4, 128]),
                        ALU.subtract, [("aux0", 0), "dt_acs"], ["sm_t"])
                    TTn("pool", sm["t"][:], sm["t"][:], bc(negm[:], 1, [128, 4, 128]), ALU.add, ["sm_t", "negm"], ["sm_t"])
                    ACT(sm["t"][:], sm["t"][:], AF.Exp, ["sm_t"], ["sm_t"])
                    TTn("dve", sm["mt"][:], sm["t"][:], bc(ps_sc, 1, [128, 4, 128]), ALU.mult, ["sm_t", ("aux1", 1)], ["sm_mt"])
                    for h in range(4):
                        MM(ps_y[:, h * 64:(h + 1) * 64], sm["mt"][:, h, :], sm["x6"][:, h * 64:(h + 1) * 64], True, True,
                           ["sm_mt", "sm_x6"], [("aux0", 1)])
                    MM(ps_yo, CT[:, qs], prevb[:, q, :], True, True, ["h2_1", ("prevb", q)], [("aux1", 0)])
                    y3 = sm["y"][:].rearrange("p (h d) -> p h d", d=64)
                    TTn("dve", y3, ps_yo.rearrange("p (h d) -> p h d", d=64), bc(dtf["ea"][:, q, hs], 2, [128, 4, 64]), ALU.mult,
                        [("aux1", 0), "dt_ea"], ["sm_y"])
                    TTn("dve", sm["y"][:], sm["y"][:], ps_y, ALU.add, ["sm_y", ("aux0", 1)], ["sm_y"])
                    xsd = sm["t"][:].rearrange("p h l -> p (h l)")[:, 0:256]
                    TTn("pool", xsd.rearrange("p (h d) -> p h d", d=64), xs3, bc(vec["dsk"][:, hs], 2, [128, 4, 64]),
                        ALU.mult, ["sm_xs", "v_dsk", "sm_t"], ["sm_t"])
                    TTn("pool", sm["y"][:], sm["y"][:], xsd, ALU.add, ["sm_y", "sm_t"], ["sm_y"])
                    for i in range(2):
                        TR(ps_yT[:, i * 128:(i + 1) * 128], sm["y"][:, i * 128:(i + 1) * 128], identf[:], ["sm_y", "identf"], [("aux1", 1)])
                    for i in range(2):
                        CP("act", yT[i][:, qs], ps_yT[:, i * 128:(i + 1) * 128], [("aux1", 1)], ["f4_%d" % (6 + i)])
                stop(11)
                if not main:
                    continue
                gt, gtok = nextbig()
                for i in range(2):
                    TTn("pool", yT[i][:, 0:TT], yT[i][:, 0:TT], zs[i][:, 0:TT], ALU.mult, ["f4_%d" % (6 + i), "f4_%d" % (4 + i)], ["f4_%d" % (6 + i)])
                    ACT(sqg[:], yT[i][:, 0:TT], AF.Square, ["f4_%d" % (6 + i)], ["h2_2"])
                    for hf in range(NH):
                        MM(gt[:, hf * HW:(hf + 1) * HW], onesb[:], sqg[:, hf * HW:(hf + 1) * HW], i == 0, i == 1, ["onesb", "h2_2"], [gtok[hf * HW // 512]])
                rstd_from(gt[:, 0:TT], gtok, 256, f4[0][:, 0:TT], "f4_0")
                for i in range(2):
                    STT(vT[:, g * 2 + i, :], yT[i][:, 0:TT], vec["snorm"][:, g * 2 + i:g * 2 + i + 1], f4[0][:, 0:TT], ALU.mult, ALU.mult,
                        ["f4_%d" % (6 + i), "f4_0", "v_snorm"], [("vT", g * 2 + i)])
            stop(9)

        def phaseB(t0):
            stop(4)
            vtoks = [("vT", k) for k in range(KM)]
            for cc in range(KD):
                bt, btok = nextbig()
                nkk = (KM + KD - 1) // KD
                wl = []
                for part in range(nkk):
                    k0 = part * KD
                    nk = min(KD, KM - k0)
                    wl.append((load_w_chunk(w_out[k0 * 128:(k0 + nk) * 128, cc * 128:(cc + 1) * 128], nk), k0, nk))
                for hf in range(NH):
                    for (wb, wtok), k0, nk in wl:
                        for k in range(nk):
                            MM(bt[:, hf * HW:(hf + 1) * HW], wb[:, k, :], vT[:, k0 + k, hf * HW:(hf + 1) * HW], k0 + k == 0, k0 + k == KM - 1,
                               [wtok] + vtoks, [btok[hf * HW // 512]])
                st, stok = f4[cc % 2], "f4_%d" % (cc % 2)
                sq, sqtok = h2[cc % 2], "h2_%d" % (cc % 2)
                CP("act", st[:, 0:TT], bt[:, 0:TT], btok, [stok])
                ACT(sq[:], bt[:, 0:TT], AF.Square, btok, [sqtok])
                for hf in range(NH):
                    MM(aux0[:, hf * HW:(hf + 1) * HW], onesb[:], sq[:, hf * HW:(hf + 1) * HW], cc == 0, cc == KD - 1, ["onesb", sqtok], [("aux0", hf * HW // 512)])
                DMA("sp", msT[cc], st[:, 0:TT], [stok], [("msT", cc)], stok + "s")
            r1 = f4[7]
            rstd_from(aux0[:, 0:TT], auxt("aux0"), D, r1[:, 0:TT], "f4_7")
            stop(5)
            for cc in range(KD):
                ms, mstok = f4[cc % 2], "f4_%d" % (cc % 2)
                xb, xbtok = f4[2 + cc % 2], "f4_%d" % (2 + cc % 2)
                sq, sqtok = h2[cc % 2], "h2_%d" % (cc % 2)
                DMA("sp", ms[:, 0:TT], msT[cc], [("msT", cc)], [mstok], mstok)
                DMA("sp", xb[:, 0:TT].rearrange("p (j m) -> p j m", m=128),
                    xm[t0:t0 + TT, cc * 128:(cc + 1) * 128].rearrange("(j p) m -> p j m", p=128), [], [xbtok], xbtok)
                bt, btok = nextbig()
                for j in range(NQ):
                    TR(bt[:, j * 128:(j + 1) * 128], xb[:, j * 128:(j + 1) * 128], identf[:], [xbtok, "identf"], [btok[(j * 128) // 512]])
                STT(ms[:, 0:TT], ms[:, 0:TT], vec["g_post"][:, cc:cc + 1], r1[:, 0:TT], ALU.mult, ALU.mult, [mstok, "f4_7", "v_g_post"], [mstok])
                TTn("dve", ms[:, 0:TT], ms[:, 0:TT], bt[:, 0:TT], ALU.add, [mstok] + btok, [mstok])
                DMA("sp", x1s[cc], ms[:, 0:TT], [mstok], [("x1s", cc)], mstok + "s")
                ACT(sq[:], ms[:, 0:TT], AF.Square, [mstok], [sqtok])
                for hf in range(NH):
                    MM(aux1[:, hf * HW:(hf + 1) * HW], onesb[:], sq[:, hf * HW:(hf + 1) * HW], cc == 0, cc == KD - 1, ["onesb", sqtok], [("aux1", hf * HW // 512)])
            r2 = f4[6]
            rstd_from(aux1[:, 0:TT], auxt("aux1"), D, r2[:, 0:TT], "f4_6")
            stop(6)
            for cc in range(KD):
                xb, xbtok = f4[cc % 2], "f4_%d" % (cc % 2)
                DMA("sp", xb[:, 0:TT], x1s[cc], [("x1s", cc)], [xbtok], xbtok)
                STT(hT[:, cc, :], xb[:, 0:TT], vec["g_mlp"][:, cc:cc + 1], r2[:, 0:TT], ALU.mult, ALU.mult, [xbtok, "f4_6", "v_g_mlp"],
                    [("hT", hf) for hf in range(NH)])
            stop(7)
            KB = c.FC // 128
            htoks = [("hT", hf) for hf in range(NH)]
            for fb in range(c.DFF // c.FC):
                for f in range(KB):
                    wb, wtok = load_w_chunk(w1[:, fb * c.FC + f * 128:fb * c.FC + (f + 1) * 128])
                    bt, btok = nextbig()
                    for hf in range(NH):
                        for k in range(KD):
                            MM(bt[:, hf * HW:(hf + 1) * HW], wb[:, k, :], hT[:, k, hf * HW:(hf + 1) * HW], k == 0, k == KD - 1, [wtok] + htoks, [btok[hf * HW // 512]])
                    rl, rltok = f4[2 + f % 2], "f4_%d" % (2 + f % 2)
                    ACT(rl[:, 0:TT], bt[:, 0:TT], AF.Relu, btok, [rltok])
                    TTn("pool", actT[:, f, :], rl[:, 0:TT], rl[:, 0:TT], ALU.mult, [rltok], [("act", f), "ovl", "ovlb"])
                for cc in range(KD):
                    wb, wtok = load_w_chunk(w2[fb * c.FC:(fb + 1) * c.FC, cc * 128:(cc + 1) * 128], KB)
                    bt, btok = nextbig()
                    for hf in range(NH):
                        for k in range(KB):
                            MM(bt[:, hf * HW:(hf + 1) * HW], wb[:, k, :], actT[:, k, hf * HW:(hf + 1) * HW], k == 0, k == KB - 1,
                               [wtok, ("act", k)], [btok[hf * HW // 512]])
                    if fb == 0:
                        CP("act", accT[:, cc, :], bt[:, 0:TT], btok + vtoks, [("acc", cc)] + vtoks)
                    else:
                        TTn("dve", accT[:, cc, :], accT[:, cc, :], bt[:, 0:TT], ALU.add, btok + [("acc", cc)], [("acc", cc)])
            stop(8)
            for cc in range(KD):
                sq, sqtok = h2[cc % 2], "h2_%d" % (cc % 2)
                ACT(sq[:], accT[:, cc, :], AF.Square, [("acc", cc)] + vtoks, [sqtok])
                for hf in range(NH):
                    MM(aux0[:, hf * HW:(hf + 1) * HW], onesb[:], sq[:, hf * HW:(hf + 1) * HW], cc == 0, cc == KD - 1, ["onesb", sqtok], [("aux0", hf * HW // 512)])
            r3 = f4[7]
            rstd_from(aux0[:, 0:TT], auxt("aux0"), D, r3[:, 0:TT], "f4_7")
            for cc in range(KD):
                xb, xbtok = f4[cc % 2], "f4_%d" % (cc % 2)
                ob, obtok = f4[2 + cc % 2], "f4_%d" % (2 + cc % 2)
                os_, ostok = f4[4 + cc % 2], "f4_%d" % (4 + cc % 2)
                DMA("sp", xb[:, 0:TT], x1s[cc], [("x1s", cc)], [xbtok], xbtok)
                STT(ob[:, 0:TT], accT[:, cc, :], vec["g_pmlp"][:, cc:cc + 1], r3[:, 0:TT], ALU.mult, ALU.mult, [("acc", cc), "f4_7", "v_g_pmlp"] + vtoks, [obtok])
                TTn("pool", ob[:, 0:TT], ob[:, 0:TT], xb[:, 0:TT], ALU.add, [obtok, xbtok], [obtok])
                bt, btok = nextbig()
                for j in range(NQ):
                    TR(bt[:, j * 128:(j + 1) * 128], ob[:, j * 128:(j + 1) * 128], identf[:], [obtok, "identf"], [btok[(j * 128) // 512]])
                CP("act", os_[:, 0:TT], bt[:, 0:TT], btok, [ostok])
                DMA("sp", out[t0:t0 + TT, cc * 128:(cc + 1) * 128].rearrange("(j p) m -> p j m", p=128),
                    os_[:, 0:TT].rearrange("p (j m) -> p j m", m=128), [ostok], [("out", t0, cc)], ostok + "o")
                S.out_tokens.append(("out", t0, cc))

        def _program():
            for ti in range(c.NPRE):
                phaseA(xp, ti * TT, False)
            alltok = [("prev", g) for g in range(G)]
            TS("dve", prev[:], prev[:], flag[:, 0:1], None, ALU.mult, None, alltok + ["flag"], alltok)
            TS("dve", hcar[:], hcar[:], flag[:, 0:1], None, ALU.mult, None, ["hcar", "flag"], ["hcar"])
            for ti in range(c.NMAIN):
                phaseA(xm, ti * TT, True)
                phaseB(ti * TT)

        S.out_tokens = []

        class _Stop(Exception):
            pass

        def stop(level):
            if getattr(cfg, "STOP", 0) == level:
                raise _Stop()

        try:
            _program()
        except _Stop:
            pass
        S.op("dve", lambda e: e.engine_nop(), S.out_tokens, [])
        S.emit()
    return nc


def _vec_layouts(cfg, p):
    c = cfg
    pm = lambda v, n: np.ascontiguousarray(np.asarray(v, np.float32).reshape(n, 128).T)
    bcast = lambda v: np.ascontiguousarray(np.broadcast_to(np.asarray(v, np.float32)[None, :], (128, len(v))))
    cw = lambda w, n: np.ascontiguousarray(np.asarray(w, np.float32).reshape(4, n, 128).transpose(2, 1, 0).reshape(128, n * 4))
    return dict(
        g_pre=pm(p["pre_mix_norm"], c.KD), g_post=pm(p["post_mix_norm"], c.KD), g_mlp=pm(p["pre_mlp_norm"], c.KD),
        g_pmlp=pm(p["post_mlp_norm"], c.KD),
        cws=cw(p["ssd_conv_w"], c.KX), cbs=pm(p["ssd_conv_b"], c.KX), cwl=cw(p["lru_conv_w"], c.NBLK), cbl=pm(p["lru_conv_b"], c.NBLK),
        lba=pm(p["lru_b_a"], c.NBLK), lbx=pm(p["lru_b_x"], c.NBLK), lam=pm(p["lru_lambda"], c.NBLK), lnorm=pm(p["lru_norm"], c.NBLK),
        snorm=pm(p["ssd_norm"], c.KS), dtb=bcast(p["ssd_dt_bias"]), alog=bcast(p["ssd_a_log"]), dsk=bcast(p["ssd_d"]))


def make_in_maps(cfg, inputs, n_batch, n_half):
    c = cfg
    p = {k: np.asarray(v)[0] for k, v in inputs.items() if k != "x"}
    x = np.asarray(inputs["x"], np.float32)
    vl = _vec_layouts(c, p)
    half = c.NMAIN * c.TT
    shared = dict(w_in=np.ascontiguousarray(p["w_in"], np.float32), w_out=np.ascontiguousarray(p["w_out"], np.float32),
                  w1=np.ascontiguousarray(p["w_mlp_in"], np.float32), w2=np.ascontiguousarray(p["w_mlp_out"], np.float32),
                  lwa=np.ascontiguousarray(p["lru_w_a"], np.float32), lwx=np.ascontiguousarray(p["lru_w_x"], np.float32), **vl)
    maps = []
    for b in range(n_batch):
        for s in range(n_half):
            m = dict(shared)
            m["xm"] = np.ascontiguousarray(x[b, s * half:(s + 1) * half])
            pre = np.zeros((max(c.NPRE, 1) * c.TT, c.D), np.float32)
            if s > 0:
                pre[:] = x[b, (s - 1) * half:s * half][-pre.shape[0]:]
            m["xp"] = pre
            m["flag"] = np.full((128, 1), 1.0 if s > 0 else 0.0, np.float32)
            maps.append(m)
    return maps


def kernel(**inputs):
    cfg = Cfg()
    nc = build(cfg)
    maps = make_in_maps(cfg, inputs, 4, 2)
    res = run_bass_kernel_spmd(nc, maps, core_ids=list(range(8)))
    x = np.asarray(inputs["x"])
    outp = np.empty(x.shape, np.float32)
    half = cfg.NMAIN * cfg.TT
    i = 0
    for b in range(4):
        for s in range(2):
            outp[b, s * half:(s + 1) * half] = np.asarray(res.results[i]["out"], np.float32)
            i += 1
    return outp
```

```python
import contextlib
import numpy as np
import concourse.bass as bass
import concourse.mybir as mybir
from concourse.bass_utils import run_bass_kernel_spmd

F32 = mybir.dt.float32
BF16 = mybir.dt.bfloat16
AF = mybir.ActivationFunctionType
ALU = mybir.AluOpType
EPS = 1e-6


class Cfg:
    def __init__(self, D=2048, H=32, G=8, DL=2048, DFF=8192, TT=1024, NMAIN=2, NPRE=2, FC=1024):
        self.D, self.H, self.G, self.DL, self.DFF = D, H, G, DL, DFF
        self.TT, self.NMAIN, self.NPRE, self.FC = TT, NMAIN, NPRE, FC
        self.P, self.N = 64, 128
        self.DS = H * 64
        self.HG = H // G
        assert self.HG == 4
        self.KD = D // 128
        self.KS = self.DS // 128
        self.DXBC = self.DS + 2 * G * 128
        self.KX = self.DXBC // 128
        self.DIN = self.DS + self.DXBC + H + 2 * DL
        self.NBLK = DL // 128
        self.DMIX = self.DS + DL
        self.KM = self.DMIX // 128
        self.KF = DFF // 128
        self.NQ = TT // 128
        self.HW = min(512, TT)
        self.NH = TT // self.HW
        self.oxs = self.DS
        self.oB = 2 * self.DS
        self.oC = 2 * self.DS + G * 128
        self.odt = self.DS + self.DXBC
        self.ogate = self.odt + H
        self.oxl = self.ogate + DL


class Sched:
    ENGS = ("pe", "act", "dve", "pool", "sp")

    def __init__(self, nc):
        self.nc = nc
        self.ops = []
        self.last_w = {}
        self.readers = {}
        self.dma_cnt = {}

    def op(self, eng, fn, reads=(), writes=(), dma_key=None, cost=300.0, nbytes=0):
        idx = len(self.ops)
        deps = set()
        for t in reads:
            w = self.last_w.get(t)
            if w is not None:
                deps.add(w)
        for t in writes:
            w = self.last_w.get(t)
            if w is not None:
                deps.add(w)
            deps.update(self.readers.get(t, ()))
        deps.discard(idx)
        o = dict(eng=eng, fn=fn, deps=deps, dma=dma_key is not None, key=dma_key, ms=False, cost=cost, nbytes=nbytes)
        if dma_key is not None:
            n = self.dma_cnt.get(dma_key, 0) + 1
            self.dma_cnt[dma_key] = n
            o["dval"] = 16 * n
        self.ops.append(o)
        for t in reads:
            self.readers.setdefault(t, []).append(idx)
        for t in writes:
            self.last_w[t] = idx
            self.readers[t] = []
        return idx

    def schedule(self, window=256):
        ops = self.ops
        pend = {e: [] for e in self.ENGS}
        for i, o in enumerate(ops):
            pend[o["eng"]].append(i)
        head = {e: 0 for e in self.ENGS}
        tfree = {e: 0.0 for e in self.ENGS}
        done = [None] * len(ops)
        sched = [False] * len(ops)
        order = {e: [] for e in self.ENGS}
        pipe_free = 0.0
        remaining = len(ops)
        while remaining:
            progressed = False
            for e in sorted(self.ENGS, key=lambda x: tfree[x]):
                lst = pend[e]
                while head[e] < len(lst) and sched[lst[head[e]]]:
                    head[e] += 1
                if head[e] >= len(lst):
                    continue
                best, best_t = None, None
                seen = 0
                j = head[e]
                while j < len(lst) and seen < window:
                    i = lst[j]
                    j += 1
                    if sched[i]:
                        continue
                    seen += 1
                    t = tfree[e]
                    ok = True
                    for d in ops[i]["deps"]:
                        dt_ = done[d]
                        if dt_ is None:
                            ok = False
                            break
                        if dt_ > t:
                            t = dt_
                    if ok and (best is None or t < best_t - 1e-9):
                        best, best_t = i, t
                        if t <= tfree[e] + 1e-9:
                            break
                if best is None:
                    continue
                o = ops[best]
                if o["dma"]:
                    issue = 900.0 if e == "pool" else 120.0
                    xs_ = max(best_t + issue, pipe_free)
                    pipe_free = xs_ + o["nbytes"] / 250.0
                    done[best] = pipe_free + 2000.0
                    tfree[e] = best_t + issue
                else:
                    done[best] = best_t + o["cost"]
                    tfree[e] = done[best]
                sched[best] = True
                order[e].append(best)
                remaining -= 1
                progressed = True
                break
            assert progressed, "list scheduler stuck"
        self.makespan = max(d for d in done if d is not None)
        return order

    def emit(self, reorder=True):
        nc, ops = self.nc, self.ops
        if reorder:
            order = self.schedule()
        else:
            order = {e: [i for i, o in enumerate(ops) if o["eng"] == e] for e in self.ENGS}
        pos = {}
        for e in self.ENGS:
            for k, i in enumerate(order[e]):
                pos[i] = k
        for i, o in enumerate(ops):
            latest = {}
            eff = []
            for d in o["deps"]:
                p = ops[d]
                if p["dma"]:
                    eff.append(d)
                    continue
                if p["eng"] == "pe" and o["eng"] == "pe" and not o["dma"]:
                    assert pos[d] < pos[i]
                    continue
                if p["eng"] == o["eng"]:
                    assert pos[d] < pos[i]
                if p["eng"] not in latest or pos[d] > pos[latest[p["eng"]]]:
                    latest[p["eng"]] = d
            o["eff"] = eff + list(latest.values())
            for d in latest.values():
                ops[d]["ms"] = True
        KSEM = 8
        cnt = {e: 0 for e in self.ENGS}
        for e in self.ENGS:
            for i in order[e]:
                o = ops[i]
                if o["ms"]:
                    o["msi"] = cnt[e]
                    cnt[e] += 1
        self.ms_counts = cnt
        with contextlib.ExitStack() as es:
            esem = {e: [es.enter_context(nc.semaphore("S_%s%d" % (e, i))) for i in range(KSEM)] for e in self.ENGS if cnt[e] > 0}
            dsem = {}
            for i, (k, n) in enumerate(self.dma_cnt.items()):
                r = 1 if k in ("setup", "setup_p") else max(1, (n * 16 + 479) // 480)
                dsem[k] = [es.enter_context(nc.semaphore("D_%d_%d" % (i, j))) for j in range(r)]
            block = es.enter_context(nc.Block())

            def semval(p):
                if p["dma"]:
                    lst = dsem[p["key"]]
                    if p["key"] in ("setup", "setup_p"):
                        return lst[0], 16 * self.dma_cnt[p["key"]]
                    n = p["dval"] // 16 - 1
                    return lst[n % len(lst)], 16 * (n // len(lst) + 1)
                i = p["msi"]
                return esem[p["eng"]][i % KSEM], i // KSEM + 1

            def run(engname, eng):
                waited = {}
                for oi in order[engname]:
                    o = ops[oi]
                    need = {}
                    for d in o["eff"]:
                        p = ops[d]
                        s, v = semval(p)
                        if need.get(id(s), (None, 0))[1] < v:
                            need[id(s)] = (s, v)
                    for s, v in need.values():
                        if waited.get(id(s), 0) < v:
                            eng.wait_ge(s, v)
                            waited[id(s)] = v
                    ins = o["fn"](eng)
                    if o["dma"]:
                        ins.then_inc(semval(o)[0], 16)
                    elif o["ms"]:
                        ins.then_inc(semval(o)[0], 1)

            block.tensor(lambda e: run("pe", e))
            block.scalar(lambda e: run("act", e))
            block.vector(lambda e: run("dve", e))
            block.gpsimd(lambda e: run("pool", e))
            block.sync(lambda e: run("sp", e))


def build(cfg):
    c = cfg
    D, H, G, TT, KD, KS, KX, KM, NQ, NH, HW, NBLK = c.D, c.H, c.G, c.TT, c.KD, c.KS, c.KX, c.KM, c.NQ, c.NH, c.HW, c.NBLK
    nc = bass.Bass("TRN2", target_bir_lowering=False)
    din = lambda name, shape: nc.dram_tensor(name, list(shape), F32, kind="ExternalInput").ap()
    xm = din("xm", [c.NMAIN * TT, D])
    xp = din("xp", [max(c.NPRE, 1) * TT, D])
    flag_d = din("flag", [128, 1])
    w_in = din("w_in", [D, c.DIN])
    w_out = din("w_out", [c.DMIX, D])
    w1 = din("w1", [D, c.DFF])
    w2 = din("w2", [c.DFF, D])
    lwa_d = din("lwa", [NBLK, 128, 128])
    lwx_d = din("lwx", [NBLK, 128, 128])
    vec_d = {}
    vec_shapes = dict(g_pre=[128, KD], g_post=[128, KD], g_mlp=[128, KD], g_pmlp=[128, KD],
                      cws=[128, KX * 4], cbs=[128, KX], cwl=[128, NBLK * 4], cbl=[128, NBLK],
                      lba=[128, NBLK], lbx=[128, NBLK], lam=[128, NBLK], lnorm=[128, NBLK],
                      snorm=[128, KS], dtb=[128, H], alog=[128, H], dsk=[128, H])
    for k, shp in vec_shapes.items():
        vec_d[k] = din(k, shp)
    out = nc.dram_tensor("out", [c.NMAIN * TT, D], F32, kind="ExternalOutput").ap()
    msT = nc.dram_tensor("msT", [KD, 128, TT], F32).ap()
    x1s = nc.dram_tensor("x1s", [KD, 128, TT], F32).ap()

    es = contextlib.ExitStack()
    with es:
        sb = lambda name, shape, dt: es.enter_context(nc.sbuf_tensor(name, list(shape), dt))
        S = Sched(nc)
        if getattr(cfg, "PAD", 0):
            sb("pad", [128, cfg.PAD // 4], F32)
        def ACT(out_, in_, func, r, w, bias=None, scale=None, accum=None):
            kw = {}
            if bias is not None:
                kw["bias"] = bias
            if scale is not None:
                kw["scale"] = scale
            if accum is not None:
                kw["accum_out"] = accum
            S.op("act", lambda e: e.activation(out=out_, in_=in_, func=func, **kw), r, w, cost=(224.0 + in_.free_size()) / 1.2 + (90.0 if accum is not None else 0.0))

        def TTn(eng, out_, in0, in1, op, r, w):
            S.op(eng, lambda e: e.tensor_tensor(out=out_, in0=in0, in1=in1, op=op), r, w, cost=(130.0 + out_.free_size()) / 0.96)

        def TS(eng, out_, in0, s1, s2, op0, op1, r, w):
            if s2 is None:
                S.op(eng, lambda e: e.tensor_scalar(out=out_, in0=in0, scalar1=s1, scalar2=None, op0=op0), r, w, cost=(130.0 + out_.free_size()) / 0.96)
            else:
                S.op(eng, lambda e: e.tensor_scalar(out=out_, in0=in0, scalar1=s1, scalar2=s2, op0=op0, op1=op1), r, w, cost=(130.0 + out_.free_size()) / 0.96)

        def STT(out_, in0, scalar, in1, op0, op1, r, w):
            S.op("dve", lambda e: e.scalar_tensor_tensor(out=out_, in0=in0, scalar=scalar, in1=in1, op0=op0, op1=op1), r, w, cost=(130.0 + out_.free_size()) / 0.96)

        def CP(eng, out_, in_, r, w):
            if eng == "act":
                S.op("act", lambda e: e.activation(out=out_, in_=in_, func=AF.Copy), r, w, cost=(224.0 + in_.free_size()) / 1.2)
            else:
                S.op(eng, lambda e: e.tensor_copy(out=out_, in_=in_), r, w, cost=(130.0 + out_.free_size()) / 0.96)

        def MM(out_, lhsT, rhs, start, stop, r, w):
            S.op("pe", lambda e: e.matmul(out_, lhsT=lhsT, rhs=rhs, start=start, stop=stop), r, w, cost=max(64.0, rhs.free_size()) / 2.4 + 12.0)

        def TR(out_, in_, ident, r, w):
            S.op("pe", lambda e: e.transpose(out_, in_, ident), r, w, cost=110.0)

        def DMA(q, out_, in_, r, w, key):
            S.op(q, lambda e: e.dma_start(out=out_, in_=in_), r, w, dma_key=key, nbytes=max(out_.nbytes(), in_.nbytes()))

        def bc(ap, axis, shape):
            return ap.unsqueeze(axis).to_broadcast(list(shape))

        identb = sb("identb", [128, 128], BF16)
        identf = sb("identf", [128, 128], F32)
        onesb = sb("onesb", [128, 128], BF16)
        trib = sb("trib", [128, 128], BF16)
        negm = sb("negm", [128, 128], F32)
        vec = {k: sb("v_" + k, shp, F32) for k, shp in vec_shapes.items()}
        flag = sb("flag_sb", [128, 1], F32)
        clru = sb("clru", [128, NBLK], F32)
        a_bc = sb("a_bc", [128, H], F32)
        wdt = sb("wdt", [128, KD, H], BF16)
        lwa = sb("lwa_sb", [128, NBLK, 128], BF16)
        lwx = sb("lwx_sb", [128, NBLK, 128], BF16)
        prev = sb("prev", [128, G * 256], F32)
        prevb = sb("prevb", [128, NQ, 256], BF16)
        hcar = sb("hcar", [128, NBLK], F32)
        car_s = sb("car_s", [128, KX, 3], F32)
        car_l = sb("car_l", [128, NBLK, 3], F32)
        QH = NQ * H
        dtf = {k: sb("dt_" + k, [128, NQ, H], F32) for k in ("dt", "adt", "acs", "ea", "cdb", "dtde", "t0", "t1", "t2")}
        adt_hi = sb("adt_hi", [128, NQ, H], BF16)
        adt_lo = sb("adt_lo", [128, NQ, H], BF16)
        ssq_x = sb("ssq_x", [128, NQ], F32)
        RAK = max(KD, (KM + 1) // 2)
        RA = sb("RA", [128, RAK * TT], F32)
        vT = RA[:].bitcast(BF16).rearrange("p (k t) -> p k t", t=TT)
        accT = RA[:].rearrange("p (k t) -> p k t", t=TT)
        hT = sb("hT", [128, KD, TT], BF16)
        OVW = max(c.FC // 128 * TT // 2, D + D // 2)
        ovl = sb("ovl", [128, OVW], F32)
        xt = ovl[:, 0:D]
        xn = ovl[:, D:D + D // 2].bitcast(BF16)
        actT = ovl[:, 0:c.FC // 128 * TT // 2].bitcast(BF16).rearrange("p (k t) -> p k t", t=TT)
        NW = 3
        wbuf = [sb("wbuf%d" % i, [128, KD, 128], BF16) for i in range(NW)]
        NF = 8
        f4 = [sb("f4_%d" % i, [128, TT + 4], F32) for i in range(NF)]
        h2 = [sb("h2_%d" % i, [128, TT], BF16) for i in range(3)]
        sm = {k: sb("sm_" + k, shp, dt) for k, (shp, dt) in dict(
            xs=([128, 256], F32), x6=([128, 256], BF16), x6e=([128, 256], BF16), bt=([128, 128], BF16),
            xhi=([128, 4, 128], BF16), xlo=([128, 4, 128], BF16), t=([128, 4, 128], F32),
            mt=([128, 4, 128], BF16), y=([128, 256], F32)).items()}
        ps = lambda name: es.enter_context(nc.psum_tensor(name, [128, 1024], F32))
        big = [ps("big0"), ps("big1")]
        aux0, aux1 = ps("aux0"), ps("aux1")
        bigi = [0]

        def nextbig():
            i = bigi[0] % 2
            bigi[0] += 1
            return big[i], [("big%d" % i, b) for b in range((TT + 511) // 512)]

        wi = [0]

        def nextw():
            i = wi[0] % NW
            wi[0] += 1
            return wbuf[i], "wbuf%d" % i

        for k in vec_shapes:
            DMA("sp", vec[k][:], vec_d[k][:], [], ["v_" + k], "setup")
        DMA("sp", flag[:], flag_d[:], [], ["flag"], "setup")
        DMA("pool", wdt[:], w_in[:, c.odt:c.odt + H].rearrange("(k p) h -> p k h", p=128), [], ["wdt"], "setup_p")
        DMA("pool", lwa[:], lwa_d.rearrange("b i j -> i b j"), [], ["lwa"], "setup_p")
        DMA("pool", lwx[:], lwx_d.rearrange("b i j -> i b j"), [], ["lwx"], "setup_p")

        def mask_const(t, tok, fill, pat, cm, op):
            S.op("pool", lambda e: e.affine_select(out=t[:], in_=t[:], pattern=pat, compare_op=op, fill=fill,
                                                   base=0, channel_multiplier=cm), [tok], [tok])

        S.op("pool", lambda e: e.memset(identb[:], 1.0), [], ["identb"])
        mask_const(identb, "identb", 0.0, [[-1, 128]], 1, ALU.is_equal)
        S.op("pool", lambda e: e.memset(identf[:], 1.0), [], ["identf"])
        mask_const(identf, "identf", 0.0, [[-1, 128]], 1, ALU.is_equal)
        S.op("pool", lambda e: e.memset(onesb[:], 1.0), [], ["onesb"])
        S.op("pool", lambda e: e.memset(trib[:], 1.0), [], ["trib"])
        mask_const(trib, "trib", 0.0, [[1, 128]], -1, ALU.is_ge)
        S.op("pool", lambda e: e.memset(negm[:], 0.0), [], ["negm"])
        mask_const(negm, "negm", -30000.0, [[1, 128]], -1, ALU.is_ge)
        S.op("pool", lambda e: e.memset(prev[:], 0.0), [], ["prev"])
        S.op("pool", lambda e: e.memset(hcar[:], 0.0), [], ["hcar"])
        S.op("pool", lambda e: e.memset(car_s[:], 0.0), [], ["car_s"])
        S.op("pool", lambda e: e.memset(car_l[:], 0.0), [], ["car_l"])

        def log1p_small(out_, e_, tmpw, tmpq, tok_o, tok_e, tok_w, tok_q):
            TS("dve", tmpw, e_, 2.0, None, ALU.add, None, [tok_e], [tok_w])
            S.op("dve", lambda e: e.reciprocal(out=tmpw, in_=tmpw), [tok_w], [tok_w])
            TTn("dve", tmpw, tmpw, e_, ALU.mult, [tok_w, tok_e], [tok_w])
            TTn("dve", out_, tmpw, tmpw, ALU.mult, [tok_w], [tok_o])
            TS("dve", tmpq, out_, 1.0 / 11.0, None, ALU.mult, None, [tok_o], [tok_q])
            for cst in (1.0 / 9.0, 1.0 / 7.0, 1.0 / 5.0, 1.0 / 3.0):
                STT(tmpq, tmpq, cst, out_, ALU.add, ALU.mult, [tok_q, tok_o], [tok_q])
            TS("dve", tmpq, tmpq, 1.0, None, ALU.add, None, [tok_q], [tok_q])
            TTn("dve", tmpq, tmpq, tmpw, ALU.mult, [tok_q, tok_w], [tok_q])
            TS("dve", out_, tmpq, 2.0, None, ALU.mult, None, [tok_q], [tok_o])

        def softplus(out_, x_, ta, tb, tcc, tok_o, tok_x, tok_a, tok_b, tok_c):
            TS("dve", ta, x_, -1.0, None, ALU.mult, None, [tok_x], [tok_a])
            TTn("dve", ta, ta, x_, ALU.max, [tok_a, tok_x], [tok_a])
            ACT(ta, ta, AF.Exp, [tok_a], [tok_a], scale=-1.0)
            log1p_small(tb, ta, tcc, out_, tok_b, tok_a, tok_c, tok_o)
            TS("dve", ta, x_, 0.0, None, ALU.max, None, [tok_x], [tok_a])
            TTn("dve", out_, ta, tb, ALU.add, [tok_a, tok_b], [tok_o])

        t0s = dtf["t0"][:, 0, 0:NBLK] if NBLK <= H else None
        assert NBLK <= H
        t1s, t2s, t3s, t4s = (dtf[k][:, 0, 0:NBLK] for k in ("t1", "t2", "dt", "adt"))
        TS("dve", t0s, vec["lam"][:], -1.0, None, ALU.mult, None, ["v_lam"], ["dt_t0"])
        softplus(t1s, t0s, t2s, t3s, t4s, "dt_t1", "dt_t0", "dt_t2", "dt_dt", "dt_adt")
        TS("dve", clru[:], t1s, -8.0, None, ALU.mult, None, ["dt_t1"], ["clru"])
        ACT(a_bc[:], vec["alog"][:], AF.Exp, ["v_alog"], ["a_bc"])
        TS("dve", a_bc[:], a_bc[:], -1.0, None, ALU.mult, None, ["a_bc"], ["a_bc"])

        def load_w_chunk(src_ap, nk=None):
            wb, tok = nextw()
            nk = KD if nk is None else nk
            DMA("pool", wb[:, 0:nk, :], src_ap.rearrange("(k p) m -> p k m", p=128), [], [tok], tok)
            return wb, tok

        def proj_chunk(col0):
            wb, wtok = load_w_chunk(w_in[:, col0:col0 + 128])
            bt, btok = nextbig()
            for hf in range(NH):
                for k in range(KD):
                    MM(bt[:, hf * HW:(hf + 1) * HW], wb[:, k, :], hT[:, k, hf * HW:(hf + 1) * HW], k == 0, k == KD - 1,
                       [wtok, ("hT", hf)], [btok[hf * HW // 512]])
            return bt, btok

        def conv_chunk(bt, btok, car, cidx, cwn, cbn, xpad, xptok, acc, acctok, car_tok):
            cw, cb = vec[cwn], vec[cbn]
            CP("pool", xpad[:, 0:3], car[:, cidx, :], [car_tok], [xptok])
            CP("act", xpad[:, 3:3 + TT], bt[:, 0:TT], btok + [xptok], [xptok])
            CP("pool", car[:, cidx, :], xpad[:, TT:TT + 3], [xptok], [car_tok])
            ACT(acc[:, 0:TT], xpad[:, 0:TT], AF.Identity, [xptok, "v_" + cwn, "v_" + cbn], [acctok],
                bias=cb[:, cidx:cidx + 1], scale=cw[:, 4 * cidx:4 * cidx + 1])
            for k in (1, 2, 3):
                STT(acc[:, 0:TT], xpad[:, k:k + TT], cw[:, 4 * cidx + k:4 * cidx + k + 1], acc[:, 0:TT], ALU.mult, ALU.add,
                    [xptok, acctok, "v_" + cwn], [acctok])

        def rstd_from(ps_ap, pstok, n, dst, dtok):
            ACT(dst, ps_ap, AF.Sqrt, pstok + ["eps"], [dtok], bias=eps_t[:, 0:1], scale=1.0 / n)
            S.op("dve", lambda e: e.reciprocal(out=dst, in_=dst), [dtok], [dtok])

        auxt = lambda nm: [(nm, b) for b in range((TT + 511) // 512)]
        eps_t = sb("eps_t", [128, 1], F32)
        S.op("pool", lambda e: e.memset(eps_t[:], EPS), [], ["eps"])

        acttoks = [("act", f) for f in range(c.FC // 128)]

        def phaseA(xsrc, t0, main):
            for j in range(NQ):
                DMA("sp", xt, xsrc[t0 + j * 128:t0 + (j + 1) * 128, :], [], ["ovl"] + acttoks, "ovl")
                ACT(xn, xt, AF.Square, ["ovl"], ["ovlb", "ssq%d" % j], accum=ssq_x[:, j:j + 1])
                ACT(ssq_x[:, j:j + 1], ssq_x[:, j:j + 1], AF.Sqrt, ["ssq%d" % j, "eps"], ["ssq%d" % j], bias=eps_t[:, 0:1], scale=1.0 / D)
                S.op("dve", lambda e, j=j: e.reciprocal(out=ssq_x[:, j:j + 1], in_=ssq_x[:, j:j + 1]), ["ssq%d" % j], ["ssq%d" % j])
                TS("dve", xn, xt, ssq_x[:, j:j + 1], None, ALU.mult, None, ["ovl", "ovlb", "ssq%d" % j], ["ovlb"])
                pt, ptok = nextbig()
                ptb = pt[:].bitcast(BF16)
                for k in range(KD):
                    TR(ptb[:, k * 128:(k + 1) * 128], xn[:, k * 128:(k + 1) * 128], identb[:], ["ovlb", "identb"], [ptok[(k * 64) // 512]])
                TTn("dve", hT[:, :, j * 128:(j + 1) * 128], ptb[:, 0:KD * 128].rearrange("p (k m) -> p k m", m=128),
                    bc(vec["g_pre"][:], 2, [128, KD, 128]), ALU.mult, ptok + ["v_g_pre"], [("hT", (j * 128) // HW)])
            stop(1)
            a0v = aux0[:, 0:QH].rearrange("p (q h) -> p q h", h=H)
            a1v = aux0[:, 512:512 + QH].rearrange("p (q h) -> p q h", h=H)
            for q in range(NQ):
                for k in range(KD):
                    MM(a0v[:, q, :], hT[:, k, q * 128:(q + 1) * 128], wdt[:, k, :], k == 0, k == KD - 1,
                       [("hT", (q * 128) // HW), "wdt"], [("aux0", 0)])
            TTn("dve", dtf["t0"][:], a0v, bc(vec["dtb"][:], 1, [128, NQ, H]), ALU.add, [("aux0", 0), "v_dtb"], ["dt_t0"])
            softplus(dtf["dt"][:], dtf["t0"][:], dtf["t1"][:], dtf["t2"][:], dtf["adt"][:], "dt_dt", "dt_t0", "dt_t1", "dt_t2", "dt_adt")
            TTn("dve", dtf["adt"][:], dtf["dt"][:], bc(a_bc[:], 1, [128, NQ, H]), ALU.mult, ["dt_dt", "a_bc"], ["dt_adt"])
            CP("dve", adt_hi[:], dtf["adt"][:], ["dt_adt"], ["adt_hi"])
            TTn("dve", adt_lo[:], dtf["adt"][:], adt_hi[:], ALU.subtract, ["dt_adt", "adt_hi"], ["adt_lo"])
            for q in range(NQ):
                MM(a0v[:, q, :], trib[:], adt_hi[:, q, :], True, False, ["trib", "adt_hi"], [("aux0", 0)])
                MM(a0v[:, q, :], trib[:], adt_lo[:, q, :], False, True, ["trib", "adt_lo"], [("aux0", 0)])
                MM(a1v[:, q, :], onesb[:], adt_hi[:, q, :], True, False, ["onesb", "adt_hi"], [("aux0", 1)])
                MM(a1v[:, q, :], onesb[:], adt_lo[:, q, :], False, True, ["onesb", "adt_lo"], [("aux0", 1)])
            CP("act", dtf["acs"][:], a0v, [("aux0", 0)], ["dt_acs"])
            if main and not getattr(cfg, "NOEA", 0):
                ACT(dtf["ea"][:], dtf["acs"][:], AF.Exp, ["dt_acs"], ["dt_ea"])
            ACT(dtf["cdb"][:], a1v, AF.Exp, [("aux0", 1)], ["dt_cdb"])
            TTn("dve", dtf["t0"][:], a1v, dtf["acs"][:], ALU.subtract, [("aux0", 1), "dt_acs"], ["dt_t0"])
            ACT(dtf["t0"][:], dtf["t0"][:], AF.Exp, ["dt_t0"], ["dt_t0"])
            TTn("dve", dtf["dtde"][:], dtf["t0"][:], dtf["dt"][:], ALU.mult, ["dt_t0", "dt_dt"], ["dt_dtde"])

            stop(2)
            xpad, xl, rr, ii, a2, hl, ge = (f4[i] for i in range(7))
            xlb, sqb = h2[0], h2[1]
            for blk in range(NBLK):
                bt, btok = proj_chunk(c.oxl + blk * 128)
                conv_chunk(bt, btok, car_l, blk, "cwl", "cbl", xpad, "f4_0", xl, "f4_1", "car_l")
                CP("pool", xlb[:], xl[:, 0:TT], ["f4_1"], ["h2_0"])
                for (wsb, wtok, bn, dst, dtok) in ((lwa, "lwa", "lba", rr, "f4_2"), (lwx, "lwx", "lbx", ii, "f4_3")):
                    gt, gtok = nextbig()
                    for hf in range(NH):
                        MM(gt[:, hf * HW:(hf + 1) * HW], wsb[:, blk, :], xlb[:, hf * HW:(hf + 1) * HW], True, True, [wtok, "h2_0"], [gtok[hf * HW // 512]])
                    ACT(dst[:, 0:TT], gt[:, 0:TT], AF.Sigmoid, gtok + ["v_" + bn], [dtok], bias=vec[bn][:, blk:blk + 1])
                ACT(rr[:, 0:TT], rr[:, 0:TT], AF.Exp, ["f4_2", "clru"], ["f4_2"], scale=clru[:, blk:blk + 1])
                ACT(a2[:, 0:TT], rr[:, 0:TT], AF.Square, ["f4_2"], ["f4_4"])
                ACT(a2[:, 0:TT], a2[:, 0:TT], AF.Sqrt, ["f4_4"], ["f4_4"], bias=1.0, scale=-1.0)
                TTn("pool", ii[:, 0:TT], ii[:, 0:TT], xl[:, 0:TT], ALU.mult, ["f4_3", "f4_1"], ["f4_3"])
                TTn("pool", ii[:, 0:TT], ii[:, 0:TT], a2[:, 0:TT], ALU.mult, ["f4_3", "f4_4"], ["f4_3"])
                S.op("dve", lambda e, blk=blk: e.tensor_tensor_scan(out=hl[:, 0:TT], data0=rr[:, 0:TT], data1=ii[:, 0:TT],
                                                                     initial=hcar[:, blk:blk + 1], op0=ALU.mult, op1=ALU.add),
                     ["f4_2", "f4_3", "hcar"], ["f4_5"], cost=(130.0 + 2 * TT) / 0.96)
                CP("pool", hcar[:, blk:blk + 1], hl[:, TT - 1:TT], ["f4_5"], ["hcar"])
                if main:
                    gt, gtok = proj_chunk(c.ogate + blk * 128)
                    gx = f4[7]
                    CP("act", gx[:, 0:TT], gt[:, 0:TT], gtok, ["f4_7"])
                    TTn("pool", ge[:, 0:TT], gx[:, 0:TT], gx[:, 0:TT], ALU.mult, ["f4_7"], ["f4_6"])
                    TS("pool", ge[:, 0:TT], ge[:, 0:TT], 0.044715, 1.0, ALU.mult, ALU.add, ["f4_6"], ["f4_6"])
                    TTn("pool", ge[:, 0:TT], ge[:, 0:TT], gx[:, 0:TT], ALU.mult, ["f4_6", "f4_7"], ["f4_6"])
                    ACT(ge[:, 0:TT], ge[:, 0:TT], AF.Sigmoid, ["f4_6"], ["f4_6"], scale=1.5957691216057308)
                    TTn("pool", ge[:, 0:TT], ge[:, 0:TT], gx[:, 0:TT], ALU.mult, ["f4_6", "f4_7"], ["f4_6"])
                    TTn("dve", ge[:, 0:TT], ge[:, 0:TT], hl[:, 0:TT], ALU.mult, ["f4_6", "f4_5"], ["f4_6"])
                    ACT(sqb[:], ge[:, 0:TT], AF.Square, ["f4_6"], ["h2_1"])
                    CP("pool", vT[:, KS + blk, :], ge[:, 0:TT], ["f4_6"], [("vT", KS + blk)])
                    for hf in range(NH):
                        MM(aux1[:, hf * HW:(hf + 1) * HW], onesb[:], sqb[:, hf * HW:(hf + 1) * HW], blk == 0, blk == NBLK - 1,
                           ["onesb", "h2_1"], [("aux1", hf * HW // 512)])
            if main:
                rl = f4[7]
                rstd_from(aux1[:, 0:TT], auxt("aux1"), c.DL, rl[:, 0:TT], "f4_7")
                for blk in range(NBLK):
                    STT(vT[:, KS + blk, :], vT[:, KS + blk, :], vec["lnorm"][:, blk:blk + 1], rl[:, 0:TT], ALU.mult, ALU.mult,
                        [("vT", KS + blk), "f4_7", "v_lnorm"], [("vT", KS + blk)])

            stop(3)
            xsT = (f4[2], f4[3])
            zs = (f4[4], f4[5])
            yT = (f4[6], f4[7])
            BT, CT = h2[0], h2[1]
            sqg = h2[2]
            ps_acsb = aux0[:, 0:512]
            ps_xs = aux0[:, 512:768]
            ps_y = aux0[:, 768:1024]
            ps_st = aux1[:, 0:256]
            ps_yo = aux1[:, 256:512]
            ps_sc = aux1[:, 512:640]
            ps_yT = aux1[:, 640:896]
            ps_bt = aux1[:, 896:960].bitcast(BF16)
            for g in range(G):
                hs = slice(g * 4, g * 4 + 4)
                pg = prev[:, g * 256:(g + 1) * 256]
                for i in range(2):
                    cidx = g * 2 + i
                    bt, btok = proj_chunk(c.oxs + cidx * 128)
                    conv_chunk(bt, btok, car_s, cidx, "cws", "cbs", f4[0], "f4_0", f4[1], "f4_1", "car_s")
                    ACT(xsT[i][:, 0:TT], f4[1][:, 0:TT], AF.Silu, ["f4_1"], ["f4_%d" % (2 + i)])
                for (cidx, dst, dtok) in ((KS + g, BT, "h2_0"), (KS + G + g, CT, "h2_1")):
                    bt, btok = proj_chunk(c.oxs + cidx * 128)
                    conv_chunk(bt, btok, car_s, cidx, "cws", "cbs", f4[0], "f4_0", f4[1], "f4_1", "car_s")
                    ACT(dst[:], f4[1][:, 0:TT], AF.Silu, ["f4_1"], [dtok])
                if main:
                    for i in range(2):
                        bt, btok = proj_chunk(g * 256 + i * 128)
                        ACT(zs[i][:, 0:TT], bt[:, 0:TT], AF.Silu, btok, ["f4_%d" % (4 + i)])
                stop(10)
                for q in range(NQ):
                    qs = slice(q * 128, (q + 1) * 128)
                    for i in range(2):
                        TR(ps_xs[:, i * 128:(i + 1) * 128], xsT[i][:, qs], identf[:], ["f4_%d" % (2 + i), "identf"], [("aux0", 1)])
                    CP("act", sm["xs"][:], ps_xs, [("aux0", 1)], ["sm_xs"])
                    xs3 = sm["xs"][:].rearrange("p (h d) -> p h d", d=64)
                    TTn("pool", sm["x6e"][:].rearrange("p (h d) -> p h d", d=64), xs3, bc(dtf["dtde"][:, q, hs], 2, [128, 4, 64]),
                        ALU.mult, ["sm_xs", "dt_dtde"], ["sm_x6e"])
                    TR(ps_bt, BT[:, qs], identb[:], ["h2_0", "identb"], [("aux1", 1)])
                    CP("act", sm["bt"][:], ps_bt, [("aux1", 1)], ["sm_bt"])
                    MM(ps_st, sm["bt"][:], sm["x6e"][:], True, True, ["sm_bt", "sm_x6e"], [("aux1", 0)])
                    if main:
                        CP("pool", prevb[:, q, :], pg, [("prev", g)], [("prevb", q)])
                    TTn("dve", pg.rearrange("p (h d) -> p h d", d=64), pg.rearrange("p (h d) -> p h d", d=64),
                        bc(dtf["cdb"][:, q, hs], 2, [128, 4, 64]), ALU.mult, [("prev", g), "dt_cdb"], [("prev", g)])
                    TTn("dve", pg, pg, ps_st, ALU.add, [("prev", g), ("aux1", 0)], [("prev", g)])
                    if not main:
                        continue
                    TTn("pool", sm["x6"][:].rearrange("p (h d) -> p h d", d=64), xs3, bc(dtf["dt"][:, q, hs], 2, [128, 4, 64]),
                        ALU.mult, ["sm_xs", "dt_dt"], ["sm_x6"])
                    TTn("pool", sm["xhi"][:], bc(adt_hi[:, q, hs], 2, [128, 4, 128]), bc(trib[:], 1, [128, 4, 128]), ALU.mult,
                        ["adt_hi", "trib"], ["sm_xhi"])
                    TTn("pool", sm["xlo"][:], bc(adt_lo[:, q, hs], 2, [128, 4, 128]), bc(trib[:], 1, [128, 4, 128]), ALU.mult,
                        ["adt_lo", "trib"], ["sm_xlo"])
                    MM(ps_acsb, onesb[:], sm["xhi"][:].rearrange("p h l -> p (h l)"), True, False, ["onesb", "sm_xhi"], [("aux0", 0)])
                    MM(ps_acsb, onesb[:], sm["xlo"][:].rearrange("p h l -> p (h l)"), False, True, ["onesb", "sm_xlo"], [("aux0", 0)])
                    MM(ps_sc, BT[:, qs], CT[:, qs], True, True, ["h2_0", "h2_1"], [("aux1", 1)])
                    TTn("dve", sm["t"][:], ps_acsb.rearrange("p (h l) -> p h l", l=128), bc(dtf["acs"][:, q, hs], 2, [128, 4, 128]),
                        ALU.subtract, [("aux0", 0), "dt_acs"], ["sm_t"])
                    TTn("pool", sm["t"][:], sm["t"][:], bc(negm[:], 1, [128, 4, 128]), ALU.add, ["sm_t", "negm"], ["sm_t"])
                    ACT(sm["t"][:], sm["t"][:], AF.Exp, ["sm_t"], ["sm_t"])
                    TTn("dve", sm["mt"][:], sm["t"][:], bc(ps_sc, 1, [128, 4, 128]), ALU.mult, ["sm_t", ("aux1", 1)], ["sm_mt"])
                    for h in range(4):
                        MM(ps_y[:, h * 64:(h + 1) * 64], sm["mt"][:, h, :], sm["x6"][:, h * 64:(h + 1) * 64], True, True,
                           ["sm_mt", "sm_x6"], [("aux0", 1)])
                    MM(ps_yo, CT[:, qs], prevb[:, q, :], True, True, ["h2_1", ("prevb", q)], [("aux1", 0)])
                    y3 = sm["y"][:].rearrange("p (h d) -> p h d", d=64)
                    TTn("dve", y3, ps_yo.rearrange("p (h d) -> p h d", d=64), bc(dtf["ea"][:, q, hs], 2, [128, 4, 64]), ALU.mult,
                        [("aux1", 0), "dt_ea"], ["sm_y"])
                    TTn("dve", sm["y"][:], sm["y"][:], ps_y, ALU.add, ["sm_y", ("aux0", 1)], ["sm_y"])
                    xsd = sm["t"][:].rearrange("p h l -> p (h l)")[:, 0:256]
                    TTn("pool", xsd.rearrange("p (h d) -> p h d", d=64), xs3, bc(vec["dsk"][:, hs], 2, [128, 4, 64]),
                        ALU.mult, ["sm_xs", "v_dsk", "sm_t"], ["sm_t"])
                    TTn("pool", sm["y"][:], sm["y"][:], xsd, ALU.add, ["sm_y", "sm_t"], ["sm_y"])
                    for i in range(2):
                        TR(ps_yT[:, i * 128:(i + 1) * 128], sm["y"][:, i * 128:(i + 1) * 128], identf[:], ["sm_y", "identf"], [("aux1", 1)])
                    for i in range(2):
                        CP("act", yT[i][:, qs], ps_yT[:, i * 128:(i + 1) * 128], [("aux1", 1)], ["f4_%d" % (6 + i)])
                stop(11)
                if not main:
                    continue
                gt, gtok = nextbig()
                for i in range(2):
                    TTn("pool", yT[i][:, 0:TT], yT[i][:, 0:TT], zs[i][:, 0:TT], ALU.mult, ["f4_%d" % (6 + i), "f4_%d" % (4 + i)], ["f4_%d" % (6 + i)])
                    ACT(sqg[:], yT[i][:, 0:TT], AF.Square, ["f4_%d" % (6 + i)], ["h2_2"])
                    for hf in range(NH):
                        MM(gt[:, hf * HW:(hf + 1) * HW], onesb[:], sqg[:, hf * HW:(hf + 1) * HW], i == 0, i == 1, ["onesb", "h2_2"], [gtok[hf * HW // 512]])
                rstd_from(gt[:, 0:TT], gtok, 256, f4[0][:, 0:TT], "f4_0")
                for i in range(2):
                    STT(vT[:, g * 2 + i, :], yT[i][:, 0:TT], vec["snorm"][:, g * 2 + i:g * 2 + i + 1], f4[0][:, 0:TT], ALU.mult, ALU.mult,
                        ["f4_%d" % (6 + i), "f4_0", "v_snorm"], [("vT", g * 2 + i)])
            stop(9)

        def phaseB(t0):
            stop(4)
            vtoks = [("vT", k) for k in range(KM)]
            for cc in range(KD):
                bt, btok = nextbig()
                nkk = (KM + KD - 1) // KD
                wl = []
                for part in range(nkk):
                    k0 = part * KD
                    nk = min(KD, KM - k0)
                    wl.append((load_w_chunk(w_out[k0 * 128:(k0 + nk) * 128, cc * 128:(cc + 1) * 128], nk), k0, nk))
                for hf in range(NH):
                    for (wb, wtok), k0, nk in wl:
                        for k in range(nk):
                            MM(bt[:, hf * HW:(hf + 1) * HW], wb[:, k, :], vT[:, k0 + k, hf * HW:(hf + 1) * HW], k0 + k == 0, k0 + k == KM - 1,
                               [wtok] + vtoks, [btok[hf * HW // 512]])
                st, stok = f4[cc % 2], "f4_%d" % (cc % 2)
                sq, sqtok = h2[cc % 2], "h2_%d" % (cc % 2)
                CP("act", st[:, 0:TT], bt[:, 0:TT], btok, [stok])
                ACT(sq[:], bt[:, 0:TT], AF.Square, btok, [sqtok])
                for hf in range(NH):
                    MM(aux0[:, hf * HW:(hf + 1) * HW], onesb[:], sq[:, hf * HW:(hf + 1) * HW], cc == 0, cc == KD - 1, ["onesb", sqtok], [("aux0", hf * HW // 512)])
                DMA("sp", msT[cc], st[:, 0:TT], [stok], [("msT", cc)], stok + "s")
            r1 = f4[7]
            rstd_from(aux0[:, 0:TT], auxt("aux0"), D, r1[:, 0:TT], "f4_7")
            stop(5)
            for cc in range(KD):
                ms, mstok = f4[cc % 2], "f4_%d" % (cc % 2)
                xb, xbtok = f4[2 + cc % 2], "f4_%d" % (2 + cc % 2)
                sq, sqtok = h2[cc % 2], "h2_%d" % (cc % 2)
                DMA("sp", ms[:, 0:TT], msT[cc], [("msT", cc)], [mstok], mstok)
                DMA("sp", xb[:, 0:TT].rearrange("p (j m) -> p j m", m=128),
                    xm[t0:t0 + TT, cc * 128:(cc + 1) * 128].rearrange("(j p) m -> p j m", p=128), [], [xbtok], xbtok)
                bt, btok = nextbig()
                for j in range(NQ):
                    TR(bt[:, j * 128:(j + 1) * 128], xb[:, j * 128:(j + 1) * 128], identf[:], [xbtok, "identf"], [btok[(j * 128) // 512]])
                STT(ms[:, 0:TT], ms[:, 0:TT], vec["g_post"][:, cc:cc + 1], r1[:, 0:TT], ALU.mult, ALU.mult, [mstok, "f4_7", "v_g_post"], [mstok])
                TTn("dve", ms[:, 0:TT], ms[:, 0:TT], bt[:, 0:TT], ALU.add, [mstok] + btok, [mstok])
                DMA("sp", x1s[cc], ms[:, 0:TT], [mstok], [("x1s", cc)], mstok + "s")
                ACT(sq[:], ms[:, 0:TT], AF.Square, [mstok], [sqtok])
                for hf in range(NH):
                    MM(aux1[:, hf * HW:(hf + 1) * HW], onesb[:], sq[:, hf * HW:(hf + 1) * HW], cc == 0, cc == KD - 1, ["onesb", sqtok], [("aux1", hf * HW // 512)])
            r2 = f4[6]
            rstd_from(aux1[:, 0:TT], auxt("aux1"), D, r2[:, 0:TT], "f4_6")
            stop(6)
            for cc in range(KD):
                xb, xbtok = f4[cc % 2], "f4_%d" % (cc % 2)
                DMA("sp", xb[:, 0:TT], x1s[cc], [("x1s", cc)], [xbtok], xbtok)
                STT(hT[:, cc, :], xb[:, 0:TT], vec["g_mlp"][:, cc:cc + 1], r2[:, 0:TT], ALU.mult, ALU.mult, [xbtok, "f4_6", "v_g_mlp"],
                    [("hT", hf) for hf in range(NH)])
            stop(7)
            KB = c.FC // 128
            htoks = [("hT", hf) for hf in range(NH)]
            for fb in range(c.DFF // c.FC):
                for f in range(KB):
                    wb, wtok = load_w_chunk(w1[:, fb * c.FC + f * 128:fb * c.FC + (f + 1) * 128])
                    bt, btok = nextbig()
                    for hf in range(NH):
                        for k in range(KD):
                            MM(bt[:, hf * HW:(hf + 1) * HW], wb[:, k, :], hT[:, k, hf * HW:(hf + 1) * HW], k == 0, k == KD - 1, [wtok] + htoks, [btok[hf * HW // 512]])
                    rl, rltok = f4[2 + f % 2], "f4_%d" % (2 + f % 2)
                    ACT(rl[:, 0:TT], bt[:, 0:TT], AF.Relu, btok, [rltok])
                    TTn("pool", actT[:, f, :], rl[:, 0:TT], rl[:, 0:TT], ALU.mult, [rltok], [("act", f), "ovl", "ovlb"])
                for cc in range(KD):
                    wb, wtok = load_w_chunk(w2[fb * c.FC:(fb + 1) * c.FC, cc * 128:(cc + 1) * 128], KB)
                    bt, btok = nextbig()
                    for hf in range(NH):
                        for k in range(KB):
                            MM(bt[:, hf * HW:(hf + 1) * HW], wb[:, k, :], actT[:, k, hf * HW:(hf + 1) * HW], k == 0, k == KB - 1,
                               [wtok, ("act", k)], [btok[hf * HW // 512]])
                    if fb == 0:
                        CP("act", accT[:, cc, :], bt[:, 0:TT], btok + vtoks, [("acc", cc)] + vtoks)
                    else:
                        TTn("dve", accT[:, cc, :], accT[:, cc, :], bt[:, 0:TT], ALU.add, btok + [("acc", cc)], [("acc", cc)])
            stop(8)
            for cc in range(KD):
                sq, sqtok = h2[cc % 2], "h2_%d" % (cc % 2)
                ACT(sq[:], accT[:, cc, :], AF.Square, [("acc", cc)] + vtoks, [sqtok])
                for hf in range(NH):
                    MM(aux0[:, hf * HW:(hf + 1) * HW], onesb[:], sq[:, hf * HW:(hf + 1) * HW], cc == 0, cc == KD - 1, ["onesb", sqtok], [("aux0", hf * HW // 512)])
            r3 = f4[7]
            rstd_from(aux0[:, 0:TT], auxt("aux0"), D, r3[:, 0:TT], "f4_7")
            for cc in range(KD):
                xb, xbtok = f4[cc % 2], "f4_%d" % (cc % 2)
                ob, obtok = f4[2 + cc % 2], "f4_%d" % (2 + cc % 2)
                os_, ostok = f4[4 + cc % 2], "f4_%d" % (4 + cc % 2)
                DMA("sp", xb[:, 0:TT], x1s[cc], [("x1s", cc)], [xbtok], xbtok)
                STT(ob[:, 0:TT], accT[:, cc, :], vec["g_pmlp"][:, cc:cc + 1], r3[:, 0:TT], ALU.mult, ALU.mult, [("acc", cc), "f4_7", "v_g_pmlp"] + vtoks, [obtok])
                TTn("pool", ob[:, 0:TT], ob[:, 0:TT], xb[:, 0:TT], ALU.add, [obtok, xbtok], [obtok])
                bt, btok = nextbig()
                for j in range(NQ):
                    TR(bt[:, j * 128:(j + 1) * 128], ob[:, j * 128:(j + 1) * 128], identf[:], [obtok, "identf"], [btok[(j * 128) // 512]])
                CP("act", os_[:, 0:TT], bt[:, 0:TT], btok, [ostok])
                DMA("sp", out[t0:t0 + TT, cc * 128:(cc + 1) * 128].rearrange("(j p) m -> p j m", p=128),
                    os_[:, 0:TT].rearrange("p (j m) -> p j m", m=128), [ostok], [("out", t0, cc)], ostok + "o")
                S.out_tokens.append(("out", t0, cc))

        def _program():
            for ti in range(c.NPRE):
                phaseA(xp, ti * TT, False)
            alltok = [("prev", g) for g in range(G)]
            TS("dve", prev[:], prev[:], flag[:, 0:1], None, ALU.mult, None, alltok + ["flag"], alltok)
            TS("dve", hcar[:], hcar[:], flag[:, 0:1], None, ALU.mult, None, ["hcar", "flag"], ["hcar"])
            for ti in range(c.NMAIN):
                phaseA(xm, ti * TT, True)
                phaseB(ti * TT)

        S.out_tokens = []

        class _Stop(Exception):
            pass

        def stop(level):
            if getattr(cfg, "STOP", 0) == level:
                raise _Stop()

        try:
            _program()
        except _Stop:
            pass
        S.op("dve", lambda e: e.engine_nop(), S.out_tokens, [])
        S.emit(reorder=getattr(cfg, "REORDER", True))
        nc._sched = S
    return nc


def _vec_layouts(cfg, p):
    c = cfg
    pm = lambda v, n: np.ascontiguousarray(np.asarray(v, np.float32).reshape(n, 128).T)
    bcast = lambda v: np.ascontiguousarray(np.broadcast_to(np.asarray(v, np.float32)[None, :], (128, len(v))))
    cw = lambda w, n: np.ascontiguousarray(np.asarray(w, np.float32).reshape(4, n, 128).transpose(2, 1, 0).reshape(128, n * 4))
    return dict(
        g_pre=pm(p["pre_mix_norm"], c.KD), g_post=pm(p["post_mix_norm"], c.KD), g_mlp=pm(p["pre_mlp_norm"], c.KD),
        g_pmlp=pm(p["post_mlp_norm"], c.KD),
        cws=cw(p["ssd_conv_w"], c.KX), cbs=pm(p["ssd_conv_b"], c.KX), cwl=cw(p["lru_conv_w"], c.NBLK), cbl=pm(p["lru_conv_b"], c.NBLK),
        lba=pm(p["lru_b_a"], c.NBLK), lbx=pm(p["lru_b_x"], c.NBLK), lam=pm(p["lru_lambda"], c.NBLK), lnorm=pm(p["lru_norm"], c.NBLK),
        snorm=pm(p["ssd_norm"], c.KS), dtb=bcast(p["ssd_dt_bias"]), alog=bcast(p["ssd_a_log"]), dsk=bcast(p["ssd_d"]))


def make_in_maps(cfg, inputs, n_batch, n_half):
    c = cfg
    p = {k: np.asarray(v)[0] for k, v in inputs.items() if k != "x"}
    x = np.asarray(inputs["x"], np.float32)
    vl = _vec_layouts(c, p)
    half = c.NMAIN * c.TT
    shared = dict(w_in=np.ascontiguousarray(p["w_in"], np.float32), w_out=np.ascontiguousarray(p["w_out"], np.float32),
                  w1=np.ascontiguousarray(p["w_mlp_in"], np.float32), w2=np.ascontiguousarray(p["w_mlp_out"], np.float32),
                  lwa=np.ascontiguousarray(p["lru_w_a"], np.float32), lwx=np.ascontiguousarray(p["lru_w_x"], np.float32), **vl)
    maps = []
    for b in range(n_batch):
        for s in range(n_half):
            m = dict(shared)
            m["xm"] = np.ascontiguousarray(x[b, s * half:(s + 1) * half])
            pre = np.zeros((max(c.NPRE, 1) * c.TT, c.D), np.float32)
            if s > 0:
                pre[:] = x[b, (s - 1) * half:s * half][-pre.shape[0]:]
            m["xp"] = pre
            m["flag"] = np.full((128, 1), 1.0 if s > 0 else 0.0, np.float32)
            maps.append(m)
    return maps


def kernel(**inputs):
    cfg = Cfg()
    nc = build(cfg)
    maps = make_in_maps(cfg, inputs, 4, 2)
    res = run_bass_kernel_spmd(nc, maps, core_ids=list(range(8)))
    x = np.asarray(inputs["x"])
    outp = np.empty(x.shape, np.float32)
    half = cfg.NMAIN * cfg.TT
    i = 0
    for b in range(4):
        for s in range(2):
            outp[b, s * half:(s + 1) * half] = np.asarray(res.results[i]["out"], np.float32)
            i += 1
    return outp
```

```python
import contextlib
import numpy as np
import concourse.bass as bass
import concourse.mybir as mybir
from concourse.bass_utils import run_bass_kernel_spmd

F32 = mybir.dt.float32
BF16 = mybir.dt.bfloat16
AF = mybir.ActivationFunctionType
ALU = mybir.AluOpType
EPS = 1e-6


class Cfg:
    def __init__(self, D=2048, H=32, G=8, DL=2048, DFF=8192, TT=1024, NMAIN=2, NPRE=2, FC=1024):
        self.D, self.H, self.G, self.DL, self.DFF = D, H, G, DL, DFF
        self.TT, self.NMAIN, self.NPRE, self.FC = TT, NMAIN, NPRE, FC
        self.P, self.N = 64, 128
        self.DS = H * 64
        self.HG = H // G
        assert self.HG == 4
        self.KD = D // 128
        self.KS = self.DS // 128
        self.DXBC = self.DS + 2 * G * 128
        self.KX = self.DXBC // 128
        self.DIN = self.DS + self.DXBC + H + 2 * DL
        self.NBLK = DL // 128
        self.DMIX = self.DS + DL
        self.KM = self.DMIX // 128
        self.KF = DFF // 128
        self.NQ = TT // 128
        self.HW = min(512, TT)
        self.NH = TT // self.HW
        self.oxs = self.DS
        self.oB = 2 * self.DS
        self.oC = 2 * self.DS + G * 128
        self.odt = self.DS + self.DXBC
        self.ogate = self.odt + H
        self.oxl = self.ogate + DL


class Sched:
    ENGS = ("pe", "act", "dve", "pool", "sp")

    def __init__(self, nc):
        self.nc = nc
        self.ops = []
        self.last_w = {}
        self.readers = {}
        self.dma_cnt = {}

    def op(self, eng, fn, reads=(), writes=(), dma_key=None, cost=300.0, nbytes=0):
        idx = len(self.ops)
        deps = set()
        for t in reads:
            w = self.last_w.get(t)
            if w is not None:
                deps.add(w)
        for t in writes:
            w = self.last_w.get(t)
            if w is not None:
                deps.add(w)
            deps.update(self.readers.get(t, ()))
        deps.discard(idx)
        o = dict(eng=eng, fn=fn, deps=deps, dma=dma_key is not None, key=dma_key, ms=False, cost=cost, nbytes=nbytes, label=getattr(self, 'label', ''))
        if dma_key is not None:
            n = self.dma_cnt.get(dma_key, 0) + 1
            self.dma_cnt[dma_key] = n
            o["dval"] = 16 * n
        self.ops.append(o)
        for t in reads:
            self.readers.setdefault(t, []).append(idx)
        for t in writes:
            self.last_w[t] = idx
            self.readers[t] = []
        return idx

    def schedule(self, window=256):
        ops = self.ops
        pend = {e: [] for e in self.ENGS}
        for i, o in enumerate(ops):
            pend[o["eng"]].append(i)
        head = {e: 0 for e in self.ENGS}
        tfree = {e: 0.0 for e in self.ENGS}
        done = [None] * len(ops)
        sched = [False] * len(ops)
        order = {e: [] for e in self.ENGS}
        pipe_free = 0.0
        remaining = len(ops)
        while remaining:
            progressed = False
            for e in sorted(self.ENGS, key=lambda x: tfree[x]):
                lst = pend[e]
                while head[e] < len(lst) and sched[lst[head[e]]]:
                    head[e] += 1
                if head[e] >= len(lst):
                    continue
                best, best_t = None, None
                seen = 0
                j = head[e]
                while j < len(lst) and seen < window:
                    i = lst[j]
                    j += 1
                    if sched[i]:
                        continue
                    seen += 1
                    t = tfree[e]
                    ok = True
                    for d in ops[i]["deps"]:
                        dt_ = done[d]
                        if dt_ is None:
                            ok = False
                            break
                        if dt_ > t:
                            t = dt_
                    if ok and (best is None or t < best_t - 1e-9):
                        best, best_t = i, t
                        if t <= tfree[e] + 1e-9:
                            break
                if best is None:
                    continue
                o = ops[best]
                if o["dma"]:
                    issue = 900.0 if e == "pool" else 120.0
                    xs_ = max(best_t + issue, pipe_free)
                    pipe_free = xs_ + o["nbytes"] / 250.0
                    done[best] = pipe_free + 2000.0
                    tfree[e] = best_t + issue
                else:
                    done[best] = best_t + o["cost"]
                    tfree[e] = done[best]
                sched[best] = True
                o['t0'] = best_t
                o['t1'] = done[best]
                order[e].append(best)
                remaining -= 1
                progressed = True
                break
            assert progressed, "list scheduler stuck"
        self.makespan = max(d for d in done if d is not None)
        return order

    def emit(self, reorder=True):
        nc, ops = self.nc, self.ops
        if reorder:
            order = self.schedule()
        else:
            order = {e: [i for i, o in enumerate(ops) if o["eng"] == e] for e in self.ENGS}
        pos = {}
        for e in self.ENGS:
            for k, i in enumerate(order[e]):
                pos[i] = k
        for i, o in enumerate(ops):
            latest = {}
            eff = []
            for d in o["deps"]:
                p = ops[d]
                if p["dma"]:
                    eff.append(d)
                    continue
                if p["eng"] == "pe" and o["eng"] == "pe" and not o["dma"]:
                    assert pos[d] < pos[i]
                    continue
                if p["eng"] == o["eng"]:
                    assert pos[d] < pos[i]
                if p["eng"] not in latest or pos[d] > pos[latest[p["eng"]]]:
                    latest[p["eng"]] = d
            o["eff"] = eff + list(latest.values())
            for d in latest.values():
                ops[d]["ms"] = True
        KSEM = 8
        cnt = {e: 0 for e in self.ENGS}
        for e in self.ENGS:
            for i in order[e]:
                o = ops[i]
                if o["ms"]:
                    o["msi"] = cnt[e]
                    cnt[e] += 1
        self.ms_counts = cnt
        with contextlib.ExitStack() as es:
            esem = {e: [es.enter_context(nc.semaphore("S_%s%d" % (e, i))) for i in range(KSEM)] for e in self.ENGS if cnt[e] > 0}
            dsem = {}
            for i, (k, n) in enumerate(self.dma_cnt.items()):
                r = 1 if k in ("setup", "setup_p") else max(1, (n * 16 + 479) // 480)
                dsem[k] = [es.enter_context(nc.semaphore("D_%d_%d" % (i, j))) for j in range(r)]
            block = es.enter_context(nc.Block())

            def semval(p):
                if p["dma"]:
                    lst = dsem[p["key"]]
                    if p["key"] in ("setup", "setup_p"):
                        return lst[0], 16 * self.dma_cnt[p["key"]]
                    n = p["dval"] // 16 - 1
                    return lst[n % len(lst)], 16 * (n // len(lst) + 1)
                i = p["msi"]
                return esem[p["eng"]][i % KSEM], i // KSEM + 1

            def run(engname, eng):
                waited = {}
                for oi in order[engname]:
                    o = ops[oi]
                    need = {}
                    for d in o["eff"]:
                        p = ops[d]
                        s, v = semval(p)
                        if need.get(id(s), (None, 0))[1] < v:
                            need[id(s)] = (s, v)
                    for s, v in need.values():
                        if waited.get(id(s), 0) < v:
                            eng.wait_ge(s, v)
                            waited[id(s)] = v
                    ins = o["fn"](eng)
                    if o["dma"]:
                        ins.then_inc(semval(o)[0], 16)
                    elif o["ms"]:
                        ins.then_inc(semval(o)[0], 1)

            block.tensor(lambda e: run("pe", e))
            block.scalar(lambda e: run("act", e))
            block.vector(lambda e: run("dve", e))
            block.gpsimd(lambda e: run("pool", e))
            block.sync(lambda e: run("sp", e))


def build(cfg):
    c = cfg
    D, H, G, TT, KD, KS, KX, KM, NQ, NH, HW, NBLK = c.D, c.H, c.G, c.TT, c.KD, c.KS, c.KX, c.KM, c.NQ, c.NH, c.HW, c.NBLK
    nc = bass.Bass("TRN2", target_bir_lowering=False)
    din = lambda name, shape: nc.dram_tensor(name, list(shape), F32, kind="ExternalInput").ap()
    xm = din("xm", [c.NMAIN * TT, D])
    xp = din("xp", [max(c.NPRE, 1) * TT, D])
    flag_d = din("flag", [128, 1])
    w_in = din("w_in", [D, c.DIN])
    w_out = din("w_out", [c.DMIX, D])
    w1 = din("w1", [D, c.DFF])
    w2 = din("w2", [c.DFF, D])
    lwa_d = din("lwa", [NBLK, 128, 128])
    lwx_d = din("lwx", [NBLK, 128, 128])
    vec_d = {}
    vec_shapes = dict(g_pre=[128, KD], g_post=[128, KD], g_mlp=[128, KD], g_pmlp=[128, KD],
                      cws=[128, KX * 4], cbs=[128, KX], cwl=[128, NBLK * 4], cbl=[128, NBLK],
                      lba=[128, NBLK], lbx=[128, NBLK], lam=[128, NBLK], lnorm=[128, NBLK],
                      snorm=[128, KS], dtb=[128, H], alog=[128, H], dsk=[128, H])
    for k, shp in vec_shapes.items():
        vec_d[k] = din(k, shp)
    out = nc.dram_tensor("out", [c.NMAIN * TT, D], F32, kind="ExternalOutput").ap()
    msT = nc.dram_tensor("msT", [KD, 128, TT], F32).ap()
    x1s = nc.dram_tensor("x1s", [KD, 128, TT], F32).ap()

    es = contextlib.ExitStack()
    with es:
        sb = lambda name, shape, dt: es.enter_context(nc.sbuf_tensor(name, list(shape), dt))
        S = Sched(nc)
        if getattr(cfg, "PAD", 0):
            sb("pad", [128, cfg.PAD // 4], F32)
        def ACT(out_, in_, func, r, w, bias=None, scale=None, accum=None):
            kw = {}
            if bias is not None:
                kw["bias"] = bias
            if scale is not None:
                kw["scale"] = scale
            if accum is not None:
                kw["accum_out"] = accum
            S.op("act", lambda e: e.activation(out=out_, in_=in_, func=func, **kw), r, w, cost=(224.0 + in_.free_size()) / 1.2 + (90.0 if accum is not None else 0.0))

        def TTn(eng, out_, in0, in1, op, r, w):
            S.op(eng, lambda e: e.tensor_tensor(out=out_, in0=in0, in1=in1, op=op), r, w, cost=(130.0 + out_.free_size()) / 0.96)

        def TS(eng, out_, in0, s1, s2, op0, op1, r, w):
            if s2 is None:
                S.op(eng, lambda e: e.tensor_scalar(out=out_, in0=in0, scalar1=s1, scalar2=None, op0=op0), r, w, cost=(130.0 + out_.free_size()) / 0.96)
            else:
                S.op(eng, lambda e: e.tensor_scalar(out=out_, in0=in0, scalar1=s1, scalar2=s2, op0=op0, op1=op1), r, w, cost=(130.0 + out_.free_size()) / 0.96)

        def STT(out_, in0, scalar, in1, op0, op1, r, w):
            S.op("dve", lambda e: e.scalar_tensor_tensor(out=out_, in0=in0, scalar=scalar, in1=in1, op0=op0, op1=op1), r, w, cost=(130.0 + out_.free_size()) / 0.96)

        def CP(eng, out_, in_, r, w):
            if eng == "act":
                S.op("act", lambda e: e.activation(out=out_, in_=in_, func=AF.Copy), r, w, cost=(224.0 + in_.free_size()) / 1.2)
            else:
                S.op(eng, lambda e: e.tensor_copy(out=out_, in_=in_), r, w, cost=(130.0 + out_.free_size()) / 0.96)

        def MM(out_, lhsT, rhs, start, stop, r, w):
            S.op("pe", lambda e: e.matmul(out_, lhsT=lhsT, rhs=rhs, start=start, stop=stop), r, w, cost=max(64.0, rhs.free_size()) / 2.4 + 12.0)

        def TR(out_, in_, ident, r, w):
            S.op("pe", lambda e: e.transpose(out_, in_, ident), r, w, cost=110.0)

        def DMA(q, out_, in_, r, w, key):
            S.op(q, lambda e: e.dma_start(out=out_, in_=in_), r, w, dma_key=key, nbytes=max(out_.nbytes(), in_.nbytes()))

        def bc(ap, axis, shape):
            return ap.unsqueeze(axis).to_broadcast(list(shape))

        identb = sb("identb", [128, 128], BF16)
        identf = sb("identf", [128, 128], F32)
        onesb = sb("onesb", [128, 128], BF16)
        trib = sb("trib", [128, 128], BF16)
        negm = sb("negm", [128, 128], F32)
        vec = {k: sb("v_" + k, shp, F32) for k, shp in vec_shapes.items()}
        flag = sb("flag_sb", [128, 1], F32)
        clru = sb("clru", [128, NBLK], F32)
        a_bc = sb("a_bc", [128, H], F32)
        wdt = sb("wdt", [128, KD, H], BF16)
        lwa = sb("lwa_sb", [128, NBLK, 128], BF16)
        lwx = sb("lwx_sb", [128, NBLK, 128], BF16)
        prev = sb("prev", [128, G * 256], F32)
        prevb = sb("prevb", [128, NQ, 256], BF16)
        hcar = sb("hcar", [128, NBLK], F32)
        car_s = sb("car_s", [128, KX, 3], F32)
        car_l = sb("car_l", [128, NBLK, 3], F32)
        QH = NQ * H
        dtf = {k: sb("dt_" + k, [128, NQ, H], F32) for k in ("dt", "adt", "acs", "ea", "cdb", "dtde", "t0", "t1", "t2")}
        adt_hi = sb("adt_hi", [128, NQ, H], BF16)
        adt_lo = sb("adt_lo", [128, NQ, H], BF16)
        ssq_x = sb("ssq_x", [128, NQ], F32)
        RAK = max(KD, (KM + 1) // 2)
        RA = sb("RA", [128, RAK * TT], F32)
        vT = RA[:].bitcast(BF16).rearrange("p (k t) -> p k t", t=TT)
        accT = RA[:].rearrange("p (k t) -> p k t", t=TT)
        hT = sb("hT", [128, KD, TT], BF16)
        OVW = max(c.FC // 128 * TT // 2, D + D // 2)
        ovl = sb("ovl", [128, OVW], F32)
        xt = ovl[:, 0:D]
        xn = ovl[:, D:D + D // 2].bitcast(BF16)
        actT = ovl[:, 0:c.FC // 128 * TT // 2].bitcast(BF16).rearrange("p (k t) -> p k t", t=TT)
        ALT = []
        if getattr(cfg, "ALTSSD", False) and OVW >= 3 * (TT + 4):
            ALT = ["alt_xs0", "alt_xs1", "alt_b", "alt_c"]
            alt_xs = (ovl[:, 0:TT + 4], ovl[:, TT + 4:2 * (TT + 4)])
            alt_b = ovl[:, 2 * (TT + 4):2 * (TT + 4) + TT // 2].bitcast(BF16)
            alt_c = ovl[:, 2 * (TT + 4) + TT // 2:2 * (TT + 4) + TT].bitcast(BF16)
        NW = 3
        wbuf = [sb("wbuf%d" % i, [128, max(KD, c.FC // 128), 128], BF16) for i in range(NW)]
        NF = 8
        f4 = [sb("f4_%d" % i, [128, TT + 4], F32) for i in range(NF)]
        h2 = [sb("h2_%d" % i, [128, TT], BF16) for i in range(3)]
        sm = {k: sb("sm_" + k, shp, dt) for k, (shp, dt) in dict(
            xs=([128, 256], F32), x6=([128, 256], BF16), x6e=([128, 256], BF16), bt=([128, 128], BF16),
            xhi=([128, 4, 128], BF16), xlo=([128, 4, 128], BF16), t=([128, 4, 128], F32),
            mt=([128, 4, 128], BF16), y=([128, 256], F32)).items()}
        ps = lambda name: es.enter_context(nc.psum_tensor(name, [128, 1024], F32))
        big = [ps("big0"), ps("big1")]
        aux0, aux1 = ps("aux0"), ps("aux1")
        bigi = [0]

        def nextbig():
            i = bigi[0] % 2
            bigi[0] += 1
            return big[i], [("big%d" % i, b) for b in range((TT + 511) // 512)]

        wi = [0]

        def nextw():
            i = wi[0] % NW
            wi[0] += 1
            return wbuf[i], "wbuf%d" % i

        for k in vec_shapes:
            DMA("sp", vec[k][:], vec_d[k][:], [], ["v_" + k], "setup")
        DMA("sp", flag[:], flag_d[:], [], ["flag"], "setup")
        DMA("pool", wdt[:], w_in[:, c.odt:c.odt + H].rearrange("(k p) h -> p k h", p=128), [], ["wdt"], "setup_p")
        DMA("pool", lwa[:], lwa_d.rearrange("b i j -> i b j"), [], ["lwa"], "setup_p")
        DMA("pool", lwx[:], lwx_d.rearrange("b i j -> i b j"), [], ["lwx"], "setup_p")

        def mask_const(t, tok, fill, pat, cm, op):
            S.op("pool", lambda e: e.affine_select(out=t[:], in_=t[:], pattern=pat, compare_op=op, fill=fill,
                                                   base=0, channel_multiplier=cm), [tok], [tok])

        S.op("pool", lambda e: e.memset(identb[:], 1.0), [], ["identb"])
        mask_const(identb, "identb", 0.0, [[-1, 128]], 1, ALU.is_equal)
        S.op("pool", lambda e: e.memset(identf[:], 1.0), [], ["identf"])
        mask_const(identf, "identf", 0.0, [[-1, 128]], 1, ALU.is_equal)
        S.op("pool", lambda e: e.memset(onesb[:], 1.0), [], ["onesb"])
        S.op("pool", lambda e: e.memset(trib[:], 1.0), [], ["trib"])
        mask_const(trib, "trib", 0.0, [[1, 128]], -1, ALU.is_ge)
        S.op("pool", lambda e: e.memset(negm[:], 0.0), [], ["negm"])
        mask_const(negm, "negm", -30000.0, [[1, 128]], -1, ALU.is_ge)
        S.op("pool", lambda e: e.memset(prev[:], 0.0), [], ["prev"])
        S.op("pool", lambda e: e.memset(hcar[:], 0.0), [], ["hcar"])
        S.op("pool", lambda e: e.memset(car_s[:], 0.0), [], ["car_s"])
        S.op("pool", lambda e: e.memset(car_l[:], 0.0), [], ["car_l"])

        def log1p_small(out_, e_, tmpw, tmpq, tok_o, tok_e, tok_w, tok_q):
            TS("dve", tmpw, e_, 2.0, None, ALU.add, None, [tok_e], [tok_w])
            S.op("dve", lambda e: e.reciprocal(out=tmpw, in_=tmpw), [tok_w], [tok_w])
            TTn("dve", tmpw, tmpw, e_, ALU.mult, [tok_w, tok_e], [tok_w])
            TTn("dve", out_, tmpw, tmpw, ALU.mult, [tok_w], [tok_o])
            TS("dve", tmpq, out_, 1.0 / 11.0, None, ALU.mult, None, [tok_o], [tok_q])
            for cst in (1.0 / 9.0, 1.0 / 7.0, 1.0 / 5.0, 1.0 / 3.0):
                STT(tmpq, tmpq, cst, out_, ALU.add, ALU.mult, [tok_q, tok_o], [tok_q])
            TS("dve", tmpq, tmpq, 1.0, None, ALU.add, None, [tok_q], [tok_q])
            TTn("dve", tmpq, tmpq, tmpw, ALU.mult, [tok_q, tok_w], [tok_q])
            TS("dve", out_, tmpq, 2.0, None, ALU.mult, None, [tok_q], [tok_o])

        def softplus(out_, x_, ta, tb, tcc, tok_o, tok_x, tok_a, tok_b, tok_c):
            TS("dve", ta, x_, -1.0, None, ALU.mult, None, [tok_x], [tok_a])
            TTn("dve", ta, ta, x_, ALU.max, [tok_a, tok_x], [tok_a])
            ACT(ta, ta, AF.Exp, [tok_a], [tok_a], scale=-1.0)
            log1p_small(tb, ta, tcc, out_, tok_b, tok_a, tok_c, tok_o)
            TS("dve", ta, x_, 0.0, None, ALU.max, None, [tok_x], [tok_a])
            TTn("dve", out_, ta, tb, ALU.add, [tok_a, tok_b], [tok_o])

        t0s = dtf["t0"][:, 0, 0:NBLK] if NBLK <= H else None
        assert NBLK <= H
        t1s, t2s, t3s, t4s = (dtf[k][:, 0, 0:NBLK] for k in ("t1", "t2", "dt", "adt"))
        TS("dve", t0s, vec["lam"][:], -1.0, None, ALU.mult, None, ["v_lam"], ["dt_t0"])
        softplus(t1s, t0s, t2s, t3s, t4s, "dt_t1", "dt_t0", "dt_t2", "dt_dt", "dt_adt")
        TS("dve", clru[:], t1s, -8.0, None, ALU.mult, None, ["dt_t1"], ["clru"])
        ACT(a_bc[:], vec["alog"][:], AF.Exp, ["v_alog"], ["a_bc"])
        TS("dve", a_bc[:], a_bc[:], -1.0, None, ALU.mult, None, ["a_bc"], ["a_bc"])

        def load_w_chunk(src_ap, nk=None):
            wb, tok = nextw()
            nk = KD if nk is None else nk
            DMA("pool", wb[:, 0:nk, :], src_ap.rearrange("(k p) m -> p k m", p=128), [], [tok], tok)
            return wb, tok

        def proj_chunk(col0):
            wb, wtok = load_w_chunk(w_in[:, col0:col0 + 128])
            bt, btok = nextbig()
            for hf in range(NH):
                for k in range(KD):
                    MM(bt[:, hf * HW:(hf + 1) * HW], wb[:, k, :], hT[:, k, hf * HW:(hf + 1) * HW], k == 0, k == KD - 1,
                       [wtok, ("hT", hf)], [btok[hf * HW // 512]])
            return bt, btok

        def conv_chunk(bt, btok, car, cidx, cwn, cbn, xpad, xptok, acc, acctok, car_tok):
            cw, cb = vec[cwn], vec[cbn]
            CP("pool", xpad[:, 0:3], car[:, cidx, :], [car_tok], [xptok])
            CP("act", xpad[:, 3:3 + TT], bt[:, 0:TT], btok + [xptok], [xptok])
            CP("pool", car[:, cidx, :], xpad[:, TT:TT + 3], [xptok], [car_tok])
            ACT(acc[:, 0:TT], xpad[:, 0:TT], AF.Identity, [xptok, "v_" + cwn, "v_" + cbn], [acctok],
                bias=cb[:, cidx:cidx + 1], scale=cw[:, 4 * cidx:4 * cidx + 1])
            for k in (1, 2, 3):
                STT(acc[:, 0:TT], xpad[:, k:k + TT], cw[:, 4 * cidx + k:4 * cidx + k + 1], acc[:, 0:TT], ALU.mult, ALU.add,
                    [xptok, acctok, "v_" + cwn], [acctok])

        def rstd_from(ps_ap, pstok, n, dst, dtok):
            ACT(dst, ps_ap, AF.Sqrt, pstok + ["eps"], [dtok], bias=eps_t[:, 0:1], scale=1.0 / n)
            S.op("dve", lambda e: e.reciprocal(out=dst, in_=dst), [dtok], [dtok])

        auxt = lambda nm: [(nm, b) for b in range((TT + 511) // 512)]
        eps_t = sb("eps_t", [128, 1], F32)
        S.op("pool", lambda e: e.memset(eps_t[:], EPS), [], ["eps"])

        acttoks = [("act", f) for f in range(c.FC // 128)]

        def phaseA(xsrc, t0, main, last_prefix=False):
            S.label = 'A0'
            for j in range(NQ):
                DMA("sp", xt, xsrc[t0 + j * 128:t0 + (j + 1) * 128, :], [], ["ovl"] + acttoks + ALT, "ovl")
                ACT(xn, xt, AF.Square, ["ovl"], ["ovlb", "ssq%d" % j] + ALT, accum=ssq_x[:, j:j + 1])
                ACT(ssq_x[:, j:j + 1], ssq_x[:, j:j + 1], AF.Sqrt, ["ssq%d" % j, "eps"], ["ssq%d" % j], bias=eps_t[:, 0:1], scale=1.0 / D)
                S.op("dve", lambda e, j=j: e.reciprocal(out=ssq_x[:, j:j + 1], in_=ssq_x[:, j:j + 1]), ["ssq%d" % j], ["ssq%d" % j])
                TS("dve", xn, xt, ssq_x[:, j:j + 1], None, ALU.mult, None, ["ovl", "ovlb", "ssq%d" % j], ["ovlb"])
                pt, ptok = nextbig()
                ptb = pt[:].bitcast(BF16)
                for k in range(KD):
                    TR(ptb[:, k * 128:(k + 1) * 128], xn[:, k * 128:(k + 1) * 128], identb[:], ["ovlb", "identb"], [ptok[(k * 64) // 512]])
                TTn("dve", hT[:, :, j * 128:(j + 1) * 128], ptb[:, 0:KD * 128].rearrange("p (k m) -> p k m", m=128),
                    bc(vec["g_pre"][:], 2, [128, KD, 128]), ALU.mult, ptok + ["v_g_pre"], [("hT", (j * 128) // HW)])
            stop(1)
            S.label = 'A1'
            a0v = aux0[:, 0:QH].rearrange("p (q h) -> p q h", h=H)
            a1v = aux0[:, 512:512 + QH].rearrange("p (q h) -> p q h", h=H)
            for q in range(NQ):
                for k in range(KD):
                    MM(a0v[:, q, :], hT[:, k, q * 128:(q + 1) * 128], wdt[:, k, :], k == 0, k == KD - 1,
                       [("hT", (q * 128) // HW), "wdt"], [("aux0", 0)])
            TTn("dve", dtf["t0"][:], a0v, bc(vec["dtb"][:], 1, [128, NQ, H]), ALU.add, [("aux0", 0), "v_dtb"], ["dt_t0"])
            softplus(dtf["dt"][:], dtf["t0"][:], dtf["t1"][:], dtf["t2"][:], dtf["adt"][:], "dt_dt", "dt_t0", "dt_t1", "dt_t2", "dt_adt")
            TTn("dve", dtf["adt"][:], dtf["dt"][:], bc(a_bc[:], 1, [128, NQ, H]), ALU.mult, ["dt_dt", "a_bc"], ["dt_adt"])
            CP("dve", adt_hi[:], dtf["adt"][:], ["dt_adt"], ["adt_hi"])
            TTn("dve", adt_lo[:], dtf["adt"][:], adt_hi[:], ALU.subtract, ["dt_adt", "adt_hi"], ["adt_lo"])
            for q in range(NQ):
                MM(a0v[:, q, :], trib[:], adt_hi[:, q, :], True, False, ["trib", "adt_hi"], [("aux0", 0)])
                MM(a0v[:, q, :], trib[:], adt_lo[:, q, :], False, True, ["trib", "adt_lo"], [("aux0", 0)])
                MM(a1v[:, q, :], onesb[:], adt_hi[:, q, :], True, False, ["onesb", "adt_hi"], [("aux0", 1)])
                MM(a1v[:, q, :], onesb[:], adt_lo[:, q, :], False, True, ["onesb", "adt_lo"], [("aux0", 1)])
            CP("act", dtf["acs"][:], a0v, [("aux0", 0)], ["dt_acs"])
            if main and not getattr(cfg, "NOEA", 0):
                ACT(dtf["ea"][:], dtf["acs"][:], AF.Exp, ["dt_acs"], ["dt_ea"])
            ACT(dtf["cdb"][:], a1v, AF.Exp, [("aux0", 1)], ["dt_cdb"])
            TTn("dve", dtf["t0"][:], a1v, dtf["acs"][:], ALU.subtract, [("aux0", 1), "dt_acs"], ["dt_t0"])
            ACT(dtf["t0"][:], dtf["t0"][:], AF.Exp, ["dt_t0"], ["dt_t0"])
            TTn("dve", dtf["dtde"][:], dtf["t0"][:], dtf["dt"][:], ALU.mult, ["dt_t0", "dt_dt"], ["dt_dtde"])

            stop(2)
            S.label = 'LRU' + ('m' if main else 'p')
            xpad, xl, rr, ii, a2, hl, ge = (f4[i] for i in range(7))
            xlb, sqb = h2[0], h2[1]
            for blk in range(NBLK):
                bt, btok = proj_chunk(c.oxl + blk * 128)
                conv_chunk(bt, btok, car_l, blk, "cwl", "cbl", xpad, "f4_0", xl, "f4_1", "car_l")
                CP("pool", xlb[:], xl[:, 0:TT], ["f4_1"], ["h2_0"])
                for (wsb, wtok, bn, dst, dtok) in ((lwa, "lwa", "lba", rr, "f4_2"), (lwx, "lwx", "lbx", ii, "f4_3")):
                    gt, gtok = nextbig()
                    for hf in range(NH):
                        MM(gt[:, hf * HW:(hf + 1) * HW], wsb[:, blk, :], xlb[:, hf * HW:(hf + 1) * HW], True, True, [wtok, "h2_0"], [gtok[hf * HW // 512]])
                    ACT(dst[:, 0:TT], gt[:, 0:TT], AF.Sigmoid, gtok + ["v_" + bn], [dtok], bias=vec[bn][:, blk:blk + 1])
                ACT(rr[:, 0:TT], rr[:, 0:TT], AF.Exp, ["f4_2", "clru"], ["f4_2"], scale=clru[:, blk:blk + 1])
                ACT(a2[:, 0:TT], rr[:, 0:TT], AF.Square, ["f4_2"], ["f4_4"])
                ACT(a2[:, 0:TT], a2[:, 0:TT], AF.Sqrt, ["f4_4"], ["f4_4"], bias=1.0, scale=-1.0)
                TTn("pool", ii[:, 0:TT], ii[:, 0:TT], xl[:, 0:TT], ALU.mult, ["f4_3", "f4_1"], ["f4_3"])
                TTn("pool", ii[:, 0:TT], ii[:, 0:TT], a2[:, 0:TT], ALU.mult, ["f4_3", "f4_4"], ["f4_3"])
                S.op("dve", lambda e, blk=blk: e.tensor_tensor_scan(out=hl[:, 0:TT], data0=rr[:, 0:TT], data1=ii[:, 0:TT],
                                                                     initial=hcar[:, blk:blk + 1], op0=ALU.mult, op1=ALU.add),
                     ["f4_2", "f4_3", "hcar"], ["f4_5"], cost=(130.0 + 2 * TT) / 0.96)
                CP("pool", hcar[:, blk:blk + 1], hl[:, TT - 1:TT], ["f4_5"], ["hcar"])
                if main:
                    gt, gtok = proj_chunk(c.ogate + blk * 128)
                    gx = f4[7]
                    CP("act", gx[:, 0:TT], gt[:, 0:TT], gtok, ["f4_7"])
                    TTn("pool", ge[:, 0:TT], gx[:, 0:TT], gx[:, 0:TT], ALU.mult, ["f4_7"], ["f4_6"])
                    TS("pool", ge[:, 0:TT], ge[:, 0:TT], 0.044715, 1.0, ALU.mult, ALU.add, ["f4_6"], ["f4_6"])
                    TTn("pool", ge[:, 0:TT], ge[:, 0:TT], gx[:, 0:TT], ALU.mult, ["f4_6", "f4_7"], ["f4_6"])
                    ACT(ge[:, 0:TT], ge[:, 0:TT], AF.Sigmoid, ["f4_6"], ["f4_6"], scale=1.5957691216057308)
                    TTn("pool", ge[:, 0:TT], ge[:, 0:TT], gx[:, 0:TT], ALU.mult, ["f4_6", "f4_7"], ["f4_6"])
                    TTn("dve", ge[:, 0:TT], ge[:, 0:TT], hl[:, 0:TT], ALU.mult, ["f4_6", "f4_5"], ["f4_6"])
                    ACT(sqb[:], ge[:, 0:TT], AF.Square, ["f4_6"], ["h2_1"])
                    CP("pool", vT[:, KS + blk, :], ge[:, 0:TT], ["f4_6"], [("vT", KS + blk)])
                    for hf in range(NH):
                        MM(aux1[:, hf * HW:(hf + 1) * HW], onesb[:], sqb[:, hf * HW:(hf + 1) * HW], blk == 0, blk == NBLK - 1,
                           ["onesb", "h2_1"], [("aux1", hf * HW // 512)])
            if main:
                rl = f4[7]
                rstd_from(aux1[:, 0:TT], auxt("aux1"), c.DL, rl[:, 0:TT], "f4_7")
                for blk in range(NBLK):
                    STT(vT[:, KS + blk, :], vT[:, KS + blk, :], vec["lnorm"][:, blk:blk + 1], rl[:, 0:TT], ALU.mult, ALU.mult,
                        [("vT", KS + blk), "f4_7", "v_lnorm"], [("vT", KS + blk)])

            stop(3)
            S.label = 'SSD' + ('m' if main else 'p')
            zs = (f4[4], f4[5])
            yT = (f4[6], f4[7])
            sqg = h2[2]
            ps_acsb = aux0[:, 0:512]
            ps_xs = aux0[:, 512:768]
            ps_y = aux0[:, 768:1024]
            ps_st = aux1[:, 0:256]
            ps_yo = aux1[:, 256:512]
            ps_sc = aux1[:, 512:640]
            ps_yT = aux1[:, 640:896]
            ps_bt = aux1[:, 896:960].bitcast(BF16)
            for g in range(G):
                hs = slice(g * 4, g * 4 + 4)
                pg = prev[:, g * 256:(g + 1) * 256]
                if ALT and g % 2 == 1:
                    xsT, BT, CT = alt_xs, alt_b, alt_c
                    xstok, btk, ctk = ("alt_xs0", "alt_xs1"), "alt_b", "alt_c"
                    altw = ["ovl", "ovlb"] + acttoks
                else:
                    xsT, BT, CT = (f4[2], f4[3]), h2[0], h2[1]
                    xstok, btk, ctk = ("f4_2", "f4_3"), "h2_0", "h2_1"
                    altw = []
                for i in range(2):
                    cidx = g * 2 + i
                    bt, btok = proj_chunk(c.oxs + cidx * 128)
                    conv_chunk(bt, btok, car_s, cidx, "cws", "cbs", f4[0], "f4_0", f4[1], "f4_1", "car_s")
                    ACT(xsT[i][:, 0:TT], f4[1][:, 0:TT], AF.Silu, ["f4_1"], [xstok[i]] + altw)
                for (cidx, dst, dtok) in ((KS + g, BT, btk), (KS + G + g, CT, ctk)):
                    if cidx >= KS + G and not main and c.NMAIN > 0 and not last_prefix:
                        continue
                    bt, btok = proj_chunk(c.oxs + cidx * 128)
                    conv_chunk(bt, btok, car_s, cidx, "cws", "cbs", f4[0], "f4_0", f4[1], "f4_1", "car_s")
                    ACT(dst[:, 0:TT], f4[1][:, 0:TT], AF.Silu, ["f4_1"], [dtok] + altw)
                if main:
                    for i in range(2):
                        bt, btok = proj_chunk(g * 256 + i * 128)
                        ACT(zs[i][:, 0:TT], bt[:, 0:TT], AF.Silu, btok, ["f4_%d" % (4 + i)])
                stop(10)
                for q in range(NQ):
                    qs = slice(q * 128, (q + 1) * 128)
                    for i in range(2):
                        TR(ps_xs[:, i * 128:(i + 1) * 128], xsT[i][:, qs], identf[:], [xstok[i], "identf"], [("aux0", 1)])
                    CP("act", sm["xs"][:], ps_xs, [("aux0", 1)], ["sm_xs"])
                    xs3 = sm["xs"][:].rearrange("p (h d) -> p h d", d=64)
                    TTn("pool", sm["x6e"][:].rearrange("p (h d) -> p h d", d=64), xs3, bc(dtf["dtde"][:, q, hs], 2, [128, 4, 64]),
                        ALU.mult, ["sm_xs", "dt_dtde"], ["sm_x6e"])
                    TR(ps_bt, BT[:, qs], identb[:], [btk, "identb"], [("aux1", 1)])
                    CP("act", sm["bt"][:], ps_bt, [("aux1", 1)], ["sm_bt"])
                    MM(ps_st, sm["bt"][:], sm["x6e"][:], True, True, ["sm_bt", "sm_x6e"], [("aux1", 0)])
                    if main:
                        CP("pool", prevb[:, q, :], pg, [("prev", g)], [("prevb", q)])
                    TTn("dve", pg.rearrange("p (h d) -> p h d", d=64), pg.rearrange("p (h d) -> p h d", d=64),
                        bc(dtf["cdb"][:, q, hs], 2, [128, 4, 64]), ALU.mult, [("prev", g), "dt_cdb"], [("prev", g)])
                    TTn("dve", pg, pg, ps_st, ALU.add, [("prev", g), ("aux1", 0)], [("prev", g)])
                    if not main:
                        continue
                    TTn("pool", sm["x6"][:].rearrange("p (h d) -> p h d", d=64), xs3, bc(dtf["dt"][:, q, hs], 2, [128, 4, 64]),
                        ALU.mult, ["sm_xs", "dt_dt"], ["sm_x6"])
                    TTn("pool", sm["xhi"][:], bc(adt_hi[:, q, hs], 2, [128, 4, 128]), bc(trib[:], 1, [128, 4, 128]), ALU.mult,
                        ["adt_hi", "trib"], ["sm_xhi"])
                    TTn("pool", sm["xlo"][:], bc(adt_lo[:, q, hs], 2, [128, 4, 128]), bc(trib[:], 1, [128, 4, 128]), ALU.mult,
                        ["adt_lo", "trib"], ["sm_xlo"])
                    MM(ps_acsb, onesb[:], sm["xhi"][:].rearrange("p h l -> p (h l)"), True, False, ["onesb", "sm_xhi"], [("aux0", 0)])
                    MM(ps_acsb, onesb[:], sm["xlo"][:].rearrange("p h l -> p (h l)"), False, True, ["onesb", "sm_xlo"], [("aux0", 0)])
                    MM(ps_sc, BT[:, qs], CT[:, qs], True, True, [btk, ctk], [("aux1", 1)])
                    TTn("dve", sm["t"][:], ps_acsb.rearrange("p (h l) -> p h l", l=128), bc(dtf["acs"][:, q, hs], 2, [128, 4, 128]),
                        ALU.subtract, [("aux0", 0), "dt_acs"], ["sm_t"])
                    TTn("pool", sm["t"][:], sm["t"][:], bc(negm[:], 1, [128, 4, 128]), ALU.add, ["sm_t", "negm"], ["sm_t"])
                    ACT(sm["t"][:], sm["t"][:], AF.Exp, ["sm_t"], ["sm_t"])
                    TTn("dve", sm["mt"][:], sm["t"][:], bc(ps_sc, 1, [128, 4, 128]), ALU.mult, ["sm_t", ("aux1", 1)], ["sm_mt"])
                    for h in range(4):
                        MM(ps_y[:, h * 64:(h + 1) * 64], sm["mt"][:, h, :], sm["x6"][:, h * 64:(h + 1) * 64], True, True,
                           ["sm_mt", "sm_x6"], [("aux0", 1)])
                    MM(ps_yo, CT[:, qs], prevb[:, q, :], True, True, [ctk, ("prevb", q)], [("aux1", 0)])
                    y3 = sm["y"][:].rearrange("p (h d) -> p h d", d=64)
                    TTn("dve", y3, ps_yo.rearrange("p (h d) -> p h d", d=64), bc(dtf["ea"][:, q, hs], 2, [128, 4, 64]), ALU.mult,
                        [("aux1", 0), "dt_ea"], ["sm_y"])
                    TTn("dve", sm["y"][:], sm["y"][:], ps_y, ALU.add, ["sm_y", ("aux0", 1)], ["sm_y"])
                    xsd = sm["t"][:].rearrange("p h l -> p (h l)")[:, 0:256]
                    TTn("pool", xsd.rearrange("p (h d) -> p h d", d=64), xs3, bc(vec["dsk"][:, hs], 2, [128, 4, 64]),
                        ALU.mult, ["sm_xs", "v_dsk", "sm_t"], ["sm_t"])
                    TTn("pool", sm["y"][:], sm["y"][:], xsd, ALU.add, ["sm_y", "sm_t"], ["sm_y"])
                    for i in range(2):
                        TR(ps_yT[:, i * 128:(i + 1) * 128], sm["y"][:, i * 128:(i + 1) * 128], identf[:], ["sm_y", "identf"], [("aux1", 1)])
                    for i in range(2):
                        CP("act", yT[i][:, qs], ps_yT[:, i * 128:(i + 1) * 128], [("aux1", 1)], ["f4_%d" % (6 + i)])
                stop(11)
                if not main:
                    continue
                gt, gtok = nextbig()
                for i in range(2):
                    TTn("pool", yT[i][:, 0:TT], yT[i][:, 0:TT], zs[i][:, 0:TT], ALU.mult, ["f4_%d" % (6 + i), "f4_%d" % (4 + i)], ["f4_%d" % (6 + i)])
                    ACT(sqg[:], yT[i][:, 0:TT], AF.Square, ["f4_%d" % (6 + i)], ["h2_2"])
                    for hf in range(NH):
                        MM(gt[:, hf * HW:(hf + 1) * HW], onesb[:], sqg[:, hf * HW:(hf + 1) * HW], i == 0, i == 1, ["onesb", "h2_2"], [gtok[hf * HW // 512]])
                rstd_from(gt[:, 0:TT], gtok, 256, f4[0][:, 0:TT], "f4_0")
                for i in range(2):
                    STT(vT[:, g * 2 + i, :], yT[i][:, 0:TT], vec["snorm"][:, g * 2 + i:g * 2 + i + 1], f4[0][:, 0:TT], ALU.mult, ALU.mult,
                        ["f4_%d" % (6 + i), "f4_0", "v_snorm"], [("vT", g * 2 + i)])
            stop(9)

        def phaseB(t0):
            stop(4)
            S.label = 'B1'
            vtoks = [("vT", k) for k in range(KM)]
            for cc in range(KD):
                bt, btok = nextbig()
                nkk = (KM + KD - 1) // KD
                wl = []
                for part in range(nkk):
                    k0 = part * KD
                    nk = min(KD, KM - k0)
                    wl.append((load_w_chunk(w_out[k0 * 128:(k0 + nk) * 128, cc * 128:(cc + 1) * 128], nk), k0, nk))
                for hf in range(NH):
                    for (wb, wtok), k0, nk in wl:
                        for k in range(nk):
                            MM(bt[:, hf * HW:(hf + 1) * HW], wb[:, k, :], vT[:, k0 + k, hf * HW:(hf + 1) * HW], k0 + k == 0, k0 + k == KM - 1,
                               [wtok] + vtoks, [btok[hf * HW // 512]])
                st, stok = f4[cc % 2], "f4_%d" % (cc % 2)
                sq, sqtok = h2[cc % 2], "h2_%d" % (cc % 2)
                CP("act", st[:, 0:TT], bt[:, 0:TT], btok, [stok])
                ACT(sq[:], bt[:, 0:TT], AF.Square, btok, [sqtok])
                for hf in range(NH):
                    MM(aux0[:, hf * HW:(hf + 1) * HW], onesb[:], sq[:, hf * HW:(hf + 1) * HW], cc == 0, cc == KD - 1, ["onesb", sqtok], [("aux0", hf * HW // 512)])
                DMA("sp", msT[cc], st[:, 0:TT], [stok], [("msT", cc)], stok + "s")
            r1 = f4[7]
            rstd_from(aux0[:, 0:TT], auxt("aux0"), D, r1[:, 0:TT], "f4_7")
            stop(5)
            S.label = 'B2'
            for cc in range(KD):
                ms, mstok = f4[cc % 2], "f4_%d" % (cc % 2)
                xb, xbtok = f4[2 + cc % 2], "f4_%d" % (2 + cc % 2)
                sq, sqtok = h2[cc % 2], "h2_%d" % (cc % 2)
                DMA("sp", ms[:, 0:TT], msT[cc], [("msT", cc)], [mstok], mstok)
                DMA("sp", xb[:, 0:TT].rearrange("p (j m) -> p j m", m=128),
                    xm[t0:t0 + TT, cc * 128:(cc + 1) * 128].rearrange("(j p) m -> p j m", p=128), [], [xbtok], xbtok)
                bt, btok = nextbig()
                for j in range(NQ):
                    TR(bt[:, j * 128:(j + 1) * 128], xb[:, j * 128:(j + 1) * 128], identf[:], [xbtok, "identf"], [btok[(j * 128) // 512]])
                STT(ms[:, 0:TT], ms[:, 0:TT], vec["g_post"][:, cc:cc + 1], r1[:, 0:TT], ALU.mult, ALU.mult, [mstok, "f4_7", "v_g_post"], [mstok])
                TTn("dve", ms[:, 0:TT], ms[:, 0:TT], bt[:, 0:TT], ALU.add, [mstok] + btok, [mstok])
                DMA("sp", x1s[cc], ms[:, 0:TT], [mstok], [("x1s", cc)], mstok + "s")
                ACT(sq[:], ms[:, 0:TT], AF.Square, [mstok], [sqtok])
                for hf in range(NH):
                    MM(aux1[:, hf * HW:(hf + 1) * HW], onesb[:], sq[:, hf * HW:(hf + 1) * HW], cc == 0, cc == KD - 1, ["onesb", sqtok], [("aux1", hf * HW // 512)])
            r2 = f4[6]
            rstd_from(aux1[:, 0:TT], auxt("aux1"), D, r2[:, 0:TT], "f4_6")
            stop(6)
            S.label = 'B3'
            for cc in range(KD):
                xb, xbtok = f4[cc % 2], "f4_%d" % (cc % 2)
                DMA("sp", xb[:, 0:TT], x1s[cc], [("x1s", cc)], [xbtok], xbtok)
                STT(hT[:, cc, :], xb[:, 0:TT], vec["g_mlp"][:, cc:cc + 1], r2[:, 0:TT], ALU.mult, ALU.mult, [xbtok, "f4_6", "v_g_mlp"],
                    [("hT", hf) for hf in range(NH)])
            stop(7)
            S.label = 'MLP'
            KB = c.FC // 128
            htoks = [("hT", hf) for hf in range(NH)]
            for fb in range(c.DFF // c.FC):
                for f in range(KB):
                    wb, wtok = load_w_chunk(w1[:, fb * c.FC + f * 128:fb * c.FC + (f + 1) * 128])
                    bt, btok = nextbig()
                    for hf in range(NH):
                        for k in range(KD):
                            MM(bt[:, hf * HW:(hf + 1) * HW], wb[:, k, :], hT[:, k, hf * HW:(hf + 1) * HW], k == 0, k == KD - 1, [wtok] + htoks, [btok[hf * HW // 512]])
                    rl, rltok = f4[2 + f % 2], "f4_%d" % (2 + f % 2)
                    ACT(rl[:, 0:TT], bt[:, 0:TT], AF.Relu, btok, [rltok])
                    TTn("pool", actT[:, f, :], rl[:, 0:TT], rl[:, 0:TT], ALU.mult, [rltok], [("act", f), "ovl", "ovlb"] + ALT)
                for cc in range(KD):
                    wb, wtok = load_w_chunk(w2[fb * c.FC:(fb + 1) * c.FC, cc * 128:(cc + 1) * 128], KB)
                    bt, btok = nextbig()
                    for hf in range(NH):
                        for k in range(KB):
                            MM(bt[:, hf * HW:(hf + 1) * HW], wb[:, k, :], actT[:, k, hf * HW:(hf + 1) * HW], k == 0, k == KB - 1,
                               [wtok, ("act", k)], [btok[hf * HW // 512]])
                    if fb == 0:
                        CP("act", accT[:, cc, :], bt[:, 0:TT], btok + vtoks, [("acc", cc)] + vtoks)
                    else:
                        TTn("dve", accT[:, cc, :], accT[:, cc, :], bt[:, 0:TT], ALU.add, btok + [("acc", cc)], [("acc", cc)])
            stop(8)
            S.label = 'B4'
            for cc in range(KD):
                sq, sqtok = h2[cc % 2], "h2_%d" % (cc % 2)
                ACT(sq[:], accT[:, cc, :], AF.Square, [("acc", cc)] + vtoks, [sqtok])
                for hf in range(NH):
                    MM(aux0[:, hf * HW:(hf + 1) * HW], onesb[:], sq[:, hf * HW:(hf + 1) * HW], cc == 0, cc == KD - 1, ["onesb", sqtok], [("aux0", hf * HW // 512)])
            r3 = f4[7]
            rstd_from(aux0[:, 0:TT], auxt("aux0"), D, r3[:, 0:TT], "f4_7")
            for cc in range(KD):
                xb, xbtok = f4[cc % 2], "f4_%d" % (cc % 2)
                ob, obtok = f4[2 + cc % 2], "f4_%d" % (2 + cc % 2)
                os_, ostok = f4[4 + cc % 2], "f4_%d" % (4 + cc % 2)
                DMA("sp", xb[:, 0:TT], x1s[cc], [("x1s", cc)], [xbtok], xbtok)
                STT(ob[:, 0:TT], accT[:, cc, :], vec["g_pmlp"][:, cc:cc + 1], r3[:, 0:TT], ALU.mult, ALU.mult, [("acc", cc), "f4_7", "v_g_pmlp"] + vtoks, [obtok])
                TTn("pool", ob[:, 0:TT], ob[:, 0:TT], xb[:, 0:TT], ALU.add, [obtok, xbtok], [obtok])
                bt, btok = nextbig()
                for j in range(NQ):
                    TR(bt[:, j * 128:(j + 1) * 128], ob[:, j * 128:(j + 1) * 128], identf[:], [obtok, "identf"], [btok[(j * 128) // 512]])
                CP("act", os_[:, 0:TT], bt[:, 0:TT], btok, [ostok])
                DMA("sp", out[t0:t0 + TT, cc * 128:(cc + 1) * 128].rearrange("(j p) m -> p j m", p=128),
                    os_[:, 0:TT].rearrange("p (j m) -> p j m", m=128), [ostok], [("out", t0, cc)], ostok + "o")
                S.out_tokens.append(("out", t0, cc))

        def _program():
            for ti in range(c.NPRE):
                phaseA(xp, ti * TT, False, last_prefix=(ti == c.NPRE - 1))
            alltok = [("prev", g) for g in range(G)]
            TS("dve", prev[:], prev[:], flag[:, 0:1], None, ALU.mult, None, alltok + ["flag"], alltok)
            TS("dve", hcar[:], hcar[:], flag[:, 0:1], None, ALU.mult, None, ["hcar", "flag"], ["hcar"])
            for ti in range(c.NMAIN):
                phaseA(xm, ti * TT, True)
                phaseB(ti * TT)

        S.out_tokens = []

        class _Stop(Exception):
            pass

        def stop(level):
            if getattr(cfg, "STOP", 0) == level:
                raise _Stop()

        try:
            _program()
        except _Stop:
            pass
        S.op("dve", lambda e: e.engine_nop(), S.out_tokens, [])
        S.emit(reorder=getattr(cfg, "REORDER", True))
        nc._sched = S
    return nc


def _vec_layouts(cfg, p):
    c = cfg
    pm = lambda v, n: np.ascontiguousarray(np.asarray(v, np.float32).reshape(n, 128).T)
    bcast = lambda v: np.ascontiguousarray(np.broadcast_to(np.asarray(v, np.float32)[None, :], (128, len(v))))
    cw = lambda w, n: np.ascontiguousarray(np.asarray(w, np.float32).reshape(4, n, 128).transpose(2, 1, 0).reshape(128, n * 4))
    return dict(
        g_pre=pm(p["pre_mix_norm"], c.KD), g_post=pm(p["post_mix_norm"], c.KD), g_mlp=pm(p["pre_mlp_norm"], c.KD),
        g_pmlp=pm(p["post_mlp_norm"], c.KD),
        cws=cw(p["ssd_conv_w"], c.KX), cbs=pm(p["ssd_conv_b"], c.KX), cwl=cw(p["lru_conv_w"], c.NBLK), cbl=pm(p["lru_conv_b"], c.NBLK),
        lba=pm(p["lru_b_a"], c.NBLK), lbx=pm(p["lru_b_x"], c.NBLK), lam=pm(p["lru_lambda"], c.NBLK), lnorm=pm(p["lru_norm"], c.NBLK),
        snorm=pm(p["ssd_norm"], c.KS), dtb=bcast(p["ssd_dt_bias"]), alog=bcast(p["ssd_a_log"]), dsk=bcast(p["ssd_d"]))


def make_in_maps(cfg, inputs, n_batch, n_half):
    c = cfg
    p = {k: np.asarray(v)[0] for k, v in inputs.items() if k != "x"}
    x = np.asarray(inputs["x"], np.float32)
    vl = _vec_layouts(c, p)
    half = c.NMAIN * c.TT
    shared = dict(w_in=np.ascontiguousarray(p["w_in"], np.float32), w_out=np.ascontiguousarray(p["w_out"], np.float32),
                  w1=np.ascontiguousarray(p["w_mlp_in"], np.float32), w2=np.ascontiguousarray(p["w_mlp_out"], np.float32),
                  lwa=np.ascontiguousarray(p["lru_w_a"], np.float32), lwx=np.ascontiguousarray(p["lru_w_x"], np.float32), **vl)
    maps = []
    for b in range(n_batch):
        for s in range(n_half):
            m = dict(shared)
            m["xm"] = np.ascontiguousarray(x[b, s * half:(s + 1) * half])
            pre = np.zeros((max(c.NPRE, 1) * c.TT, c.D), np.float32)
            if s > 0:
                pre[:] = x[b, (s - 1) * half:s * half][-pre.shape[0]:]
            m["xp"] = pre
            m["flag"] = np.full((128, 1), 1.0 if s > 0 else 0.0, np.float32)
            maps.append(m)
    return maps


def kernel(**inputs):
    cfg = Cfg()
    nc = build(cfg)
    maps = make_in_maps(cfg, inputs, 4, 2)
    res = run_bass_kernel_spmd(nc, maps, core_ids=list(range(8)))
    x = np.asarray(inputs["x"])
    outp = np.empty(x.shape, np.float32)
    half = cfg.NMAIN * cfg.TT
    i = 0
    for b in range(4):
        for s in range(2):
            outp[b, s * half:(s + 1) * half] = np.asarray(res.results[i]["out"], np.float32)
            i += 1
    return outp
```

```python
import contextlib
import numpy as np
import concourse.bass as bass
import concourse.mybir as mybir
from concourse.bass_utils import run_bass_kernel_spmd

F32 = mybir.dt.float32
BF16 = mybir.dt.bfloat16
AF = mybir.ActivationFunctionType
ALU = mybir.AluOpType
EPS = 1e-6


class Cfg:
    def __init__(self, D=2048, H=32, G=8, DL=2048, DFF=8192, TT=1024, NMAIN=2, NPRE=2, FC=1024):
        self.D, self.H, self.G, self.DL, self.DFF = D, H, G, DL, DFF
        self.TT, self.NMAIN, self.NPRE, self.FC = TT, NMAIN, NPRE, FC
        self.P, self.N = 64, 128
        self.DS = H * 64
        self.HG = H // G
        assert self.HG == 4
        self.KD = D // 128
        self.KS = self.DS // 128
        self.DXBC = self.DS + 2 * G * 128
        self.KX = self.DXBC // 128
        self.DIN = self.DS + self.DXBC + H + 2 * DL
        self.NBLK = DL // 128
        self.DMIX = self.DS + DL
        self.KM = self.DMIX // 128
        self.KF = DFF // 128
        self.NQ = TT // 128
        self.HW = min(512, TT)
        self.NH = TT // self.HW
        self.oxs = self.DS
        self.oB = 2 * self.DS
        self.oC = 2 * self.DS + G * 128
        self.odt = self.DS + self.DXBC
        self.ogate = self.odt + H
        self.oxl = self.ogate + DL


TUNE = dict(win=256, poolx=1.0, lat=0.0, dmaiss=900.0, actx=1.0, dvex=1.0, pex=1.0, bw=250.0)


class Sched:
    ENGS = ("pe", "act", "dve", "pool", "sp")

    def __init__(self, nc):
        self.nc = nc
        self.ops = []
        self.last_w = {}
        self.readers = {}
        self.dma_cnt = {}

    def op(self, eng, fn, reads=(), writes=(), dma_key=None, cost=300.0, nbytes=0):
        idx = len(self.ops)
        deps = set()
        for t in reads:
            w = self.last_w.get(t)
            if w is not None:
                deps.add(w)
        for t in writes:
            w = self.last_w.get(t)
            if w is not None:
                deps.add(w)
            deps.update(self.readers.get(t, ()))
        deps.discard(idx)
        o = dict(eng=eng, fn=fn, deps=deps, dma=dma_key is not None, key=dma_key, ms=False, cost=cost, nbytes=nbytes, label=getattr(self, 'label', ''))
        if dma_key is not None:
            n = self.dma_cnt.get(dma_key, 0) + 1
            self.dma_cnt[dma_key] = n
            o["dval"] = 16 * n
        self.ops.append(o)
        for t in reads:
            self.readers.setdefault(t, []).append(idx)
        for t in writes:
            self.last_w[t] = idx
            self.readers[t] = []
        return idx

    def schedule(self, window=None):
        ops = self.ops
        window = window or TUNE["win"]
        mult = dict(pool=TUNE["poolx"], act=TUNE["actx"], dve=TUNE["dvex"], pe=TUNE["pex"], sp=1.0)
        pend = {e: [] for e in self.ENGS}
        for i, o in enumerate(ops):
            pend[o["eng"]].append(i)
        head = {e: 0 for e in self.ENGS}
        tfree = {e: 0.0 for e in self.ENGS}
        done = [None] * len(ops)
        sched = [False] * len(ops)
        order = {e: [] for e in self.ENGS}
        pipe_free = 0.0
        tf_prev = {}
        remaining = len(ops)
        while remaining:
            progressed = False
            for e in sorted(self.ENGS, key=lambda x: tfree[x]):
                lst = pend[e]
                while head[e] < len(lst) and sched[lst[head[e]]]:
                    head[e] += 1
                if head[e] >= len(lst):
                    continue
                best, best_t = None, None
                seen = 0
                j = head[e]
                while j < len(lst) and seen < window:
                    i = lst[j]
                    j += 1
                    if sched[i]:
                        continue
                    seen += 1
                    t = tfree[e]
                    ok = True
                    for d in ops[i]["deps"]:
                        dt_ = done[d]
                        if dt_ is None:
                            ok = False
                            break
                        if dt_ > t:
                            t = dt_
                    if ok and (best is None or t < best_t - 1e-9):
                        best, best_t = i, t
                        if t <= tfree[e] + 1e-9:
                            break
                if best is None:
                    continue
                o = ops[best]
                if o["dma"]:
                    issue = TUNE["dmaiss"] if e == "pool" else 120.0
                    xs_ = max(best_t + issue, pipe_free)
                    pipe_free = xs_ + o["nbytes"] / TUNE["bw"]
                    done[best] = pipe_free + 2000.0
                    tfree[e] = best_t + issue
                else:
                    tfree[e] = best_t + o["cost"] * mult[e]
                    tf_prev[best] = tfree[e]
                    done[best] = tfree[e] + TUNE["lat"]
                sched[best] = True
                o['t0'] = best_t
                o['crit'] = max(list(o['deps']) + ([order[e][-1]] if order[e] else []), key=lambda d_: (done[d_] if (ops[d_]['eng'] != e or ops[d_]['dma']) else tf_prev.get(d_, done[d_])), default=None)
                o['t1'] = done[best]
                order[e].append(best)
                remaining -= 1
                progressed = True
                break
            assert progressed, "list scheduler stuck"
        self.makespan = max(d for d in done if d is not None)
        return order

    def emit(self, reorder=True):
        nc, ops = self.nc, self.ops
        if reorder:
            order = self.schedule()
        else:
            order = {e: [i for i, o in enumerate(ops) if o["eng"] == e] for e in self.ENGS}
        pos = {}
        for e in self.ENGS:
            for k, i in enumerate(order[e]):
                pos[i] = k
        for i, o in enumerate(ops):
            latest = {}
            eff = []
            for d in o["deps"]:
                p = ops[d]
                if p["dma"]:
                    eff.append(d)
                    continue
                if p["eng"] == "pe" and o["eng"] == "pe" and not o["dma"]:
                    assert pos[d] < pos[i]
                    continue
                if p["eng"] == o["eng"]:
                    assert pos[d] < pos[i]
                if p["eng"] not in latest or pos[d] > pos[latest[p["eng"]]]:
                    latest[p["eng"]] = d
            o["eff"] = eff + list(latest.values())
            for d in latest.values():
                ops[d]["ms"] = True
        KSEM = 8
        cnt = {e: 0 for e in self.ENGS}
        for e in self.ENGS:
            for i in order[e]:
                o = ops[i]
                if o["ms"]:
                    o["msi"] = cnt[e]
                    cnt[e] += 1
        self.ms_counts = cnt
        with contextlib.ExitStack() as es:
            esem = {e: [es.enter_context(nc.semaphore("S_%s%d" % (e, i))) for i in range(KSEM)] for e in self.ENGS if cnt[e] > 0}
            dsem = {}
            for i, (k, n) in enumerate(self.dma_cnt.items()):
                r = 1 if k in ("setup", "setup_p") else max(1, (n * 16 + 479) // 480)
                dsem[k] = [es.enter_context(nc.semaphore("D_%d_%d" % (i, j))) for j in range(r)]
            block = es.enter_context(nc.Block())

            def semval(p):
                if p["dma"]:
                    lst = dsem[p["key"]]
                    if p["key"] in ("setup", "setup_p"):
                        return lst[0], 16 * self.dma_cnt[p["key"]]
                    n = p["dval"] // 16 - 1
                    return lst[n % len(lst)], 16 * (n // len(lst) + 1)
                i = p["msi"]
                return esem[p["eng"]][i % KSEM], i // KSEM + 1

            def run(engname, eng):
                waited = {}
                for oi in order[engname]:
                    o = ops[oi]
                    need = {}
                    for d in o["eff"]:
                        p = ops[d]
                        s, v = semval(p)
                        if need.get(id(s), (None, 0))[1] < v:
                            need[id(s)] = (s, v)
                    for s, v in need.values():
                        if waited.get(id(s), 0) < v:
                            eng.wait_ge(s, v)
                            waited[id(s)] = v
                    ins = o["fn"](eng)
                    if o["dma"]:
                        ins.then_inc(semval(o)[0], 16)
                    elif o["ms"]:
                        ins.then_inc(semval(o)[0], 1)

            block.tensor(lambda e: run("pe", e))
            block.scalar(lambda e: run("act", e))
            block.vector(lambda e: run("dve", e))
            block.gpsimd(lambda e: run("pool", e))
            block.sync(lambda e: run("sp", e))


def build(cfg):
    c = cfg
    D, H, G, TT, KD, KS, KX, KM, NQ, NH, HW, NBLK = c.D, c.H, c.G, c.TT, c.KD, c.KS, c.KX, c.KM, c.NQ, c.NH, c.HW, c.NBLK
    nc = bass.Bass("TRN2", target_bir_lowering=False)
    din = lambda name, shape: nc.dram_tensor(name, list(shape), F32, kind="ExternalInput").ap()
    xm = din("xm", [c.NMAIN * TT, D])
    xp = din("xp", [max(c.NPRE, 1) * TT, D])
    flag_d = din("flag", [128, 1])
    w_in = din("w_in", [D, c.DIN])
    w_out = din("w_out", [c.DMIX, D])
    w1 = din("w1", [D, c.DFF])
    w2 = din("w2", [c.DFF, D])
    lw_d = din("lw", [NBLK, 128, 256])
    vec_d = {}
    vec_shapes = dict(g_pre=[128, KD], g_post=[128, KD], g_mlp=[128, KD], g_pmlp=[128, KD],
                      cws=[128, KX * 4], cbs=[128, KX], cwl=[128, NBLK * 4], cbl=[128, NBLK],
                      lba=[128, NBLK], lbx=[128, NBLK], lam=[128, NBLK], lnorm=[128, NBLK],
                      snorm=[128, KS], dtb=[128, H], alog=[128, H], dsk=[128, H])
    for k, shp in vec_shapes.items():
        vec_d[k] = din(k, shp)
    out = nc.dram_tensor("out", [c.NMAIN * TT, D], F32, kind="ExternalOutput").ap()
    msT = nc.dram_tensor("msT", [KD, 128, TT], F32).ap()
    x1s = nc.dram_tensor("x1s", [KD, 128, TT], F32).ap()

    es = contextlib.ExitStack()
    with es:
        sb = lambda name, shape, dt: es.enter_context(nc.sbuf_tensor(name, list(shape), dt))
        S = Sched(nc)
        if getattr(cfg, "PAD", 0):
            sb("pad", [128, cfg.PAD // 4], F32)
        def ACT(out_, in_, func, r, w, bias=None, scale=None, accum=None):
            kw = {}
            if bias is not None:
                kw["bias"] = bias
            if scale is not None:
                kw["scale"] = scale
            if accum is not None:
                kw["accum_out"] = accum
            S.op("act", lambda e: e.activation(out=out_, in_=in_, func=func, **kw), r, w, cost=(224.0 + in_.free_size()) / 1.2 + (90.0 if accum is not None else 0.0))

        def TTn(eng, out_, in0, in1, op, r, w):
            S.op(eng, lambda e: e.tensor_tensor(out=out_, in0=in0, in1=in1, op=op), r, w, cost=(130.0 + out_.free_size()) / 0.96)

        def TS(eng, out_, in0, s1, s2, op0, op1, r, w):
            if s2 is None:
                S.op(eng, lambda e: e.tensor_scalar(out=out_, in0=in0, scalar1=s1, scalar2=None, op0=op0), r, w, cost=(130.0 + out_.free_size()) / 0.96)
            else:
                S.op(eng, lambda e: e.tensor_scalar(out=out_, in0=in0, scalar1=s1, scalar2=s2, op0=op0, op1=op1), r, w, cost=(130.0 + out_.free_size()) / 0.96)

        def STT(out_, in0, scalar, in1, op0, op1, r, w):
            S.op("dve", lambda e: e.scalar_tensor_tensor(out=out_, in0=in0, scalar=scalar, in1=in1, op0=op0, op1=op1), r, w, cost=(130.0 + out_.free_size()) / 0.96)

        def CP(eng, out_, in_, r, w):
            if eng == "act":
                S.op("act", lambda e: e.activation(out=out_, in_=in_, func=AF.Copy), r, w, cost=(224.0 + in_.free_size()) / 1.2)
            else:
                S.op(eng, lambda e: e.tensor_copy(out=out_, in_=in_), r, w, cost=(130.0 + out_.free_size()) / 0.96)

        def MM(out_, lhsT, rhs, start, stop, r, w):
            S.op("pe", lambda e: e.matmul(out_, lhsT=lhsT, rhs=rhs, start=start, stop=stop), r, w, cost=max(64.0, rhs.free_size()) / 2.4 + 12.0)

        def TR(out_, in_, ident, r, w):
            S.op("pe", lambda e: e.transpose(out_, in_, ident), r, w, cost=110.0)

        def DMA(q, out_, in_, r, w, key):
            S.op(q, lambda e: e.dma_start(out=out_, in_=in_), r, w, dma_key=key, nbytes=max(out_.nbytes(), in_.nbytes()))

        def bc(ap, axis, shape):
            return ap.unsqueeze(axis).to_broadcast(list(shape))

        identb = sb("identb", [128, 128], BF16)
        identf = sb("identf", [128, 128], F32)
        onesb = sb("onesb", [128, 128], BF16)
        trib = sb("trib", [128, 128], BF16)
        negm = sb("negm", [128, 128], F32)
        vec = {k: sb("v_" + k, shp, F32) for k, shp in vec_shapes.items()}
        flag = sb("flag_sb", [128, 1], F32)
        clru = sb("clru", [128, NBLK], F32)
        a_bc = sb("a_bc", [128, H], F32)
        wdt = sb("wdt", [128, KD, H], BF16)
        lwb = sb("lwb", [128, 2, 256], BF16)
        prev = sb("prev", [128, G * 256], F32)
        prevb = sb("prevb", [128, NQ, 256], BF16)
        hcar = sb("hcar", [128, NBLK], F32)
        car_s = sb("car_s", [128, KX, 3], F32)
        car_l = sb("car_l", [128, NBLK, 3], F32)
        QH = NQ * H
        dtf = {k: sb("dt_" + k, [128, NQ, H], F32) for k in ("dt", "adt", "acs", "ea", "cdb", "dtde", "t0", "t1", "t2")}
        adt_hi = sb("adt_hi", [128, NQ, H], BF16)
        adt_lo = sb("adt_lo", [128, NQ, H], BF16)
        ssq_x = sb("ssq_x", [128, NQ], F32)
        RAK = max(KD, (KM + 1) // 2)
        RA = sb("RA", [128, RAK * TT], F32)
        vT = RA[:].bitcast(BF16).rearrange("p (k t) -> p k t", t=TT)
        accT = RA[:].rearrange("p (k t) -> p k t", t=TT)
        hT = sb("hT", [128, KD, TT], BF16)
        OVW = max(c.FC // 128 * TT // 2, D + D // 2)
        ovl = sb("ovl", [128, OVW], F32)
        xt = ovl[:, 0:D]
        xn = ovl[:, D:D + D // 2].bitcast(BF16)
        actT = ovl[:, 0:c.FC // 128 * TT // 2].bitcast(BF16).rearrange("p (k t) -> p k t", t=TT)
        ALT = []
        if getattr(cfg, "ALTSSD", False) and OVW >= 3 * (TT + 4):
            ALT = ["alt_xs0", "alt_xs1", "alt_b", "alt_c"]
            alt_xs = (ovl[:, 0:TT + 4], ovl[:, TT + 4:2 * (TT + 4)])
            alt_b = ovl[:, 2 * (TT + 4):2 * (TT + 4) + TT // 2].bitcast(BF16)
            alt_c = ovl[:, 2 * (TT + 4) + TT // 2:2 * (TT + 4) + TT].bitcast(BF16)
        NW = 3
        wbuf = [sb("wbuf%d" % i, [128, max(KD, c.FC // 128), 128], BF16) for i in range(NW)]
        NF = 8
        f4 = [sb("f4_%d" % i, [128, TT + 4], F32) for i in range(NF)]
        h2 = [sb("h2_%d" % i, [128, TT], BF16) for i in range(3)]
        NSM = 2
        smsets = [{k: sb("sm%d_%s" % (n_, k), shp, dt) for k, (shp, dt) in dict(
            xs=([128, 256], F32), x6=([128, 256], BF16), x6e=([128, 256], BF16), bt=([128, 128], BF16),
            xhi=([128, 4, 128], BF16), xlo=([128, 4, 128], BF16), t=([128, 4, 128], F32),
            mt=([128, 4, 128], BF16), y=([128, 256], F32)).items()} for n_ in range(NSM)]
        ps = lambda name: es.enter_context(nc.psum_tensor(name, [128, 1024], F32))
        big = [ps("big0"), ps("big1")]
        aux0, aux1 = ps("aux0"), ps("aux1")
        bigi = [0]

        def nextbig():
            i = bigi[0] % 2
            bigi[0] += 1
            return big[i], [("big%d" % i, b) for b in range((TT + 511) // 512)]

        wi = [0]

        def nextw():
            i = wi[0] % NW
            wi[0] += 1
            return wbuf[i], "wbuf%d" % i

        for k in vec_shapes:
            DMA("sp", vec[k][:], vec_d[k][:], [], ["v_" + k], "setup")
        DMA("sp", flag[:], flag_d[:], [], ["flag"], "setup")
        DMA("pool", wdt[:], w_in[:, c.odt:c.odt + H].rearrange("(k p) h -> p k h", p=128), [], ["wdt"], "setup_p")

        def mask_const(t, tok, fill, pat, cm, op):
            S.op("pool", lambda e: e.affine_select(out=t[:], in_=t[:], pattern=pat, compare_op=op, fill=fill,
                                                   base=0, channel_multiplier=cm), [tok], [tok])

        S.op("pool", lambda e: e.memset(identb[:], 1.0), [], ["identb"])
        mask_const(identb, "identb", 0.0, [[-1, 128]], 1, ALU.is_equal)
        S.op("pool", lambda e: e.memset(identf[:], 1.0), [], ["identf"])
        mask_const(identf, "identf", 0.0, [[-1, 128]], 1, ALU.is_equal)
        S.op("pool", lambda e: e.memset(onesb[:], 1.0), [], ["onesb"])
        S.op("pool", lambda e: e.memset(trib[:], 1.0), [], ["trib"])
        mask_const(trib, "trib", 0.0, [[1, 128]], -1, ALU.is_ge)
        S.op("pool", lambda e: e.memset(negm[:], 0.0), [], ["negm"])
        mask_const(negm, "negm", -30000.0, [[1, 128]], -1, ALU.is_ge)
        S.op("pool", lambda e: e.memset(prev[:], 0.0), [], ["prev"])
        S.op("pool", lambda e: e.memset(hcar[:], 0.0), [], ["hcar"])
        S.op("pool", lambda e: e.memset(car_s[:], 0.0), [], ["car_s"])
        S.op("pool", lambda e: e.memset(car_l[:], 0.0), [], ["car_l"])

        def log1p_small(out_, e_, tmpw, tmpq, tok_o, tok_e, tok_w, tok_q):
            TS("dve", tmpw, e_, 2.0, None, ALU.add, None, [tok_e], [tok_w])
            S.op("dve", lambda e: e.reciprocal(out=tmpw, in_=tmpw), [tok_w], [tok_w])
            TTn("dve", tmpw, tmpw, e_, ALU.mult, [tok_w, tok_e], [tok_w])
            TTn("dve", out_, tmpw, tmpw, ALU.mult, [tok_w], [tok_o])
            TS("dve", tmpq, out_, 1.0 / 11.0, None, ALU.mult, None, [tok_o], [tok_q])
            for cst in (1.0 / 9.0, 1.0 / 7.0, 1.0 / 5.0, 1.0 / 3.0):
                STT(tmpq, tmpq, cst, out_, ALU.add, ALU.mult, [tok_q, tok_o], [tok_q])
            TS("dve", tmpq, tmpq, 1.0, None, ALU.add, None, [tok_q], [tok_q])
            TTn("dve", tmpq, tmpq, tmpw, ALU.mult, [tok_q, tok_w], [tok_q])
            TS("dve", out_, tmpq, 2.0, None, ALU.mult, None, [tok_q], [tok_o])

        def softplus(out_, x_, ta, tb, tcc, tok_o, tok_x, tok_a, tok_b, tok_c):
            TS("dve", ta, x_, -1.0, None, ALU.mult, None, [tok_x], [tok_a])
            TTn("dve", ta, ta, x_, ALU.max, [tok_a, tok_x], [tok_a])
            ACT(ta, ta, AF.Exp, [tok_a], [tok_a], scale=-1.0)
            log1p_small(tb, ta, tcc, out_, tok_b, tok_a, tok_c, tok_o)
            TS("dve", ta, x_, 0.0, None, ALU.max, None, [tok_x], [tok_a])
            TTn("dve", out_, ta, tb, ALU.add, [tok_a, tok_b], [tok_o])

        t0s = dtf["t0"][:, 0, 0:NBLK] if NBLK <= H else None
        assert NBLK <= H
        t1s, t2s, t3s, t4s = (dtf[k][:, 0, 0:NBLK] for k in ("t1", "t2", "dt", "adt"))
        TS("dve", t0s, vec["lam"][:], -1.0, None, ALU.mult, None, ["v_lam"], ["dt_t0"])
        softplus(t1s, t0s, t2s, t3s, t4s, "dt_t1", "dt_t0", "dt_t2", "dt_dt", "dt_adt")
        TS("dve", clru[:], t1s, -8.0, None, ALU.mult, None, ["dt_t1"], ["clru"])
        ACT(a_bc[:], vec["alog"][:], AF.Exp, ["v_alog"], ["a_bc"])
        TS("dve", a_bc[:], a_bc[:], -1.0, None, ALU.mult, None, ["a_bc"], ["a_bc"])

        def load_w_chunk(src_ap, nk=None):
            wb, tok = nextw()
            nk = KD if nk is None else nk
            DMA("pool", wb[:, 0:nk, :], src_ap.rearrange("(k p) m -> p k m", p=128), [], [tok], tok)
            return wb, tok

        def proj_chunk(col0):
            wb, wtok = load_w_chunk(w_in[:, col0:col0 + 128])
            bt, btok = nextbig()
            for hf in range(NH):
                for k in range(KD):
                    MM(bt[:, hf * HW:(hf + 1) * HW], wb[:, k, :], hT[:, k, hf * HW:(hf + 1) * HW], k == 0, k == KD - 1,
                       [wtok, ("hT", hf)], [btok[hf * HW // 512]])
            return bt, btok

        def conv_chunk(bt, btok, car, cidx, cwn, cbn, xpad, xptok, acc, acctok, car_tok):
            cw, cb = vec[cwn], vec[cbn]
            CP("pool", xpad[:, 0:3], car[:, cidx, :], [car_tok], [xptok])
            CP("act", xpad[:, 3:3 + TT], bt[:, 0:TT], btok + [xptok], [xptok])
            CP("pool", car[:, cidx, :], xpad[:, TT:TT + 3], [xptok], [car_tok])
            ACT(acc[:, 0:TT], xpad[:, 0:TT], AF.Identity, [xptok, "v_" + cwn, "v_" + cbn], [acctok],
                bias=cb[:, cidx:cidx + 1], scale=cw[:, 4 * cidx:4 * cidx + 1])
            for k in (1, 2, 3):
                STT(acc[:, 0:TT], xpad[:, k:k + TT], cw[:, 4 * cidx + k:4 * cidx + k + 1], acc[:, 0:TT], ALU.mult, ALU.add,
                    [xptok, acctok, "v_" + cwn], [acctok])

        def rstd_from(ps_ap, pstok, n, dst, dtok):
            ACT(dst, ps_ap, AF.Sqrt, pstok + ["eps"], [dtok], bias=eps_t[:, 0:1], scale=1.0 / n)
            S.op("dve", lambda e: e.reciprocal(out=dst, in_=dst), [dtok], [dtok])

        auxt = lambda nm: [(nm, b) for b in range((TT + 511) // 512)]
        eps_t = sb("eps_t", [128, 1], F32)
        S.op("pool", lambda e: e.memset(eps_t[:], EPS), [], ["eps"])

        acttoks = [("act", f) for f in range(c.FC // 128)]

        def phaseA(xsrc, t0, main, last_prefix=False):
            S.label = 'A0'
            for j in range(NQ):
                DMA("sp", xt, xsrc[t0 + j * 128:t0 + (j + 1) * 128, :], [], ["ovl"] + acttoks + ALT, "ovl")
                ACT(xn, xt, AF.Square, ["ovl"], ["ovlb", "ssq%d" % j] + ALT, accum=ssq_x[:, j:j + 1])
                ACT(ssq_x[:, j:j + 1], ssq_x[:, j:j + 1], AF.Sqrt, ["ssq%d" % j, "eps"], ["ssq%d" % j], bias=eps_t[:, 0:1], scale=1.0 / D)
                S.op("dve", lambda e, j=j: e.reciprocal(out=ssq_x[:, j:j + 1], in_=ssq_x[:, j:j + 1]), ["ssq%d" % j], ["ssq%d" % j])
                TS("dve", xn, xt, ssq_x[:, j:j + 1], None, ALU.mult, None, ["ovl", "ovlb", "ssq%d" % j], ["ovlb"])
                pt, ptok = nextbig()
                ptb = pt[:].bitcast(BF16)
                for k in range(KD):
                    TR(ptb[:, k * 128:(k + 1) * 128], xn[:, k * 128:(k + 1) * 128], identb[:], ["ovlb", "identb"], [ptok[(k * 64) // 512]])
                TTn("dve", hT[:, :, j * 128:(j + 1) * 128], ptb[:, 0:KD * 128].rearrange("p (k m) -> p k m", m=128),
                    bc(vec["g_pre"][:], 2, [128, KD, 128]), ALU.mult, ptok + ["v_g_pre"], [("hT", (j * 128) // HW)])
            stop(1)
            S.label = 'A1'
            a0v = aux0[:, 0:QH].rearrange("p (q h) -> p q h", h=H)
            a1v = aux0[:, 512:512 + QH].rearrange("p (q h) -> p q h", h=H)
            for q in range(NQ):
                for k in range(KD):
                    MM(a0v[:, q, :], hT[:, k, q * 128:(q + 1) * 128], wdt[:, k, :], k == 0, k == KD - 1,
                       [("hT", (q * 128) // HW), "wdt"], [("aux0", 0)])
            TTn("dve", dtf["t0"][:], a0v, bc(vec["dtb"][:], 1, [128, NQ, H]), ALU.add, [("aux0", 0), "v_dtb"], ["dt_t0"])
            softplus(dtf["dt"][:], dtf["t0"][:], dtf["t1"][:], dtf["t2"][:], dtf["adt"][:], "dt_dt", "dt_t0", "dt_t1", "dt_t2", "dt_adt")
            TTn("dve", dtf["adt"][:], dtf["dt"][:], bc(a_bc[:], 1, [128, NQ, H]), ALU.mult, ["dt_dt", "a_bc"], ["dt_adt"])
            CP("dve", adt_hi[:], dtf["adt"][:], ["dt_adt"], ["adt_hi"])
            TTn("dve", adt_lo[:], dtf["adt"][:], adt_hi[:], ALU.subtract, ["dt_adt", "adt_hi"], ["adt_lo"])
            for q in range(NQ):
                MM(a0v[:, q, :], trib[:], adt_hi[:, q, :], True, False, ["trib", "adt_hi"], [("aux0", 0)])
                MM(a0v[:, q, :], trib[:], adt_lo[:, q, :], False, True, ["trib", "adt_lo"], [("aux0", 0)])
                MM(a1v[:, q, :], onesb[:], adt_hi[:, q, :], True, False, ["onesb", "adt_hi"], [("aux0", 1)])
                MM(a1v[:, q, :], onesb[:], adt_lo[:, q, :], False, True, ["onesb", "adt_lo"], [("aux0", 1)])
            CP("act", dtf["acs"][:], a0v, [("aux0", 0)], ["dt_acs"])
            if main and not getattr(cfg, "NOEA", 0):
                ACT(dtf["ea"][:], dtf["acs"][:], AF.Exp, ["dt_acs"], ["dt_ea"])
            ACT(dtf["cdb"][:], a1v, AF.Exp, [("aux0", 1)], ["dt_cdb"])
            TTn("dve", dtf["t0"][:], a1v, dtf["acs"][:], ALU.subtract, [("aux0", 1), "dt_acs"], ["dt_t0"])
            ACT(dtf["t0"][:], dtf["t0"][:], AF.Exp, ["dt_t0"], ["dt_t0"])
            TTn("dve", dtf["dtde"][:], dtf["t0"][:], dtf["dt"][:], ALU.mult, ["dt_t0", "dt_dt"], ["dt_dtde"])

            stop(2)
            S.label = 'LRU' + ('m' if main else 'p')
            xpad, xl, rr, ii, a2, hl, ge = (f4[i] for i in range(7))
            xlb, sqb = h2[0], h2[1]
            for blk in range(NBLK):
                bt, btok = proj_chunk(c.oxl + blk * 128)
                conv_chunk(bt, btok, car_l, blk, "cwl", "cbl", xpad, "f4_0", xl, "f4_1", "car_l")
                CP("act", xlb[:], xl[:, 0:TT], ["f4_1"], ["h2_0"])
                lwtok = "lwb%d" % (blk % 2)
                DMA("pool", lwb[:, blk % 2, :], lw_d[blk], [], [lwtok], lwtok)
                for (wofs, bn, dst, dtok) in ((0, "lba", rr, "f4_2"), (128, "lbx", ii, "f4_3")):
                    gt, gtok = nextbig()
                    for hf in range(NH):
                        MM(gt[:, hf * HW:(hf + 1) * HW], lwb[:, blk % 2, wofs:wofs + 128], xlb[:, hf * HW:(hf + 1) * HW], True, True, [lwtok, "h2_0"], [gtok[hf * HW // 512]])
                    ACT(dst[:, 0:TT], gt[:, 0:TT], AF.Sigmoid, gtok + ["v_" + bn], [dtok], bias=vec[bn][:, blk:blk + 1])
                ACT(rr[:, 0:TT], rr[:, 0:TT], AF.Exp, ["f4_2", "clru"], ["f4_2"], scale=clru[:, blk:blk + 1])
                ACT(a2[:, 0:TT], rr[:, 0:TT], AF.Square, ["f4_2"], ["f4_4"])
                ACT(a2[:, 0:TT], a2[:, 0:TT], AF.Sqrt, ["f4_4"], ["f4_4"], bias=1.0, scale=-1.0)
                TTn("dve", ii[:, 0:TT], ii[:, 0:TT], xl[:, 0:TT], ALU.mult, ["f4_3", "f4_1"], ["f4_3"])
                TTn("dve", ii[:, 0:TT], ii[:, 0:TT], a2[:, 0:TT], ALU.mult, ["f4_3", "f4_4"], ["f4_3"])
                S.op("dve", lambda e, blk=blk: e.tensor_tensor_scan(out=hl[:, 0:TT], data0=rr[:, 0:TT], data1=ii[:, 0:TT],
                                                                     initial=hcar[:, blk:blk + 1], op0=ALU.mult, op1=ALU.add),
                     ["f4_2", "f4_3", "hcar"], ["f4_5"], cost=(130.0 + 2 * TT) / 0.96)
                CP("pool", hcar[:, blk:blk + 1], hl[:, TT - 1:TT], ["f4_5"], ["hcar"])
                if main:
                    gt, gtok = proj_chunk(c.ogate + blk * 128)
                    gx = f4[7]
                    CP("act", gx[:, 0:TT], gt[:, 0:TT], gtok, ["f4_7"])
                    TTn("dve", ge[:, 0:TT], gx[:, 0:TT], gx[:, 0:TT], ALU.mult, ["f4_7"], ["f4_6"])
                    TS("dve", ge[:, 0:TT], ge[:, 0:TT], 0.044715, 1.0, ALU.mult, ALU.add, ["f4_6"], ["f4_6"])
                    TTn("dve", ge[:, 0:TT], ge[:, 0:TT], gx[:, 0:TT], ALU.mult, ["f4_6", "f4_7"], ["f4_6"])
                    ACT(ge[:, 0:TT], ge[:, 0:TT], AF.Sigmoid, ["f4_6"], ["f4_6"], scale=1.5957691216057308)
                    TTn("dve", ge[:, 0:TT], ge[:, 0:TT], gx[:, 0:TT], ALU.mult, ["f4_6", "f4_7"], ["f4_6"])
                    TTn("dve", ge[:, 0:TT], ge[:, 0:TT], hl[:, 0:TT], ALU.mult, ["f4_6", "f4_5"], ["f4_6"])
                    ACT(sqb[:], ge[:, 0:TT], AF.Square, ["f4_6"], ["h2_1"])
                    CP("act", vT[:, KS + blk, :], ge[:, 0:TT], ["f4_6"], [("vT", KS + blk)])
                    for hf in range(NH):
                        MM(aux1[:, hf * HW:(hf + 1) * HW], onesb[:], sqb[:, hf * HW:(hf + 1) * HW], blk == 0, blk == NBLK - 1,
                           ["onesb", "h2_1"], [("aux1", hf * HW // 512)])
            if main:
                rl = f4[7]
                rstd_from(aux1[:, 0:TT], auxt("aux1"), c.DL, rl[:, 0:TT], "f4_7")
                for blk in range(NBLK):
                    STT(vT[:, KS + blk, :], vT[:, KS + blk, :], vec["lnorm"][:, blk:blk + 1], rl[:, 0:TT], ALU.mult, ALU.mult,
                        [("vT", KS + blk), "f4_7", "v_lnorm"], [("vT", KS + blk)])

            stop(3)
            S.label = 'SSD' + ('m' if main else 'p')
            zs = (f4[4], f4[5])
            yT = (f4[6], f4[7])
            sqg = h2[2]
            ps_acsb = aux0[:, 0:512]
            ps_xs = aux0[:, 512:768]
            ps_y = aux0[:, 768:1024]
            ps_st = aux1[:, 0:256]
            ps_yo = aux1[:, 256:512]
            ps_sc = aux1[:, 512:640]
            ps_yT = aux1[:, 640:896]
            ps_bt = aux1[:, 896:960].bitcast(BF16)
            for g in range(G):
                hs = slice(g * 4, g * 4 + 4)
                pg = prev[:, g * 256:(g + 1) * 256]
                if ALT and g % 2 == 1:
                    xsT, BT, CT = alt_xs, alt_b, alt_c
                    xstok, btk, ctk = ("alt_xs0", "alt_xs1"), "alt_b", "alt_c"
                    altw = ["ovl", "ovlb"] + acttoks
                else:
                    xsT, BT, CT = (f4[2], f4[3]), h2[0], h2[1]
                    xstok, btk, ctk = ("f4_2", "f4_3"), "h2_0", "h2_1"
                    altw = []
                for i in range(2):
                    cidx = g * 2 + i
                    bt, btok = proj_chunk(c.oxs + cidx * 128)
                    conv_chunk(bt, btok, car_s, cidx, "cws", "cbs", f4[0], "f4_0", f4[1], "f4_1", "car_s")
                    ACT(xsT[i][:, 0:TT], f4[1][:, 0:TT], AF.Silu, ["f4_1"], [xstok[i]] + altw)
                for (cidx, dst, dtok) in ((KS + g, BT, btk), (KS + G + g, CT, ctk)):
                    if cidx >= KS + G and not main and c.NMAIN > 0 and not last_prefix:
                        continue
                    bt, btok = proj_chunk(c.oxs + cidx * 128)
                    conv_chunk(bt, btok, car_s, cidx, "cws", "cbs", f4[0], "f4_0", f4[1], "f4_1", "car_s")
                    ACT(dst[:, 0:TT], f4[1][:, 0:TT], AF.Silu, ["f4_1"], [dtok] + altw)
                if main:
                    for i in range(2):
                        bt, btok = proj_chunk(g * 256 + i * 128)
                        ACT(zs[i][:, 0:TT], bt[:, 0:TT], AF.Silu, btok, ["f4_%d" % (4 + i)])
                stop(10)
                for q in range(NQ):
                    qs = slice(q * 128, (q + 1) * 128)
                    sm = smsets[q % NSM]
                    smp = "sm%d_" % (q % NSM)
                    for i in range(2):
                        TR(ps_xs[:, i * 128:(i + 1) * 128], xsT[i][:, qs], identf[:], [xstok[i], "identf"], [("aux0", 1)])
                    CP("act", sm["xs"][:], ps_xs, [("aux0", 1)], [smp + "xs"])
                    xs3 = sm["xs"][:].rearrange("p (h d) -> p h d", d=64)
                    TTn("dve", sm["x6e"][:].rearrange("p (h d) -> p h d", d=64), xs3, bc(dtf["dtde"][:, q, hs], 2, [128, 4, 64]),
                        ALU.mult, [smp + "xs", "dt_dtde"], [smp + "x6e"])
                    TR(ps_bt, BT[:, qs], identb[:], [btk, "identb"], [("aux1", 1)])
                    CP("act", sm["bt"][:], ps_bt, [("aux1", 1)], [smp + "bt"])
                    MM(ps_st, sm["bt"][:], sm["x6e"][:], True, True, [smp + "bt", smp + "x6e"], [("aux1", 0)])
                    if main:
                        CP("act", prevb[:, q, :], pg, [("prev", g)], [("prevb", q)])
                    TTn("dve", pg.rearrange("p (h d) -> p h d", d=64), pg.rearrange("p (h d) -> p h d", d=64),
                        bc(dtf["cdb"][:, q, hs], 2, [128, 4, 64]), ALU.mult, [("prev", g), "dt_cdb"], [("prev", g)])
                    TTn("dve", pg, pg, ps_st, ALU.add, [("prev", g), ("aux1", 0)], [("prev", g)])
                    if not main:
                        continue
                    TTn("dve", sm["x6"][:].rearrange("p (h d) -> p h d", d=64), xs3, bc(dtf["dt"][:, q, hs], 2, [128, 4, 64]),
                        ALU.mult, [smp + "xs", "dt_dt"], [smp + "x6"])
                    TTn("dve", sm["xhi"][:], bc(adt_hi[:, q, hs], 2, [128, 4, 128]), bc(trib[:], 1, [128, 4, 128]), ALU.mult,
                        ["adt_hi", "trib"], [smp + "xhi"])
                    TTn("dve", sm["xlo"][:], bc(adt_lo[:, q, hs], 2, [128, 4, 128]), bc(trib[:], 1, [128, 4, 128]), ALU.mult,
                        ["adt_lo", "trib"], [smp + "xlo"])
                    MM(ps_acsb, onesb[:], sm["xhi"][:].rearrange("p h l -> p (h l)"), True, False, ["onesb", smp + "xhi"], [("aux0", 0)])
                    MM(ps_acsb, onesb[:], sm["xlo"][:].rearrange("p h l -> p (h l)"), False, True, ["onesb", smp + "xlo"], [("aux0", 0)])
                    MM(ps_sc, BT[:, qs], CT[:, qs], True, True, [btk, ctk], [("aux1", 1)])
                    TTn("dve", sm["t"][:], ps_acsb.rearrange("p (h l) -> p h l", l=128), bc(dtf["acs"][:, q, hs], 2, [128, 4, 128]),
                        ALU.subtract, [("aux0", 0), "dt_acs"], [smp + "t"])
                    TTn("dve", sm["t"][:], sm["t"][:], bc(negm[:], 1, [128, 4, 128]), ALU.add, [smp + "t", "negm"], [smp + "t"])
                    ACT(sm["t"][:], sm["t"][:], AF.Exp, [smp + "t"], [smp + "t"])
                    TTn("dve", sm["mt"][:], sm["t"][:], bc(ps_sc, 1, [128, 4, 128]), ALU.mult, [smp + "t", ("aux1", 1)], [smp + "mt"])
                    for h in range(4):
                        MM(ps_y[:, h * 64:(h + 1) * 64], sm["mt"][:, h, :], sm["x6"][:, h * 64:(h + 1) * 64], True, True,
                           [smp + "mt", smp + "x6"], [("aux0", 1)])
                    MM(ps_yo, CT[:, qs], prevb[:, q, :], True, True, [ctk, ("prevb", q)], [("aux1", 0)])
                    y3 = sm["y"][:].rearrange("p (h d) -> p h d", d=64)
                    TTn("dve", y3, ps_yo.rearrange("p (h d) -> p h d", d=64), bc(dtf["ea"][:, q, hs], 2, [128, 4, 64]), ALU.mult,
                        [("aux1", 0), "dt_ea"], [smp + "y"])
                    TTn("dve", sm["y"][:], sm["y"][:], ps_y, ALU.add, [smp + "y", ("aux0", 1)], [smp + "y"])
                    xsd = sm["t"][:].rearrange("p h l -> p (h l)")[:, 0:256]
                    TTn("dve", xsd.rearrange("p (h d) -> p h d", d=64), xs3, bc(vec["dsk"][:, hs], 2, [128, 4, 64]),
                        ALU.mult, [smp + "xs", "v_dsk", smp + "t"], [smp + "t"])
                    TTn("dve", sm["y"][:], sm["y"][:], xsd, ALU.add, [smp + "y", smp + "t"], [smp + "y"])
                    for i in range(2):
                        TR(ps_yT[:, i * 128:(i + 1) * 128], sm["y"][:, i * 128:(i + 1) * 128], identf[:], [smp + "y", "identf"], [("aux1", 1)])
                    for i in range(2):
                        CP("act", yT[i][:, qs], ps_yT[:, i * 128:(i + 1) * 128], [("aux1", 1)], ["f4_%d" % (6 + i)])
                stop(11)
                if not main:
                    continue
                gt, gtok = nextbig()
                for i in range(2):
                    TTn("dve", yT[i][:, 0:TT], yT[i][:, 0:TT], zs[i][:, 0:TT], ALU.mult, ["f4_%d" % (6 + i), "f4_%d" % (4 + i)], ["f4_%d" % (6 + i)])
                    ACT(sqg[:], yT[i][:, 0:TT], AF.Square, ["f4_%d" % (6 + i)], ["h2_2"])
                    for hf in range(NH):
                        MM(gt[:, hf * HW:(hf + 1) * HW], onesb[:], sqg[:, hf * HW:(hf + 1) * HW], i == 0, i == 1, ["onesb", "h2_2"], [gtok[hf * HW // 512]])
                rstd_from(gt[:, 0:TT], gtok, 256, f4[0][:, 0:TT], "f4_0")
                for i in range(2):
                    STT(vT[:, g * 2 + i, :], yT[i][:, 0:TT], vec["snorm"][:, g * 2 + i:g * 2 + i + 1], f4[0][:, 0:TT], ALU.mult, ALU.mult,
                        ["f4_%d" % (6 + i), "f4_0", "v_snorm"], [("vT", g * 2 + i)])
            stop(9)

        def phaseB(t0):
            stop(4)
            S.label = 'B1'
            vtoks = [("vT", k) for k in range(KM)]
            for cc in range(KD):
                bt, btok = nextbig()
                nkk = (KM + KD - 1) // KD
                wl = []
                for part in range(nkk):
                    k0 = part * KD
                    nk = min(KD, KM - k0)
                    wl.append((load_w_chunk(w_out[k0 * 128:(k0 + nk) * 128, cc * 128:(cc + 1) * 128], nk), k0, nk))
                for hf in range(NH):
                    for (wb, wtok), k0, nk in wl:
                        for k in range(nk):
                            MM(bt[:, hf * HW:(hf + 1) * HW], wb[:, k, :], vT[:, k0 + k, hf * HW:(hf + 1) * HW], k0 + k == 0, k0 + k == KM - 1,
                               [wtok] + vtoks, [btok[hf * HW // 512]])
                st, stok = f4[cc % 2], "f4_%d" % (cc % 2)
                sq, sqtok = h2[cc % 2], "h2_%d" % (cc % 2)
                CP("act", st[:, 0:TT], bt[:, 0:TT], btok, [stok])
                ACT(sq[:], bt[:, 0:TT], AF.Square, btok, [sqtok])
                for hf in range(NH):
                    MM(aux0[:, hf * HW:(hf + 1) * HW], onesb[:], sq[:, hf * HW:(hf + 1) * HW], cc == 0, cc == KD - 1, ["onesb", sqtok], [("aux0", hf * HW // 512)])
                DMA("sp", msT[cc], st[:, 0:TT], [stok], [("msT", cc)], stok + "s")
            r1 = f4[7]
            rstd_from(aux0[:, 0:TT], auxt("aux0"), D, r1[:, 0:TT], "f4_7")
            stop(5)
            S.label = 'B2'
            for cc in range(KD):
                ms, mstok = f4[cc % 2], "f4_%d" % (cc % 2)
                xb, xbtok = f4[2 + cc % 2], "f4_%d" % (2 + cc % 2)
                sq, sqtok = h2[cc % 2], "h2_%d" % (cc % 2)
                DMA("sp", ms[:, 0:TT], msT[cc], [("msT", cc)], [mstok], mstok)
                DMA("sp", xb[:, 0:TT].rearrange("p (j m) -> p j m", m=128),
                    xm[t0:t0 + TT, cc * 128:(cc + 1) * 128].rearrange("(j p) m -> p j m", p=128), [], [xbtok], xbtok)
                bt, btok = nextbig()
                for j in range(NQ):
                    TR(bt[:, j * 128:(j + 1) * 128], xb[:, j * 128:(j + 1) * 128], identf[:], [xbtok, "identf"], [btok[(j * 128) // 512]])
                STT(ms[:, 0:TT], ms[:, 0:TT], vec["g_post"][:, cc:cc + 1], r1[:, 0:TT], ALU.mult, ALU.mult, [mstok, "f4_7", "v_g_post"], [mstok])
                TTn("dve", ms[:, 0:TT], ms[:, 0:TT], bt[:, 0:TT], ALU.add, [mstok] + btok, [mstok])
                DMA("sp", x1s[cc], ms[:, 0:TT], [mstok], [("x1s", cc)], mstok + "s")
                ACT(sq[:], ms[:, 0:TT], AF.Square, [mstok], [sqtok])
                for hf in range(NH):
                    MM(aux1[:, hf * HW:(hf + 1) * HW], onesb[:], sq[:, hf * HW:(hf + 1) * HW], cc == 0, cc == KD - 1, ["onesb", sqtok], [("aux1", hf * HW // 512)])
            r2 = f4[6]
            rstd_from(aux1[:, 0:TT], auxt("aux1"), D, r2[:, 0:TT], "f4_6")
            stop(6)
            S.label = 'B3'
            for cc in range(KD):
                xb, xbtok = f4[cc % 2], "f4_%d" % (cc % 2)
                DMA("sp", xb[:, 0:TT], x1s[cc], [("x1s", cc)], [xbtok], xbtok)
                STT(hT[:, cc, :], xb[:, 0:TT], vec["g_mlp"][:, cc:cc + 1], r2[:, 0:TT], ALU.mult, ALU.mult, [xbtok, "f4_6", "v_g_mlp"],
                    [("hT", hf) for hf in range(NH)])
            stop(7)
            S.label = 'MLP'
            KB = c.FC // 128
            htoks = [("hT", hf) for hf in range(NH)]
            for fb in range(c.DFF // c.FC):
                for f in range(KB):
                    wb, wtok = load_w_chunk(w1[:, fb * c.FC + f * 128:fb * c.FC + (f + 1) * 128])
                    bt, btok = nextbig()
                    for hf in range(NH):
                        for k in range(KD):
                            MM(bt[:, hf * HW:(hf + 1) * HW], wb[:, k, :], hT[:, k, hf * HW:(hf + 1) * HW], k == 0, k == KD - 1, [wtok] + htoks, [btok[hf * HW // 512]])
                    rl, rltok = f4[2 + f % 2], "f4_%d" % (2 + f % 2)
                    ACT(rl[:, 0:TT], bt[:, 0:TT], AF.Relu, btok, [rltok])
                    TTn("dve", actT[:, f, :], rl[:, 0:TT], rl[:, 0:TT], ALU.mult, [rltok], [("act", f), "ovl", "ovlb"] + ALT)
                for cc in range(KD):
                    wb, wtok = load_w_chunk(w2[fb * c.FC:(fb + 1) * c.FC, cc * 128:(cc + 1) * 128], KB)
                    bt, btok = nextbig()
                    for hf in range(NH):
                        for k in range(KB):
                            MM(bt[:, hf * HW:(hf + 1) * HW], wb[:, k, :], actT[:, k, hf * HW:(hf + 1) * HW], k == 0, k == KB - 1,
                               [wtok, ("act", k)], [btok[hf * HW // 512]])
                    if fb == 0:
                        CP("act", accT[:, cc, :], bt[:, 0:TT], btok + vtoks, [("acc", cc)] + vtoks)
                    else:
                        TTn("dve", accT[:, cc, :], accT[:, cc, :], bt[:, 0:TT], ALU.add, btok + [("acc", cc)], [("acc", cc)])
            stop(8)
            S.label = 'B4'
            for cc in range(KD):
                sq, sqtok = h2[cc % 2], "h2_%d" % (cc % 2)
                ACT(sq[:], accT[:, cc, :], AF.Square, [("acc", cc)] + vtoks, [sqtok])
                for hf in range(NH):
                    MM(aux0[:, hf * HW:(hf + 1) * HW], onesb[:], sq[:, hf * HW:(hf + 1) * HW], cc == 0, cc == KD - 1, ["onesb", sqtok], [("aux0", hf * HW // 512)])
            r3 = f4[7]
            rstd_from(aux0[:, 0:TT], auxt("aux0"), D, r3[:, 0:TT], "f4_7")
            for cc in range(KD):
                xb, xbtok = f4[cc % 2], "f4_%d" % (cc % 2)
                ob, obtok = f4[2 + cc % 2], "f4_%d" % (2 + cc % 2)
                os_, ostok = f4[4 + cc % 2], "f4_%d" % (4 + cc % 2)
                DMA("sp", xb[:, 0:TT], x1s[cc], [("x1s", cc)], [xbtok], xbtok)
                STT(ob[:, 0:TT], accT[:, cc, :], vec["g_pmlp"][:, cc:cc + 1], r3[:, 0:TT], ALU.mult, ALU.mult, [("acc", cc), "f4_7", "v_g_pmlp"] + vtoks, [obtok])
                TTn("dve", ob[:, 0:TT], ob[:, 0:TT], xb[:, 0:TT], ALU.add, [obtok, xbtok], [obtok])
                bt, btok = nextbig()
                for j in range(NQ):
                    TR(bt[:, j * 128:(j + 1) * 128], ob[:, j * 128:(j + 1) * 128], identf[:], [obtok, "identf"], [btok[(j * 128) // 512]])
                CP("act", os_[:, 0:TT], bt[:, 0:TT], btok, [ostok])
                DMA("sp", out[t0:t0 + TT, cc * 128:(cc + 1) * 128].rearrange("(j p) m -> p j m", p=128),
                    os_[:, 0:TT].rearrange("p (j m) -> p j m", m=128), [ostok], [("out", t0, cc)], ostok + "o")
                S.out_tokens.append(("out", t0, cc))

        def _program():
            for ti in range(c.NPRE):
                phaseA(xp, ti * TT, False, last_prefix=(ti == c.NPRE - 1))
            alltok = [("prev", g) for g in range(G)]
            TS("dve", prev[:], prev[:], flag[:, 0:1], None, ALU.mult, None, alltok + ["flag"], alltok)
            TS("dve", hcar[:], hcar[:], flag[:, 0:1], None, ALU.mult, None, ["hcar", "flag"], ["hcar"])
            for ti in range(c.NMAIN):
                phaseA(xm, ti * TT, True)
                phaseB(ti * TT)

        S.out_tokens = []

        class _Stop(Exception):
            pass

        def stop(level):
            if getattr(cfg, "STOP", 0) == level:
                raise _Stop()

        try:
            _program()
        except _Stop:
            pass
        S.op("dve", lambda e: e.engine_nop(), S.out_tokens, [])
        S.emit(reorder=getattr(cfg, "REORDER", True))
        nc._sched = S
    return nc


def _vec_layouts(cfg, p):
    c = cfg
    pm = lambda v, n: np.ascontiguousarray(np.asarray(v, np.float32).reshape(n, 128).T)
    bcast = lambda v: np.ascontiguousarray(np.broadcast_to(np.asarray(v, np.float32)[None, :], (128, len(v))))
    cw = lambda w, n: np.ascontiguousarray(np.asarray(w, np.float32).reshape(4, n, 128).transpose(2, 1, 0).reshape(128, n * 4))
    return dict(
        g_pre=pm(p["pre_mix_norm"], c.KD), g_post=pm(p["post_mix_norm"], c.KD), g_mlp=pm(p["pre_mlp_norm"], c.KD),
        g_pmlp=pm(p["post_mlp_norm"], c.KD),
        cws=cw(p["ssd_conv_w"], c.KX), cbs=pm(p["ssd_conv_b"], c.KX), cwl=cw(p["lru_conv_w"], c.NBLK), cbl=pm(p["lru_conv_b"], c.NBLK),
        lba=pm(p["lru_b_a"], c.NBLK), lbx=pm(p["lru_b_x"], c.NBLK), lam=pm(p["lru_lambda"], c.NBLK), lnorm=pm(p["lru_norm"], c.NBLK),
        snorm=pm(p["ssd_norm"], c.KS), dtb=bcast(p["ssd_dt_bias"]), alog=bcast(p["ssd_a_log"]), dsk=bcast(p["ssd_d"]))


def make_in_maps(cfg, inputs, n_batch, n_half):
    c = cfg
    p = {k: np.asarray(v)[0] for k, v in inputs.items() if k != "x"}
    x = np.asarray(inputs["x"], np.float32)
    vl = _vec_layouts(c, p)
    half = c.NMAIN * c.TT
    shared = dict(w_in=np.ascontiguousarray(p["w_in"], np.float32), w_out=np.ascontiguousarray(p["w_out"], np.float32),
                  w1=np.ascontiguousarray(p["w_mlp_in"], np.float32), w2=np.ascontiguousarray(p["w_mlp_out"], np.float32),
                  lw=np.ascontiguousarray(np.concatenate([np.asarray(p["lru_w_a"], np.float32), np.asarray(p["lru_w_x"], np.float32)], axis=2)), **vl)
    maps = []
    for b in range(n_batch):
        for s in range(n_half):
            m = dict(shared)
            m["xm"] = np.ascontiguousarray(x[b, s * half:(s + 1) * half])
            pre = np.zeros((max(c.NPRE, 1) * c.TT, c.D), np.float32)
            if s > 0:
                pre[:] = x[b, (s - 1) * half:s * half][-pre.shape[0]:]
            m["xp"] = pre
            m["flag"] = np.full((128, 1), 1.0 if s > 0 else 0.0, np.float32)
            maps.append(m)
    return maps


def kernel(**inputs):
    cfg = Cfg()
    nc = build(cfg)
    maps = make_in_maps(cfg, inputs, 4, 2)
    res = run_bass_kernel_spmd(nc, maps, core_ids=list(range(8)))
    x = np.asarray(inputs["x"])
    outp = np.empty(x.shape, np.float32)
    half = cfg.NMAIN * cfg.TT
    i = 0
    for b in range(4):
        for s in range(2):
            outp[b, s * half:(s + 1) * half] = np.asarray(res.results[i]["out"], np.float32)
            i += 1
    return outp
```

```python
import contextlib
import numpy as np
import concourse.bass as bass
import concourse.mybir as mybir
from concourse.bass_utils import run_bass_kernel_spmd

F32 = mybir.dt.float32
BF16 = mybir.dt.bfloat16
AF = mybir.ActivationFunctionType
ALU = mybir.AluOpType
EPS = 1e-6


class Cfg:
    def __init__(self, D=2048, H=32, G=8, DL=2048, DFF=8192, TT=1024, NMAIN=2, NPRE=2, FC=1024):
        self.D, self.H, self.G, self.DL, self.DFF = D, H, G, DL, DFF
        self.TT, self.NMAIN, self.NPRE, self.FC = TT, NMAIN, NPRE, FC
        self.P, self.N = 64, 128
        self.DS = H * 64
        self.HG = H // G
        assert self.HG == 4
        self.KD = D // 128
        self.KS = self.DS // 128
        self.DXBC = self.DS + 2 * G * 128
        self.KX = self.DXBC // 128
        self.DIN = self.DS + self.DXBC + H + 2 * DL
        self.NBLK = DL // 128
        self.DMIX = self.DS + DL
        self.KM = self.DMIX // 128
        self.KF = DFF // 128
        self.NQ = TT // 128
        self.HW = min(512, TT)
        self.NH = TT // self.HW
        self.oxs = self.DS
        self.oB = 2 * self.DS
        self.oC = 2 * self.DS + G * 128
        self.odt = self.DS + self.DXBC
        self.ogate = self.odt + H
        self.oxl = self.ogate + DL


TUNE = dict(win=256, poolx=1.0, lat=0.0, dmaiss=900.0, actx=1.0, dvex=1.0, pex=1.0, bw=250.0)


class Sched:
    ENGS = ("pe", "act", "dve", "pool", "sp")

    def __init__(self, nc):
        self.nc = nc
        self.ops = []
        self.last_w = {}
        self.readers = {}
        self.dma_cnt = {}

    def op(self, eng, fn, reads=(), writes=(), dma_key=None, cost=300.0, nbytes=0):
        idx = len(self.ops)
        deps = set()
        for t in reads:
            w = self.last_w.get(t)
            if w is not None:
                deps.add(w)
        for t in writes:
            w = self.last_w.get(t)
            if w is not None:
                deps.add(w)
            deps.update(self.readers.get(t, ()))
        deps.discard(idx)
        o = dict(eng=eng, fn=fn, deps=deps, dma=dma_key is not None, key=dma_key, ms=False, cost=cost, nbytes=nbytes, label=getattr(self, 'label', ''))
        if dma_key is not None:
            n = self.dma_cnt.get(dma_key, 0) + 1
            self.dma_cnt[dma_key] = n
            o["dval"] = 16 * n
        self.ops.append(o)
        for t in reads:
            self.readers.setdefault(t, []).append(idx)
        for t in writes:
            self.last_w[t] = idx
            self.readers[t] = []
        return idx

    def schedule(self, window=None):
        ops = self.ops
        window = window or TUNE["win"]
        mult = dict(pool=TUNE["poolx"], act=TUNE["actx"], dve=TUNE["dvex"], pe=TUNE["pex"], sp=1.0)
        pend = {e: [] for e in self.ENGS}
        for i, o in enumerate(ops):
            pend[o["eng"]].append(i)
        head = {e: 0 for e in self.ENGS}
        tfree = {e: 0.0 for e in self.ENGS}
        done = [None] * len(ops)
        sched = [False] * len(ops)
        order = {e: [] for e in self.ENGS}
        pipe_free = 0.0
        tf_prev = {}
        remaining = len(ops)
        while remaining:
            progressed = False
            for e in sorted(self.ENGS, key=lambda x: tfree[x]):
                lst = pend[e]
                while head[e] < len(lst) and sched[lst[head[e]]]:
                    head[e] += 1
                if head[e] >= len(lst):
                    continue
                best, best_t = None, None
                seen = 0
                j = head[e]
                while j < len(lst) and seen < window:
                    i = lst[j]
                    j += 1
                    if sched[i]:
                        continue
                    seen += 1
                    t = tfree[e]
                    ok = True
                    for d in ops[i]["deps"]:
                        dt_ = done[d]
                        if dt_ is None:
                            ok = False
                            break
                        if dt_ > t:
                            t = dt_
                    if ok and (best is None or t < best_t - 1e-9):
                        best, best_t = i, t
                        if t <= tfree[e] + 1e-9:
                            break
                if best is None:
                    continue
                o = ops[best]
                if o["dma"]:
                    issue = TUNE["dmaiss"] if e == "pool" else 120.0
                    xs_ = max(best_t + issue, pipe_free)
                    pipe_free = xs_ + o["nbytes"] / TUNE["bw"]
                    done[best] = pipe_free + 2000.0
                    tfree[e] = best_t + issue
                else:
                    tfree[e] = best_t + o["cost"] * mult[e]
                    tf_prev[best] = tfree[e]
                    done[best] = tfree[e] + TUNE["lat"]
                sched[best] = True
                o['t0'] = best_t
                o['crit'] = max(list(o['deps']) + ([order[e][-1]] if order[e] else []), key=lambda d_: (done[d_] if (ops[d_]['eng'] != e or ops[d_]['dma']) else tf_prev.get(d_, done[d_])), default=None)
                o['t1'] = done[best]
                order[e].append(best)
                remaining -= 1
                progressed = True
                break
            assert progressed, "list scheduler stuck"
        self.makespan = max(d for d in done if d is not None)
        return order

    def emit(self, reorder=True):
        nc, ops = self.nc, self.ops
        if reorder:
            order = self.schedule()
        else:
            order = {e: [i for i, o in enumerate(ops) if o["eng"] == e] for e in self.ENGS}
        pos = {}
        for e in self.ENGS:
            for k, i in enumerate(order[e]):
                pos[i] = k
        for i, o in enumerate(ops):
            latest = {}
            eff = []
            for d in o["deps"]:
                p = ops[d]
                if p["dma"]:
                    eff.append(d)
                    continue
                if p["eng"] == "pe" and o["eng"] == "pe" and not o["dma"]:
                    assert pos[d] < pos[i]
                    continue
                if p["eng"] == o["eng"]:
                    assert pos[d] < pos[i]
                if p["eng"] not in latest or pos[d] > pos[latest[p["eng"]]]:
                    latest[p["eng"]] = d
            o["eff"] = eff + list(latest.values())
            for d in latest.values():
                ops[d]["ms"] = True
        KSEM = 8
        cnt = {e: 0 for e in self.ENGS}
        for e in self.ENGS:
            for i in order[e]:
                o = ops[i]
                if o["ms"]:
                    o["msi"] = cnt[e]
                    cnt[e] += 1
        self.ms_counts = cnt
        with contextlib.ExitStack() as es:
            esem = {e: [es.enter_context(nc.semaphore("S_%s%d" % (e, i))) for i in range(KSEM)] for e in self.ENGS if cnt[e] > 0}
            dsem = {}
            for i, (k, n) in enumerate(self.dma_cnt.items()):
                r = 1 if k in ("setup", "setup_p") else max(1, (n * 16 + 479) // 480)
                dsem[k] = [es.enter_context(nc.semaphore("D_%d_%d" % (i, j))) for j in range(r)]
            block = es.enter_context(nc.Block())

            def semval(p):
                if p["dma"]:
                    lst = dsem[p["key"]]
                    if p["key"] in ("setup", "setup_p"):
                        return lst[0], 16 * self.dma_cnt[p["key"]]
                    n = p["dval"] // 16 - 1
                    return lst[n % len(lst)], 16 * (n // len(lst) + 1)
                i = p["msi"]
                return esem[p["eng"]][i % KSEM], i // KSEM + 1

            def run(engname, eng):
                waited = {}
                for oi in order[engname]:
                    o = ops[oi]
                    need = {}
                    for d in o["eff"]:
                        p = ops[d]
                        s, v = semval(p)
                        if need.get(id(s), (None, 0))[1] < v:
                            need[id(s)] = (s, v)
                    for s, v in need.values():
                        if waited.get(id(s), 0) < v:
                            eng.wait_ge(s, v)
                            waited[id(s)] = v
                    ins = o["fn"](eng)
                    if o["dma"]:
                        ins.then_inc(semval(o)[0], 16)
                    elif o["ms"]:
                        ins.then_inc(semval(o)[0], 1)

            block.tensor(lambda e: run("pe", e))
            block.scalar(lambda e: run("act", e))
            block.vector(lambda e: run("dve", e))
            block.gpsimd(lambda e: run("pool", e))
            block.sync(lambda e: run("sp", e))


def build(cfg):
    c = cfg
    D, H, G, TT, KD, KS, KX, KM, NQ, NH, HW, NBLK = c.D, c.H, c.G, c.TT, c.KD, c.KS, c.KX, c.KM, c.NQ, c.NH, c.HW, c.NBLK
    nc = bass.Bass("TRN2", target_bir_lowering=False)
    din = lambda name, shape: nc.dram_tensor(name, list(shape), F32, kind="ExternalInput").ap()
    xm = din("xm", [c.NMAIN * TT, D])
    xp = din("xp", [max(c.NPRE, 1) * TT, D])
    flag_d = din("flag", [128, 1])
    w_in = din("w_in", [D, c.DIN])
    w_out = din("w_out", [c.DMIX, D])
    w1 = din("w1", [D, c.DFF])
    w2 = din("w2", [c.DFF, D])
    lw_d = din("lw", [NBLK, 128, 256])
    vec_d = {}
    vec_shapes = dict(g_pre=[128, KD], g_post=[128, KD], g_mlp=[128, KD], g_pmlp=[128, KD],
                      cws=[128, KX * 4], cbs=[128, KX], cwl=[128, NBLK * 4], cbl=[128, NBLK],
                      lba=[128, NBLK], lbx=[128, NBLK], lam=[128, NBLK], lnorm=[128, NBLK],
                      snorm=[128, KS], dtb=[128, H], alog=[128, H], dsk=[128, H])
    for k, shp in vec_shapes.items():
        vec_d[k] = din(k, shp)
    out = nc.dram_tensor("out", [c.NMAIN * TT, D], F32, kind="ExternalOutput").ap()
    msT = nc.dram_tensor("msT", [KD, 128, TT], F32).ap()
    x1s = nc.dram_tensor("x1s", [KD, 128, TT], F32).ap()

    es = contextlib.ExitStack()
    with es:
        sb = lambda name, shape, dt: es.enter_context(nc.sbuf_tensor(name, list(shape), dt))
        S = Sched(nc)
        if getattr(cfg, "PAD", 0):
            sb("pad", [128, cfg.PAD // 4], F32)
        def ACT(out_, in_, func, r, w, bias=None, scale=None, accum=None):
            kw = {}
            if bias is not None:
                kw["bias"] = bias
            if scale is not None:
                kw["scale"] = scale
            if accum is not None:
                kw["accum_out"] = accum
            S.op("act", lambda e: e.activation(out=out_, in_=in_, func=func, **kw), r, w, cost=(224.0 + in_.free_size()) / 1.2 + (90.0 if accum is not None else 0.0))

        def TTn(eng, out_, in0, in1, op, r, w):
            S.op(eng, lambda e: e.tensor_tensor(out=out_, in0=in0, in1=in1, op=op), r, w, cost=(130.0 + out_.free_size()) / 0.96)

        def TS(eng, out_, in0, s1, s2, op0, op1, r, w):
            if s2 is None:
                S.op(eng, lambda e: e.tensor_scalar(out=out_, in0=in0, scalar1=s1, scalar2=None, op0=op0), r, w, cost=(130.0 + out_.free_size()) / 0.96)
            else:
                S.op(eng, lambda e: e.tensor_scalar(out=out_, in0=in0, scalar1=s1, scalar2=s2, op0=op0, op1=op1), r, w, cost=(130.0 + out_.free_size()) / 0.96)

        def STT(out_, in0, scalar, in1, op0, op1, r, w):
            S.op("dve", lambda e: e.scalar_tensor_tensor(out=out_, in0=in0, scalar=scalar, in1=in1, op0=op0, op1=op1), r, w, cost=(130.0 + out_.free_size()) / 0.96)

        def CP(eng, out_, in_, r, w):
            if eng == "act":
                S.op("act", lambda e: e.activation(out=out_, in_=in_, func=AF.Copy), r, w, cost=(224.0 + in_.free_size()) / 1.2)
            else:
                S.op(eng, lambda e: e.tensor_copy(out=out_, in_=in_), r, w, cost=(130.0 + out_.free_size()) / 0.96)

        def MM(out_, lhsT, rhs, start, stop, r, w):
            S.op("pe", lambda e: e.matmul(out_, lhsT=lhsT, rhs=rhs, start=start, stop=stop), r, w, cost=max(64.0, rhs.free_size()) / 2.4 + 12.0)

        def TR(out_, in_, ident, r, w):
            S.op("pe", lambda e: e.transpose(out_, in_, ident), r, w, cost=110.0)

        def DMA(q, out_, in_, r, w, key):
            S.op(q, lambda e: e.dma_start(out=out_, in_=in_), r, w, dma_key=key, nbytes=max(out_.nbytes(), in_.nbytes()))

        def bc(ap, axis, shape):
            return ap.unsqueeze(axis).to_broadcast(list(shape))

        identb = sb("identb", [128, 128], BF16)
        identf = sb("identf", [128, 128], F32)
        onesb = sb("onesb", [128, 128], BF16)
        trib = sb("trib", [128, 128], BF16)
        negm = sb("negm", [128, 4, 128], BF16)
        vec = {k: sb("v_" + k, shp, F32) for k, shp in vec_shapes.items()}
        flag = sb("flag_sb", [128, 1], F32)
        clru = sb("clru", [128, NBLK], F32)
        a_bc = sb("a_bc", [128, H], F32)
        wdt = sb("wdt", [128, KD, H], BF16)
        lwb = sb("lwb", [128, 2, 256], BF16)
        prev = sb("prev", [128, G * 256], F32)
        prevb = sb("prevb", [128, NQ, 256], BF16)
        hcar = sb("hcar", [128, NBLK], F32)
        car_s = sb("car_s", [128, KX, 3], F32)
        car_l = sb("car_l", [128, NBLK, 3], F32)
        QH = NQ * H
        dtf = {k: sb("dt_" + k, [128, NQ, H], F32) for k in ("dt", "adt", "acs", "ea", "cdb", "dtde", "t0", "t1", "t2")}
        adt_hi = sb("adt_hi", [128, NQ, H], BF16)
        adt_lo = sb("adt_lo", [128, NQ, H], BF16)
        ssq_x = sb("ssq_x", [128, NQ], F32)
        RAK = max(KD, (KM + 1) // 2)
        RA = sb("RA", [128, RAK * TT], F32)
        vT = RA[:].bitcast(BF16).rearrange("p (k t) -> p k t", t=TT)
        accT = RA[:].rearrange("p (k t) -> p k t", t=TT)
        hT = sb("hT", [128, KD, TT], BF16)
        OVW = max(c.FC // 128 * TT // 2, D + D // 2)
        ovl = sb("ovl", [128, OVW], F32)
        xt = ovl[:, 0:D]
        xn = ovl[:, D:D + D // 2].bitcast(BF16)
        actT = ovl[:, 0:c.FC // 128 * TT // 2].bitcast(BF16).rearrange("p (k t) -> p k t", t=TT)
        ALT = []
        if getattr(cfg, "ALTSSD", False) and OVW >= 3 * (TT + 4):
            ALT = ["alt_xs0", "alt_xs1", "alt_b", "alt_c"]
            alt_xs = (ovl[:, 0:TT + 4], ovl[:, TT + 4:2 * (TT + 4)])
            alt_b = ovl[:, 2 * (TT + 4):2 * (TT + 4) + TT // 2].bitcast(BF16)
            alt_c = ovl[:, 2 * (TT + 4) + TT // 2:2 * (TT + 4) + TT].bitcast(BF16)
        NW = 3
        wbuf = [sb("wbuf%d" % i, [128, max(KD, c.FC // 128), 128], BF16) for i in range(NW)]
        NF = 8
        f4 = [sb("f4_%d" % i, [128, TT + 4], F32) for i in range(NF)]
        h2 = [sb("h2_%d" % i, [128, TT], BF16) for i in range(3)]
        NSM = 2
        smsets = [{k: sb("sm%d_%s" % (n_, k), shp, dt) for k, (shp, dt) in dict(
            xs=([128, 256], F32), x6=([128, 256], BF16), x6e=([128, 256], BF16), bt=([128, 128], BF16),
            xhi=([128, 4, 128], BF16), xlo=([128, 4, 128], BF16), t=([128, 4, 128], F32),
            mt=([128, 4, 128], BF16), y=([128, 256], F32)).items()} for n_ in range(NSM)]
        ps = lambda name: es.enter_context(nc.psum_tensor(name, [128, 1024], F32))
        big = [ps("big0"), ps("big1")]
        aux0, aux1 = ps("aux0"), ps("aux1")
        bigi = [0]

        def nextbig():
            i = bigi[0] % 2
            bigi[0] += 1
            return big[i], [("big%d" % i, b) for b in range((TT + 511) // 512)]

        wi = [0]

        def nextw():
            i = wi[0] % NW
            wi[0] += 1
            return wbuf[i], "wbuf%d" % i

        for k in vec_shapes:
            DMA("sp", vec[k][:], vec_d[k][:], [], ["v_" + k], "setup")
        DMA("sp", flag[:], flag_d[:], [], ["flag"], "setup")
        DMA("pool", wdt[:], w_in[:, c.odt:c.odt + H].rearrange("(k p) h -> p k h", p=128), [], ["wdt"], "setup_p")

        def mask_const(t, tok, fill, pat, cm, op):
            S.op("pool", lambda e: e.affine_select(out=t[:], in_=t[:], pattern=pat, compare_op=op, fill=fill,
                                                   base=0, channel_multiplier=cm), [tok], [tok])

        S.op("pool", lambda e: e.memset(identb[:], 1.0), [], ["identb"])
        mask_const(identb, "identb", 0.0, [[-1, 128]], 1, ALU.is_equal)
        S.op("pool", lambda e: e.memset(identf[:], 1.0), [], ["identf"])
        mask_const(identf, "identf", 0.0, [[-1, 128]], 1, ALU.is_equal)
        S.op("pool", lambda e: e.memset(onesb[:], 1.0), [], ["onesb"])
        S.op("pool", lambda e: e.memset(trib[:], 1.0), [], ["trib"])
        mask_const(trib, "trib", 0.0, [[1, 128]], -1, ALU.is_ge)
        S.op("pool", lambda e: e.memset(negm[:], 0.0), [], ["negm"])
        mask_const(negm, "negm", -30000.0, [[0, 4], [1, 128]], -1, ALU.is_ge)
        S.op("pool", lambda e: e.memset(prev[:], 0.0), [], ["prev"])
        S.op("pool", lambda e: e.memset(hcar[:], 0.0), [], ["hcar"])
        S.op("pool", lambda e: e.memset(car_s[:], 0.0), [], ["car_s"])
        S.op("pool", lambda e: e.memset(car_l[:], 0.0), [], ["car_l"])

        def log1p_small(out_, e_, tmpw, tmpq, tok_o, tok_e, tok_w, tok_q):
            TS("dve", tmpw, e_, 2.0, None, ALU.add, None, [tok_e], [tok_w])
            S.op("dve", lambda e: e.reciprocal(out=tmpw, in_=tmpw), [tok_w], [tok_w])
            TTn("dve", tmpw, tmpw, e_, ALU.mult, [tok_w, tok_e], [tok_w])
            TTn("dve", out_, tmpw, tmpw, ALU.mult, [tok_w], [tok_o])
            TS("dve", tmpq, out_, 1.0 / 11.0, None, ALU.mult, None, [tok_o], [tok_q])
            for cst in (1.0 / 9.0, 1.0 / 7.0, 1.0 / 5.0, 1.0 / 3.0):
                STT(tmpq, tmpq, cst, out_, ALU.add, ALU.mult, [tok_q, tok_o], [tok_q])
            TS("dve", tmpq, tmpq, 1.0, None, ALU.add, None, [tok_q], [tok_q])
            TTn("dve", tmpq, tmpq, tmpw, ALU.mult, [tok_q, tok_w], [tok_q])
            TS("dve", out_, tmpq, 2.0, None, ALU.mult, None, [tok_q], [tok_o])

        def softplus(out_, x_, ta, tb, tcc, tok_o, tok_x, tok_a, tok_b, tok_c):
            TS("dve", ta, x_, -1.0, None, ALU.mult, None, [tok_x], [tok_a])
            TTn("dve", ta, ta, x_, ALU.max, [tok_a, tok_x], [tok_a])
            ACT(ta, ta, AF.Exp, [tok_a], [tok_a], scale=-1.0)
            log1p_small(tb, ta, tcc, out_, tok_b, tok_a, tok_c, tok_o)
            TS("dve", ta, x_, 0.0, None, ALU.max, None, [tok_x], [tok_a])
            TTn("dve", out_, ta, tb, ALU.add, [tok_a, tok_b], [tok_o])

        t0s = dtf["t0"][:, 0, 0:NBLK] if NBLK <= H else None
        assert NBLK <= H
        t1s, t2s, t3s, t4s = (dtf[k][:, 0, 0:NBLK] for k in ("t1", "t2", "dt", "adt"))
        TS("dve", t0s, vec["lam"][:], -1.0, None, ALU.mult, None, ["v_lam"], ["dt_t0"])
        softplus(t1s, t0s, t2s, t3s, t4s, "dt_t1", "dt_t0", "dt_t2", "dt_dt", "dt_adt")
        TS("dve", clru[:], t1s, -8.0, None, ALU.mult, None, ["dt_t1"], ["clru"])
        ACT(a_bc[:], vec["alog"][:], AF.Exp, ["v_alog"], ["a_bc"])
        TS("dve", a_bc[:], a_bc[:], -1.0, None, ALU.mult, None, ["a_bc"], ["a_bc"])

        def load_w_chunk(src_ap, nk=None):
            wb, tok = nextw()
            nk = KD if nk is None else nk
            DMA("pool", wb[:, 0:nk, :], src_ap.rearrange("(k p) m -> p k m", p=128), [], [tok], tok)
            return wb, tok

        def proj_chunk(col0):
            wb, wtok = load_w_chunk(w_in[:, col0:col0 + 128])
            bt, btok = nextbig()
            for hf in range(NH):
                for k in range(KD):
                    MM(bt[:, hf * HW:(hf + 1) * HW], wb[:, k, :], hT[:, k, hf * HW:(hf + 1) * HW], k == 0, k == KD - 1,
                       [wtok, ("hT", hf)], [btok[hf * HW // 512]])
            return bt, btok

        def conv_chunk(bt, btok, car, cidx, cwn, cbn, xpad, xptok, acc, acctok, car_tok):
            cw, cb = vec[cwn], vec[cbn]
            CP("pool", xpad[:, 0:3], car[:, cidx, :], [car_tok], [xptok])
            CP("act", xpad[:, 3:3 + TT], bt[:, 0:TT], btok + [xptok], [xptok])
            CP("pool", car[:, cidx, :], xpad[:, TT:TT + 3], [xptok], [car_tok])
            ACT(acc[:, 0:TT], xpad[:, 0:TT], AF.Identity, [xptok, "v_" + cwn, "v_" + cbn], [acctok],
                bias=cb[:, cidx:cidx + 1], scale=cw[:, 4 * cidx:4 * cidx + 1])
            for k in (1, 2, 3):
                STT(acc[:, 0:TT], xpad[:, k:k + TT], cw[:, 4 * cidx + k:4 * cidx + k + 1], acc[:, 0:TT], ALU.mult, ALU.add,
                    [xptok, acctok, "v_" + cwn], [acctok])

        def rstd_from(ps_ap, pstok, n, dst, dtok):
            ACT(dst, ps_ap, AF.Sqrt, pstok + ["eps"], [dtok], bias=eps_t[:, 0:1], scale=1.0 / n)
            S.op("dve", lambda e: e.reciprocal(out=dst, in_=dst), [dtok], [dtok])

        auxt = lambda nm: [(nm, b) for b in range((TT + 511) // 512)]
        eps_t = sb("eps_t", [128, 1], F32)
        S.op("pool", lambda e: e.memset(eps_t[:], EPS), [], ["eps"])

        acttoks = [("act", f) for f in range(c.FC // 128)]

        def phaseA(xsrc, t0, main, last_prefix=False):
            S.label = 'A0'
            for j in range(NQ):
                DMA("sp", xt, xsrc[t0 + j * 128:t0 + (j + 1) * 128, :], [], ["ovl"] + acttoks + ALT, "ovl")
                ACT(xn, xt, AF.Square, ["ovl"], ["ovlb", "ssq%d" % j] + ALT, accum=ssq_x[:, j:j + 1])
                ACT(ssq_x[:, j:j + 1], ssq_x[:, j:j + 1], AF.Sqrt, ["ssq%d" % j, "eps"], ["ssq%d" % j], bias=eps_t[:, 0:1], scale=1.0 / D)
                S.op("dve", lambda e, j=j: e.reciprocal(out=ssq_x[:, j:j + 1], in_=ssq_x[:, j:j + 1]), ["ssq%d" % j], ["ssq%d" % j])
                TS("dve", xn, xt, ssq_x[:, j:j + 1], None, ALU.mult, None, ["ovl", "ovlb", "ssq%d" % j], ["ovlb"])
                pt, ptok = nextbig()
                ptb = pt[:].bitcast(BF16)
                for k in range(KD):
                    TR(ptb[:, k * 128:(k + 1) * 128], xn[:, k * 128:(k + 1) * 128], identb[:], ["ovlb", "identb"], [ptok[(k * 64) // 512]])
                TTn("dve", hT[:, :, j * 128:(j + 1) * 128], ptb[:, 0:KD * 128].rearrange("p (k m) -> p k m", m=128),
                    bc(vec["g_pre"][:], 2, [128, KD, 128]), ALU.mult, ptok + ["v_g_pre"], [("hT", (j * 128) // HW)])
            stop(1)
            S.label = 'A1'
            a0v = aux0[:, 0:QH].rearrange("p (q h) -> p q h", h=H)
            a1v = aux0[:, 512:512 + QH].rearrange("p (q h) -> p q h", h=H)
            for q in range(NQ):
                for k in range(KD):
                    MM(a0v[:, q, :], hT[:, k, q * 128:(q + 1) * 128], wdt[:, k, :], k == 0, k == KD - 1,
                       [("hT", (q * 128) // HW), "wdt"], [("aux0", 0)])
            TTn("dve", dtf["t0"][:], a0v, bc(vec["dtb"][:], 1, [128, NQ, H]), ALU.add, [("aux0", 0), "v_dtb"], ["dt_t0"])
            softplus(dtf["dt"][:], dtf["t0"][:], dtf["t1"][:], dtf["t2"][:], dtf["adt"][:], "dt_dt", "dt_t0", "dt_t1", "dt_t2", "dt_adt")
            TTn("dve", dtf["adt"][:], dtf["dt"][:], bc(a_bc[:], 1, [128, NQ, H]), ALU.mult, ["dt_dt", "a_bc"], ["dt_adt"])
            CP("dve", adt_hi[:], dtf["adt"][:], ["dt_adt"], ["adt_hi"])
            TTn("dve", adt_lo[:], dtf["adt"][:], adt_hi[:], ALU.subtract, ["dt_adt", "adt_hi"], ["adt_lo"])
            for q in range(NQ):
                MM(a0v[:, q, :], trib[:], adt_hi[:, q, :], True, False, ["trib", "adt_hi"], [("aux0", 0)])
                MM(a0v[:, q, :], trib[:], adt_lo[:, q, :], False, True, ["trib", "adt_lo"], [("aux0", 0)])
                MM(a1v[:, q, :], onesb[:], adt_hi[:, q, :], True, False, ["onesb", "adt_hi"], [("aux0", 1)])
                MM(a1v[:, q, :], onesb[:], adt_lo[:, q, :], False, True, ["onesb", "adt_lo"], [("aux0", 1)])
            CP("act", dtf["acs"][:], a0v, [("aux0", 0)], ["dt_acs"])
            if main and not getattr(cfg, "NOEA", 0):
                ACT(dtf["ea"][:], dtf["acs"][:], AF.Exp, ["dt_acs"], ["dt_ea"])
            ACT(dtf["cdb"][:], a1v, AF.Exp, [("aux0", 1)], ["dt_cdb"])
            TTn("dve", dtf["t0"][:], a1v, dtf["acs"][:], ALU.subtract, [("aux0", 1), "dt_acs"], ["dt_t0"])
            ACT(dtf["t0"][:], dtf["t0"][:], AF.Exp, ["dt_t0"], ["dt_t0"])
            TTn("dve", dtf["dtde"][:], dtf["t0"][:], dtf["dt"][:], ALU.mult, ["dt_t0", "dt_dt"], ["dt_dtde"])

            stop(2)
            S.label = 'LRU' + ('m' if main else 'p')
            xpad, xl, rr, ii, a2, hl, ge = (f4[i] for i in range(7))
            xlb, sqb = h2[0], h2[1]
            for blk in range(NBLK):
                bt, btok = proj_chunk(c.oxl + blk * 128)
                conv_chunk(bt, btok, car_l, blk, "cwl", "cbl", xpad, "f4_0", xl, "f4_1", "car_l")
                CP("act", xlb[:], xl[:, 0:TT], ["f4_1"], ["h2_0"])
                lwtok = "lwb%d" % (blk % 2)
                DMA("pool", lwb[:, blk % 2, :], lw_d[blk], [], [lwtok], lwtok)
                for (wofs, bn, dst, dtok) in ((0, "lba", rr, "f4_2"), (128, "lbx", ii, "f4_3")):
                    gt, gtok = nextbig()
                    for hf in range(NH):
                        MM(gt[:, hf * HW:(hf + 1) * HW], lwb[:, blk % 2, wofs:wofs + 128], xlb[:, hf * HW:(hf + 1) * HW], True, True, [lwtok, "h2_0"], [gtok[hf * HW // 512]])
                    ACT(dst[:, 0:TT], gt[:, 0:TT], AF.Sigmoid, gtok + ["v_" + bn], [dtok], bias=vec[bn][:, blk:blk + 1])
                ACT(rr[:, 0:TT], rr[:, 0:TT], AF.Exp, ["f4_2", "clru"], ["f4_2"], scale=clru[:, blk:blk + 1])
                ACT(a2[:, 0:TT], rr[:, 0:TT], AF.Square, ["f4_2"], ["f4_4"])
                ACT(a2[:, 0:TT], a2[:, 0:TT], AF.Sqrt, ["f4_4"], ["f4_4"], bias=1.0, scale=-1.0)
                TTn("dve", ii[:, 0:TT], ii[:, 0:TT], xl[:, 0:TT], ALU.mult, ["f4_3", "f4_1"], ["f4_3"])
                TTn("dve", ii[:, 0:TT], ii[:, 0:TT], a2[:, 0:TT], ALU.mult, ["f4_3", "f4_4"], ["f4_3"])
                S.op("dve", lambda e, blk=blk: e.tensor_tensor_scan(out=hl[:, 0:TT], data0=rr[:, 0:TT], data1=ii[:, 0:TT],
                                                                     initial=hcar[:, blk:blk + 1], op0=ALU.mult, op1=ALU.add),
                     ["f4_2", "f4_3", "hcar"], ["f4_5"], cost=(130.0 + 2 * TT) / 0.96)
                CP("pool", hcar[:, blk:blk + 1], hl[:, TT - 1:TT], ["f4_5"], ["hcar"])
                if main:
                    gt, gtok = proj_chunk(c.ogate + blk * 128)
                    gx = f4[7]
                    CP("act", gx[:, 0:TT], gt[:, 0:TT], gtok, ["f4_7"])
                    TTn("dve", ge[:, 0:TT], gx[:, 0:TT], gx[:, 0:TT], ALU.mult, ["f4_7"], ["f4_6"])
                    TS("dve", ge[:, 0:TT], ge[:, 0:TT], 0.044715, 1.0, ALU.mult, ALU.add, ["f4_6"], ["f4_6"])
                    TTn("dve", ge[:, 0:TT], ge[:, 0:TT], gx[:, 0:TT], ALU.mult, ["f4_6", "f4_7"], ["f4_6"])
                    ACT(ge[:, 0:TT], ge[:, 0:TT], AF.Sigmoid, ["f4_6"], ["f4_6"], scale=1.5957691216057308)
                    TTn("dve", ge[:, 0:TT], ge[:, 0:TT], gx[:, 0:TT], ALU.mult, ["f4_6", "f4_7"], ["f4_6"])
                    TTn("dve", ge[:, 0:TT], ge[:, 0:TT], hl[:, 0:TT], ALU.mult, ["f4_6", "f4_5"], ["f4_6"])
                    ACT(sqb[:], ge[:, 0:TT], AF.Square, ["f4_6"], ["h2_1"])
                    CP("act", vT[:, KS + blk, :], ge[:, 0:TT], ["f4_6"], [("vT", KS + blk)])
                    for hf in range(NH):
                        MM(aux1[:, hf * HW:(hf + 1) * HW], onesb[:], sqb[:, hf * HW:(hf + 1) * HW], blk == 0, blk == NBLK - 1,
                           ["onesb", "h2_1"], [("aux1", hf * HW // 512)])
            if main:
                rl = f4[7]
                rstd_from(aux1[:, 0:TT], auxt("aux1"), c.DL, rl[:, 0:TT], "f4_7")
                for blk in range(NBLK):
                    STT(vT[:, KS + blk, :], vT[:, KS + blk, :], vec["lnorm"][:, blk:blk + 1], rl[:, 0:TT], ALU.mult, ALU.mult,
                        [("vT", KS + blk), "f4_7", "v_lnorm"], [("vT", KS + blk)])

            stop(3)
            S.label = 'SSD' + ('m' if main else 'p')
            zs = (f4[4], f4[5])
            yT = (f4[6], f4[7])
            sqg = h2[2]
            ps_acsb = aux0[:, 0:512]
            ps_xs = aux0[:, 512:768]
            ps_y = aux0[:, 768:1024]
            ps_st = aux1[:, 0:256]
            ps_yo = aux1[:, 256:512]
            ps_sc = aux1[:, 512:640]
            ps_yT = aux1[:, 640:896]
            ps_bt = aux1[:, 896:960].bitcast(BF16)
            for g in range(G):
                hs = slice(g * 4, g * 4 + 4)
                pg = prev[:, g * 256:(g + 1) * 256]
                if ALT and g % 2 == 1:
                    xsT, BT, CT = alt_xs, alt_b, alt_c
                    xstok, btk, ctk = ("alt_xs0", "alt_xs1"), "alt_b", "alt_c"
                    altw = ["ovl", "ovlb"] + acttoks
                else:
                    xsT, BT, CT = (f4[2], f4[3]), h2[0], h2[1]
                    xstok, btk, ctk = ("f4_2", "f4_3"), "h2_0", "h2_1"
                    altw = []
                for i in range(2):
                    cidx = g * 2 + i
                    bt, btok = proj_chunk(c.oxs + cidx * 128)
                    conv_chunk(bt, btok, car_s, cidx, "cws", "cbs", f4[0], "f4_0", f4[1], "f4_1", "car_s")
                    ACT(xsT[i][:, 0:TT], f4[1][:, 0:TT], AF.Silu, ["f4_1"], [xstok[i]] + altw)
                for (cidx, dst, dtok) in ((KS + g, BT, btk), (KS + G + g, CT, ctk)):
                    if cidx >= KS + G and not main and c.NMAIN > 0 and not last_prefix:
                        continue
                    bt, btok = proj_chunk(c.oxs + cidx * 128)
                    conv_chunk(bt, btok, car_s, cidx, "cws", "cbs", f4[0], "f4_0", f4[1], "f4_1", "car_s")
                    ACT(dst[:, 0:TT], f4[1][:, 0:TT], AF.Silu, ["f4_1"], [dtok] + altw)
                if main:
                    for i in range(2):
                        bt, btok = proj_chunk(g * 256 + i * 128)
                        ACT(zs[i][:, 0:TT], bt[:, 0:TT], AF.Silu, btok, ["f4_%d" % (4 + i)])
                stop(10)
                for q in range(NQ):
                    qs = slice(q * 128, (q + 1) * 128)
                    sm = smsets[q % NSM]
                    smp = "sm%d_" % (q % NSM)
                    for i in range(2):
                        TR(ps_xs[:, i * 128:(i + 1) * 128], xsT[i][:, qs], identf[:], [xstok[i], "identf"], [("aux0", 1)])
                    CP("act", sm["xs"][:], ps_xs, [("aux0", 1)], [smp + "xs"])
                    xs3 = sm["xs"][:].rearrange("p (h d) -> p h d", d=64)
                    TTn("dve", sm["x6e"][:].rearrange("p (h d) -> p h d", d=64), xs3, bc(dtf["dtde"][:, q, hs], 2, [128, 4, 64]),
                        ALU.mult, [smp + "xs", "dt_dtde"], [smp + "x6e"])
                    if main:
                        TTn("dve", sm["y"][:].rearrange("p (h d) -> p h d", d=64), xs3, bc(vec["dsk"][:, hs], 2, [128, 4, 64]),
                            ALU.mult, [smp + "xs", "v_dsk"], [smp + "y"])
                    TR(ps_bt, BT[:, qs], identb[:], [btk, "identb"], [("aux1", 1)])
                    CP("act", sm["bt"][:], ps_bt, [("aux1", 1)], [smp + "bt"])
                    MM(ps_st, sm["bt"][:], sm["x6e"][:], True, True, [smp + "bt", smp + "x6e"], [("aux1", 0)])
                    if main:
                        CP("act", prevb[:, q, :], pg, [("prev", g)], [("prevb", q)])
                    TTn("dve", pg.rearrange("p (h d) -> p h d", d=64), pg.rearrange("p (h d) -> p h d", d=64),
                        bc(dtf["cdb"][:, q, hs], 2, [128, 4, 64]), ALU.mult, [("prev", g), "dt_cdb"], [("prev", g)])
                    TTn("dve", pg, pg, ps_st, ALU.add, [("prev", g), ("aux1", 0)], [("prev", g)])
                    if not main:
                        continue
                    TTn("dve", sm["x6"][:].rearrange("p (h d) -> p h d", d=64), xs3, bc(dtf["dt"][:, q, hs], 2, [128, 4, 64]),
                        ALU.mult, [smp + "xs", "dt_dt"], [smp + "x6"])
                    TTn("dve", sm["xhi"][:], bc(adt_hi[:, q, hs], 2, [128, 4, 128]), bc(trib[:], 1, [128, 4, 128]), ALU.mult,
                        ["adt_hi", "trib"], [smp + "xhi"])
                    TTn("dve", sm["xlo"][:], bc(adt_lo[:, q, hs], 2, [128, 4, 128]), bc(trib[:], 1, [128, 4, 128]), ALU.mult,
                        ["adt_lo", "trib"], [smp + "xlo"])
                    MM(ps_acsb, onesb[:], sm["xhi"][:].rearrange("p h l -> p (h l)"), True, False, ["onesb", smp + "xhi"], [("aux0", 0)])
                    MM(ps_acsb, onesb[:], sm["xlo"][:].rearrange("p h l -> p (h l)"), False, False, ["onesb", smp + "xlo"], [("aux0", 0)])
                    MM(ps_acsb, identb[:], negm[:].rearrange("p h l -> p (h l)"), False, True, ["identb", "negm"], [("aux0", 0)])
                    MM(ps_sc, BT[:, qs], CT[:, qs], True, True, [btk, ctk], [("aux1", 1)])
                    TTn("dve", sm["t"][:], ps_acsb.rearrange("p (h l) -> p h l", l=128), bc(dtf["acs"][:, q, hs], 2, [128, 4, 128]),
                        ALU.subtract, [("aux0", 0), "dt_acs"], [smp + "t"])
                    ACT(sm["t"][:], sm["t"][:], AF.Exp, [smp + "t"], [smp + "t"])
                    TTn("dve", sm["mt"][:], sm["t"][:], bc(ps_sc, 1, [128, 4, 128]), ALU.mult, [smp + "t", ("aux1", 1)], [smp + "mt"])
                    for h in range(4):
                        MM(ps_y[:, h * 64:(h + 1) * 64], sm["mt"][:, h, :], sm["x6"][:, h * 64:(h + 1) * 64], True, True,
                           [smp + "mt", smp + "x6"], [("aux0", 1)])
                    MM(ps_yo, CT[:, qs], prevb[:, q, :], True, True, [ctk, ("prevb", q)], [("aux1", 0)])
                    y3 = sm["y"][:].rearrange("p (h d) -> p h d", d=64)
                    tmp = sm["t"][:].rearrange("p h l -> p (h l)")[:, 0:256]
                    TTn("dve", tmp.rearrange("p (h d) -> p h d", d=64), ps_yo.rearrange("p (h d) -> p h d", d=64), bc(dtf["ea"][:, q, hs], 2, [128, 4, 64]), ALU.mult,
                        [("aux1", 0), "dt_ea", smp + "t"], [smp + "t"])
                    TTn("dve", sm["y"][:], sm["y"][:], ps_y, ALU.add, [smp + "y", ("aux0", 1)], [smp + "y"])
                    TTn("dve", sm["y"][:], sm["y"][:], tmp, ALU.add, [smp + "y", smp + "t"], [smp + "y"])
                    for i in range(2):
                        TR(ps_yT[:, i * 128:(i + 1) * 128], sm["y"][:, i * 128:(i + 1) * 128], identf[:], [smp + "y", "identf"], [("aux1", 1)])
                    for i in range(2):
                        CP("act", yT[i][:, qs], ps_yT[:, i * 128:(i + 1) * 128], [("aux1", 1)], ["f4_%d" % (6 + i)])
                stop(11)
                if not main:
                    continue
                gt, gtok = nextbig()
                for i in range(2):
                    TTn("dve", yT[i][:, 0:TT], yT[i][:, 0:TT], zs[i][:, 0:TT], ALU.mult, ["f4_%d" % (6 + i), "f4_%d" % (4 + i)], ["f4_%d" % (6 + i)])
                    ACT(sqg[:], yT[i][:, 0:TT], AF.Square, ["f4_%d" % (6 + i)], ["h2_2"])
                    for hf in range(NH):
                        MM(gt[:, hf * HW:(hf + 1) * HW], onesb[:], sqg[:, hf * HW:(hf + 1) * HW], i == 0, i == 1, ["onesb", "h2_2"], [gtok[hf * HW // 512]])
                rstd_from(gt[:, 0:TT], gtok, 256, f4[0][:, 0:TT], "f4_0")
                for i in range(2):
                    STT(vT[:, g * 2 + i, :], yT[i][:, 0:TT], vec["snorm"][:, g * 2 + i:g * 2 + i + 1], f4[0][:, 0:TT], ALU.mult, ALU.mult,
                        ["f4_%d" % (6 + i), "f4_0", "v_snorm"], [("vT", g * 2 + i)])
            stop(9)

        def phaseB(t0):
            stop(4)
            S.label = 'B1'
            vtoks = [("vT", k) for k in range(KM)]
            for cc in range(KD):
                bt, btok = nextbig()
                nkk = (KM + KD - 1) // KD
                wl = []
                for part in range(nkk):
                    k0 = part * KD
                    nk = min(KD, KM - k0)
                    wl.append((load_w_chunk(w_out[k0 * 128:(k0 + nk) * 128, cc * 128:(cc + 1) * 128], nk), k0, nk))
                for hf in range(NH):
                    for (wb, wtok), k0, nk in wl:
                        for k in range(nk):
                            MM(bt[:, hf * HW:(hf + 1) * HW], wb[:, k, :], vT[:, k0 + k, hf * HW:(hf + 1) * HW], k0 + k == 0, k0 + k == KM - 1,
                               [wtok] + vtoks, [btok[hf * HW // 512]])
                st, stok = f4[cc % 2], "f4_%d" % (cc % 2)
                sq, sqtok = h2[cc % 2], "h2_%d" % (cc % 2)
                CP("act", st[:, 0:TT], bt[:, 0:TT], btok, [stok])
                ACT(sq[:], bt[:, 0:TT], AF.Square, btok, [sqtok])
                for hf in range(NH):
                    MM(aux0[:, hf * HW:(hf + 1) * HW], onesb[:], sq[:, hf * HW:(hf + 1) * HW], cc == 0, cc == KD - 1, ["onesb", sqtok], [("aux0", hf * HW // 512)])
                DMA("sp", msT[cc], st[:, 0:TT], [stok], [("msT", cc)], stok + "s")
            r1 = f4[7]
            rstd_from(aux0[:, 0:TT], auxt("aux0"), D, r1[:, 0:TT], "f4_7")
            stop(5)
            S.label = 'B2'
            for cc in range(KD):
                ms, mstok = f4[cc % 2], "f4_%d" % (cc % 2)
                xb, xbtok = f4[2 + cc % 2], "f4_%d" % (2 + cc % 2)
                sq, sqtok = h2[cc % 2], "h2_%d" % (cc % 2)
                DMA("sp", ms[:, 0:TT], msT[cc], [("msT", cc)], [mstok], mstok)
                DMA("sp", xb[:, 0:TT].rearrange("p (j m) -> p j m", m=128),
                    xm[t0:t0 + TT, cc * 128:(cc + 1) * 128].rearrange("(j p) m -> p j m", p=128), [], [xbtok], xbtok)
                bt, btok = nextbig()
                for j in range(NQ):
                    TR(bt[:, j * 128:(j + 1) * 128], xb[:, j * 128:(j + 1) * 128], identf[:], [xbtok, "identf"], [btok[(j * 128) // 512]])
                STT(ms[:, 0:TT], ms[:, 0:TT], vec["g_post"][:, cc:cc + 1], r1[:, 0:TT], ALU.mult, ALU.mult, [mstok, "f4_7", "v_g_post"], [mstok])
                TTn("dve", ms[:, 0:TT], ms[:, 0:TT], bt[:, 0:TT], ALU.add, [mstok] + btok, [mstok])
                DMA("sp", x1s[cc], ms[:, 0:TT], [mstok], [("x1s", cc)], mstok + "s")
                ACT(sq[:], ms[:, 0:TT], AF.Square, [mstok], [sqtok])
                for hf in range(NH):
                    MM(aux1[:, hf * HW:(hf + 1) * HW], onesb[:], sq[:, hf * HW:(hf + 1) * HW], cc == 0, cc == KD - 1, ["onesb", sqtok], [("aux1", hf * HW // 512)])
            r2 = f4[6]
            rstd_from(aux1[:, 0:TT], auxt("aux1"), D, r2[:, 0:TT], "f4_6")
            stop(6)
            S.label = 'B3'
            for cc in range(KD):
                xb, xbtok = f4[cc % 2], "f4_%d" % (cc % 2)
                DMA("sp", xb[:, 0:TT], x1s[cc], [("x1s", cc)], [xbtok], xbtok)
                STT(hT[:, cc, :], xb[:, 0:TT], vec["g_mlp"][:, cc:cc + 1], r2[:, 0:TT], ALU.mult, ALU.mult, [xbtok, "f4_6", "v_g_mlp"],
                    [("hT", hf) for hf in range(NH)])
            stop(7)
            S.label = 'MLP'
            KB = c.FC // 128
            htoks = [("hT", hf) for hf in range(NH)]
            for fb in range(c.DFF // c.FC):
                for f in range(KB):
                    wb, wtok = load_w_chunk(w1[:, fb * c.FC + f * 128:fb * c.FC + (f + 1) * 128])
                    bt, btok = nextbig()
                    for hf in range(NH):
                        for k in range(KD):
                            MM(bt[:, hf * HW:(hf + 1) * HW], wb[:, k, :], hT[:, k, hf * HW:(hf + 1) * HW], k == 0, k == KD - 1, [wtok] + htoks, [btok[hf * HW // 512]])
                    rl, rltok = f4[2 + f % 2], "f4_%d" % (2 + f % 2)
                    ACT(rl[:, 0:TT], bt[:, 0:TT], AF.Relu, btok, [rltok])
                    TTn("dve", actT[:, f, :], rl[:, 0:TT], rl[:, 0:TT], ALU.mult, [rltok], [("act", f), "ovl", "ovlb"] + ALT)
                for cc in range(KD):
                    wb, wtok = load_w_chunk(w2[fb * c.FC:(fb + 1) * c.FC, cc * 128:(cc + 1) * 128], KB)
                    bt, btok = nextbig()
                    for hf in range(NH):
                        for k in range(KB):
                            MM(bt[:, hf * HW:(hf + 1) * HW], wb[:, k, :], actT[:, k, hf * HW:(hf + 1) * HW], k == 0, k == KB - 1,
                               [wtok, ("act", k)], [btok[hf * HW // 512]])
                    if fb == 0:
                        CP("act", accT[:, cc, :], bt[:, 0:TT], btok + vtoks, [("acc", cc)] + vtoks)
                    else:
                        TTn("dve", accT[:, cc, :], accT[:, cc, :], bt[:, 0:TT], ALU.add, btok + [("acc", cc)], [("acc", cc)])
            stop(8)
            S.label = 'B4'
            for cc in range(KD):
                sq, sqtok = h2[cc % 2], "h2_%d" % (cc % 2)
                ACT(sq[:], accT[:, cc, :], AF.Square, [("acc", cc)] + vtoks, [sqtok])
                for hf in range(NH):
                    MM(aux0[:, hf * HW:(hf + 1) * HW], onesb[:], sq[:, hf * HW:(hf + 1) * HW], cc == 0, cc == KD - 1, ["onesb", sqtok], [("aux0", hf * HW // 512)])
            r3 = f4[7]
            rstd_from(aux0[:, 0:TT], auxt("aux0"), D, r3[:, 0:TT], "f4_7")
            for cc in range(KD):
                xb, xbtok = f4[cc % 2], "f4_%d" % (cc % 2)
                ob, obtok = f4[2 + cc % 2], "f4_%d" % (2 + cc % 2)
                os_, ostok = f4[4 + cc % 2], "f4_%d" % (4 + cc % 2)
                DMA("sp", xb[:, 0:TT], x1s[cc], [("x1s", cc)], [xbtok], xbtok)
                STT(ob[:, 0:TT], accT[:, cc, :], vec["g_pmlp"][:, cc:cc + 1], r3[:, 0:TT], ALU.mult, ALU.mult, [("acc", cc), "f4_7", "v_g_pmlp"] + vtoks, [obtok])
                TTn("dve", ob[:, 0:TT], ob[:, 0:TT], xb[:, 0:TT], ALU.add, [obtok, xbtok], [obtok])
                bt, btok = nextbig()
                for j in range(NQ):
                    TR(bt[:, j * 128:(j + 1) * 128], ob[:, j * 128:(j + 1) * 128], identf[:], [obtok, "identf"], [btok[(j * 128) // 512]])
                CP("act", os_[:, 0:TT], bt[:, 0:TT], btok, [ostok])
                DMA("sp", out[t0:t0 + TT, cc * 128:(cc + 1) * 128].rearrange("(j p) m -> p j m", p=128),
                    os_[:, 0:TT].rearrange("p (j m) -> p j m", m=128), [ostok], [("out", t0, cc)], ostok + "o")
                S.out_tokens.append(("out", t0, cc))

        def _program():
            for ti in range(c.NPRE):
                phaseA(xp, ti * TT, False, last_prefix=(ti == c.NPRE - 1))
            alltok = [("prev", g) for g in range(G)]
            TS("dve", prev[:], prev[:], flag[:, 0:1], None, ALU.mult, None, alltok + ["flag"], alltok)
            TS("dve", hcar[:], hcar[:], flag[:, 0:1], None, ALU.mult, None, ["hcar", "flag"], ["hcar"])
            for ti in range(c.NMAIN):
                phaseA(xm, ti * TT, True)
                phaseB(ti * TT)

        S.out_tokens = []

        class _Stop(Exception):
            pass

        def stop(level):
            if getattr(cfg, "STOP", 0) == level:
                raise _Stop()

        try:
            _program()
        except _Stop:
            pass
        S.op("dve", lambda e: e.engine_nop(), S.out_tokens, [])
        S.emit(reorder=getattr(cfg, "REORDER", True))
        nc._sched = S
    return nc


def _vec_layouts(cfg, p):
    c = cfg
    pm = lambda v, n: np.ascontiguousarray(np.asarray(v, np.float32).reshape(n, 128).T)
    bcast = lambda v: np.ascontiguousarray(np.broadcast_to(np.asarray(v, np.float32)[None, :], (128, len(v))))
    cw = lambda w, n: np.ascontiguousarray(np.asarray(w, np.float32).reshape(4, n, 128).transpose(2, 1, 0).reshape(128, n * 4))
    return dict(
        g_pre=pm(p["pre_mix_norm"], c.KD), g_post=pm(p["post_mix_norm"], c.KD), g_mlp=pm(p["pre_mlp_norm"], c.KD),
        g_pmlp=pm(p["post_mlp_norm"], c.KD),
        cws=cw(p["ssd_conv_w"], c.KX), cbs=pm(p["ssd_conv_b"], c.KX), cwl=cw(p["lru_conv_w"], c.NBLK), cbl=pm(p["lru_conv_b"], c.NBLK),
        lba=pm(p["lru_b_a"], c.NBLK), lbx=pm(p["lru_b_x"], c.NBLK), lam=pm(p["lru_lambda"], c.NBLK), lnorm=pm(p["lru_norm"], c.NBLK),
        snorm=pm(p["ssd_norm"], c.KS), dtb=bcast(p["ssd_dt_bias"]), alog=bcast(p["ssd_a_log"]), dsk=bcast(p["ssd_d"]))


def make_in_maps(cfg, inputs, n_batch, n_half):
    c = cfg
    p = {k: np.asarray(v)[0] for k, v in inputs.items() if k != "x"}
    x = np.asarray(inputs["x"], np.float32)
    vl = _vec_layouts(c, p)
    half = c.NMAIN * c.TT
    shared = dict(w_in=np.ascontiguousarray(p["w_in"], np.float32), w_out=np.ascontiguousarray(p["w_out"], np.float32),
                  w1=np.ascontiguousarray(p["w_mlp_in"], np.float32), w2=np.ascontiguousarray(p["w_mlp_out"], np.float32),
                  lw=np.ascontiguousarray(np.concatenate([np.asarray(p["lru_w_a"], np.float32), np.asarray(p["lru_w_x"], np.float32)], axis=2)), **vl)
    maps = []
    for b in range(n_batch):
        for s in range(n_half):
            m = dict(shared)
            m["xm"] = np.ascontiguousarray(x[b, s * half:(s + 1) * half])
            pre = np.zeros((max(c.NPRE, 1) * c.TT, c.D), np.float32)
            if s > 0:
                pre[:] = x[b, (s - 1) * half:s * half][-pre.shape[0]:]
            m["xp"] = pre
            m["flag"] = np.full((128, 1), 1.0 if s > 0 else 0.0, np.float32)
            maps.append(m)
    return maps


def kernel(**inputs):
    cfg = Cfg()
    nc = build(cfg)
    maps = make_in_maps(cfg, inputs, 4, 2)
    res = run_bass_kernel_spmd(nc, maps, core_ids=list(range(8)))
    x = np.asarray(inputs["x"])
    outp = np.empty(x.shape, np.float32)
    half = cfg.NMAIN * cfg.TT
    i = 0
    for b in range(4):
        for s in range(2):
            outp[b, s * half:(s + 1) * half] = np.asarray(res.results[i]["out"], np.float32)
            i += 1
    return outp
```

```python
import contextlib
import numpy as np
import concourse.bass as bass
import concourse.mybir as mybir
from concourse.bass_utils import run_bass_kernel_spmd

F32 = mybir.dt.float32
BF16 = mybir.dt.bfloat16
AF = mybir.ActivationFunctionType
ALU = mybir.AluOpType
EPS = 1e-6


class Cfg:
    def __init__(self, D=2048, H=32, G=8, DL=2048, DFF=8192, TT=1024, NMAIN=2, NPRE=2, FC=1024):
        self.D, self.H, self.G, self.DL, self.DFF = D, H, G, DL, DFF
        self.TT, self.NMAIN, self.NPRE, self.FC = TT, NMAIN, NPRE, FC
        self.P, self.N = 64, 128
        self.DS = H * 64
        self.HG = H // G
        assert self.HG == 4
        self.KD = D // 128
        self.KS = self.DS // 128
        self.DXBC = self.DS + 2 * G * 128
        self.KX = self.DXBC // 128
        self.DIN = self.DS + self.DXBC + H + 2 * DL
        self.NBLK = DL // 128
        self.DMIX = self.DS + DL
        self.KM = self.DMIX // 128
        self.KF = DFF // 128
        self.NQ = TT // 128
        self.HW = min(512, TT)
        self.NH = TT // self.HW
        self.oxs = self.DS
        self.oB = 2 * self.DS
        self.oC = 2 * self.DS + G * 128
        self.odt = self.DS + self.DXBC
        self.ogate = self.odt + H
        self.oxl = self.ogate + DL


TUNE = dict(win=256, poolx=1.0, lat=250.0, dmaiss=1500.0, actx=1.2, dvex=1.3, pex=1.1, bw=250.0)


class Sched:
    ENGS = ("pe", "act", "dve", "pool", "sp")

    def __init__(self, nc):
        self.nc = nc
        self.ops = []
        self.last_w = {}
        self.readers = {}
        self.dma_cnt = {}

    def op(self, eng, fn, reads=(), writes=(), dma_key=None, cost=300.0, nbytes=0):
        idx = len(self.ops)
        deps = set()
        for t in reads:
            w = self.last_w.get(t)
            if w is not None:
                deps.add(w)
        for t in writes:
            w = self.last_w.get(t)
            if w is not None:
                deps.add(w)
            deps.update(self.readers.get(t, ()))
        deps.discard(idx)
        o = dict(eng=eng, fn=fn, deps=deps, dma=dma_key is not None, key=dma_key, ms=False, cost=cost, nbytes=nbytes, label=getattr(self, 'label', ''))
        if dma_key is not None:
            n = self.dma_cnt.get(dma_key, 0) + 1
            self.dma_cnt[dma_key] = n
            o["dval"] = 16 * n
        self.ops.append(o)
        for t in reads:
            self.readers.setdefault(t, []).append(idx)
        for t in writes:
            self.last_w[t] = idx
            self.readers[t] = []
        return idx

    def schedule(self, window=None):
        ops = self.ops
        window = window or TUNE["win"]
        mult = dict(pool=TUNE["poolx"], act=TUNE["actx"], dve=TUNE["dvex"], pe=TUNE["pex"], sp=1.0)
        pend = {e: [] for e in self.ENGS}
        for i, o in enumerate(ops):
            pend[o["eng"]].append(i)
        head = {e: 0 for e in self.ENGS}
        tfree = {e: 0.0 for e in self.ENGS}
        done = [None] * len(ops)
        sched = [False] * len(ops)
        order = {e: [] for e in self.ENGS}
        pipe_free = 0.0
        tf_prev = {}
        remaining = len(ops)
        while remaining:
            progressed = False
            for e in sorted(self.ENGS, key=lambda x: tfree[x]):
                lst = pend[e]
                while head[e] < len(lst) and sched[lst[head[e]]]:
                    head[e] += 1
                if head[e] >= len(lst):
                    continue
                best, best_t = None, None
                seen = 0
                j = head[e]
                while j < len(lst) and seen < window:
                    i = lst[j]
                    j += 1
                    if sched[i]:
                        continue
                    seen += 1
                    t = tfree[e]
                    ok = True
                    for d in ops[i]["deps"]:
                        dt_ = done[d]
                        if dt_ is None:
                            ok = False
                            break
                        if dt_ > t:
                            t = dt_
                    if ok and (best is None or t < best_t - 1e-9):
                        best, best_t = i, t
                        if t <= tfree[e] + 1e-9:
                            break
                if best is None:
                    continue
                o = ops[best]
                if o["dma"]:
                    issue = TUNE["dmaiss"] if e == "pool" else 120.0
                    xs_ = max(best_t + issue, pipe_free)
                    pipe_free = xs_ + o["nbytes"] / TUNE["bw"]
                    done[best] = pipe_free + 2000.0
                    tfree[e] = best_t + issue
                else:
                    tfree[e] = best_t + o["cost"] * mult[e]
                    tf_prev[best] = tfree[e]
                    done[best] = tfree[e] + (TUNE["lat"] if e != "pe" else 0.0)
                sched[best] = True
                o['t0'] = best_t
                o['crit'] = max(list(o['deps']) + ([order[e][-1]] if order[e] else []), key=lambda d_: (done[d_] if (ops[d_]['eng'] != e or ops[d_]['dma']) else tf_prev.get(d_, done[d_])), default=None)
                o['t1'] = done[best]
                order[e].append(best)
                remaining -= 1
                progressed = True
                break
            assert progressed, "list scheduler stuck"
        self.makespan = max(d for d in done if d is not None)
        return order

    def emit(self, reorder=True):
        nc, ops = self.nc, self.ops
        if reorder:
            order = self.schedule()
        else:
            order = {e: [i for i, o in enumerate(ops) if o["eng"] == e] for e in self.ENGS}
        pos = {}
        for e in self.ENGS:
            for k, i in enumerate(order[e]):
                pos[i] = k
        for i, o in enumerate(ops):
            latest = {}
            eff = []
            for d in o["deps"]:
                p = ops[d]
                if p["dma"]:
                    eff.append(d)
                    continue
                if p["eng"] == "pe" and o["eng"] == "pe" and not o["dma"]:
                    assert pos[d] < pos[i]
                    continue
                if p["eng"] == o["eng"]:
                    assert pos[d] < pos[i]
                if p["eng"] not in latest or pos[d] > pos[latest[p["eng"]]]:
                    latest[p["eng"]] = d
            o["eff"] = eff + list(latest.values())
            for d in latest.values():
                ops[d]["ms"] = True
        KSEM = 8
        cnt = {e: 0 for e in self.ENGS}
        for e in self.ENGS:
            for i in order[e]:
                o = ops[i]
                if o["ms"]:
                    o["msi"] = cnt[e]
                    cnt[e] += 1
        self.ms_counts = cnt
        with contextlib.ExitStack() as es:
            esem = {e: [es.enter_context(nc.semaphore("S_%s%d" % (e, i))) for i in range(KSEM)] for e in self.ENGS if cnt[e] > 0}
            dsem = {}
            for i, (k, n) in enumerate(self.dma_cnt.items()):
                r = 1 if k in ("setup", "setup_p") else max(1, (n * 16 + 479) // 480)
                dsem[k] = [es.enter_context(nc.semaphore("D_%d_%d" % (i, j))) for j in range(r)]
            block = es.enter_context(nc.Block())

            def semval(p):
                if p["dma"]:
                    lst = dsem[p["key"]]
                    if p["key"] in ("setup", "setup_p"):
                        return lst[0], 16 * self.dma_cnt[p["key"]]
                    n = p["dval"] // 16 - 1
                    return lst[n % len(lst)], 16 * (n // len(lst) + 1)
                i = p["msi"]
                return esem[p["eng"]][i % KSEM], i // KSEM + 1

            def run(engname, eng):
                waited = {}
                for oi in order[engname]:
                    o = ops[oi]
                    need = {}
                    for d in o["eff"]:
                        p = ops[d]
                        s, v = semval(p)
                        if need.get(id(s), (None, 0))[1] < v:
                            need[id(s)] = (s, v)
                    for s, v in need.values():
                        if waited.get(id(s), 0) < v:
                            eng.wait_ge(s, v)
                            waited[id(s)] = v
                    ins = o["fn"](eng)
                    if o["dma"]:
                        ins.then_inc(semval(o)[0], 16)
                    elif o["ms"]:
                        ins.then_inc(semval(o)[0], 1)

            block.tensor(lambda e: run("pe", e))
            block.scalar(lambda e: run("act", e))
            block.vector(lambda e: run("dve", e))
            block.gpsimd(lambda e: run("pool", e))
            block.sync(lambda e: run("sp", e))


def build(cfg):
    c = cfg
    D, H, G, TT, KD, KS, KX, KM, NQ, NH, HW, NBLK = c.D, c.H, c.G, c.TT, c.KD, c.KS, c.KX, c.KM, c.NQ, c.NH, c.HW, c.NBLK
    nc = bass.Bass("TRN2", target_bir_lowering=False)
    din = lambda name, shape: nc.dram_tensor(name, list(shape), F32, kind="ExternalInput").ap()
    xm = din("xm", [c.NMAIN * TT, D])
    xp = din("xp", [max(c.NPRE, 1) * TT, D])
    flag_d = din("flag", [128, 1])
    w_in = din("w_in", [D, c.DIN])
    w_out = din("w_out", [c.DMIX, D])
    w1 = din("w1", [D, c.DFF])
    w2 = din("w2", [c.DFF, D])
    lw_d = din("lw", [NBLK, 128, 256])
    vec_d = {}
    vec_shapes = dict(g_pre=[128, KD], g_post=[128, KD], g_mlp=[128, KD], g_pmlp=[128, KD],
                      cws=[128, KX * 4], cbs=[128, KX], cwl=[128, NBLK * 4], cbl=[128, NBLK],
                      lba=[128, NBLK], lbx=[128, NBLK], lam=[128, NBLK], lnorm=[128, NBLK],
                      snorm=[128, KS], dtb=[128, H], alog=[128, H], dsk=[128, H])
    for k, shp in vec_shapes.items():
        vec_d[k] = din(k, shp)
    out = nc.dram_tensor("out", [c.NMAIN * TT, D], F32, kind="ExternalOutput").ap()
    msT = nc.dram_tensor("msT", [KD, 128, TT], F32).ap()
    x1s = nc.dram_tensor("x1s", [KD, 128, TT], F32).ap()

    es = contextlib.ExitStack()
    with es:
        sb = lambda name, shape, dt: es.enter_context(nc.sbuf_tensor(name, list(shape), dt))
        S = Sched(nc)
        if getattr(cfg, "PAD", 0):
            sb("pad", [128, cfg.PAD // 4], F32)
        def ACT(out_, in_, func, r, w, bias=None, scale=None, accum=None):
            kw = {}
            if bias is not None:
                kw["bias"] = bias
            if scale is not None:
                kw["scale"] = scale
            if accum is not None:
                kw["accum_out"] = accum
            S.op("act", lambda e: e.activation(out=out_, in_=in_, func=func, **kw), r, w, cost=(224.0 + in_.free_size()) / 1.2 + (90.0 if accum is not None else 0.0))

        def TTn(eng, out_, in0, in1, op, r, w):
            S.op(eng, lambda e: e.tensor_tensor(out=out_, in0=in0, in1=in1, op=op), r, w, cost=(130.0 + out_.free_size()) / 0.96)

        def TS(eng, out_, in0, s1, s2, op0, op1, r, w):
            if s2 is None:
                S.op(eng, lambda e: e.tensor_scalar(out=out_, in0=in0, scalar1=s1, scalar2=None, op0=op0), r, w, cost=(130.0 + out_.free_size()) / 0.96)
            else:
                S.op(eng, lambda e: e.tensor_scalar(out=out_, in0=in0, scalar1=s1, scalar2=s2, op0=op0, op1=op1), r, w, cost=(130.0 + out_.free_size()) / 0.96)

        def STT(out_, in0, scalar, in1, op0, op1, r, w):
            S.op("dve", lambda e: e.scalar_tensor_tensor(out=out_, in0=in0, scalar=scalar, in1=in1, op0=op0, op1=op1), r, w, cost=(130.0 + out_.free_size()) / 0.96)

        def CP(eng, out_, in_, r, w):
            if eng == "act":
                S.op("act", lambda e: e.activation(out=out_, in_=in_, func=AF.Copy), r, w, cost=(224.0 + in_.free_size()) / 1.2)
            else:
                S.op(eng, lambda e: e.tensor_copy(out=out_, in_=in_), r, w, cost=(130.0 + out_.free_size()) / 0.96)

        def MM(out_, lhsT, rhs, start, stop, r, w):
            S.op("pe", lambda e: e.matmul(out_, lhsT=lhsT, rhs=rhs, start=start, stop=stop), r, w, cost=max(64.0, rhs.free_size()) / 2.4 + 12.0)

        def TR(out_, in_, ident, r, w):
            S.op("pe", lambda e: e.transpose(out_, in_, ident), r, w, cost=110.0)

        def DMA(q, out_, in_, r, w, key):
            S.op(q, lambda e: e.dma_start(out=out_, in_=in_), r, w, dma_key=key, nbytes=max(out_.nbytes(), in_.nbytes()))

        def bc(ap, axis, shape):
            return ap.unsqueeze(axis).to_broadcast(list(shape))

        identb = sb("identb", [128, 128], BF16)
        identf = sb("identf", [128, 128], F32)
        onesb = sb("onesb", [128, 128], BF16)
        trib = sb("trib", [128, 128], BF16)
        negm = sb("negm", [128, 4, 128], BF16)
        vec = {k: sb("v_" + k, shp, F32) for k, shp in vec_shapes.items()}
        flag = sb("flag_sb", [128, 1], F32)
        clru = sb("clru", [128, NBLK], F32)
        a_bc = sb("a_bc", [128, H], F32)
        wdt = sb("wdt", [128, KD, H], BF16)
        lwb = sb("lwb", [128, 2, 256], BF16)
        prev = sb("prev", [128, G * 256], F32)
        prevb = sb("prevb", [128, NQ, 256], BF16)
        hcar = sb("hcar", [128, NBLK], F32)
        car_s = sb("car_s", [128, KX, 3], F32)
        car_l = sb("car_l", [128, NBLK, 3], F32)
        QH = NQ * H
        dtf = {k: sb("dt_" + k, [128, NQ, H], F32) for k in ("dt", "adt", "acs", "ea", "cdb", "dtde", "t0", "t1", "t2")}
        adt_hi = sb("adt_hi", [128, NQ, H], BF16)
        adt_lo = sb("adt_lo", [128, NQ, H], BF16)
        ssq_x = sb("ssq_x", [128, NQ], F32)
        RAK = max(KD, (KM + 1) // 2)
        RA = sb("RA", [128, RAK * TT], F32)
        vT = RA[:].bitcast(BF16).rearrange("p (k t) -> p k t", t=TT)
        accT = RA[:].rearrange("p (k t) -> p k t", t=TT)
        hT = sb("hT", [128, KD, TT], BF16)
        OVW = max(c.FC // 128 * TT // 2, D + D // 2)
        ovl = sb("ovl", [128, OVW], F32)
        xt = ovl[:, 0:D]
        xn = ovl[:, D:D + D // 2].bitcast(BF16)
        actT = ovl[:, 0:c.FC // 128 * TT // 2].bitcast(BF16).rearrange("p (k t) -> p k t", t=TT)
        ALT = []
        if getattr(cfg, "ALTSSD", False) and OVW >= 3 * (TT + 4):
            ALT = ["alt_xs0", "alt_xs1", "alt_b", "alt_c"]
            alt_xs = (ovl[:, 0:TT + 4], ovl[:, TT + 4:2 * (TT + 4)])
            alt_b = ovl[:, 2 * (TT + 4):2 * (TT + 4) + TT // 2].bitcast(BF16)
            alt_c = ovl[:, 2 * (TT + 4) + TT // 2:2 * (TT + 4) + TT].bitcast(BF16)
        NW = 3
        wbuf = [sb("wbuf%d" % i, [128, max(KD, c.FC // 128), 128], BF16) for i in range(NW)]
        NF = 8
        f4 = [sb("f4_%d" % i, [128, TT + 4], F32) for i in range(NF)]
        h2 = [sb("h2_%d" % i, [128, TT], BF16) for i in range(3)]
        NSM = 2
        smsets = [{k: sb("sm%d_%s" % (n_, k), shp, dt) for k, (shp, dt) in dict(
            xs=([128, 256], F32), x6=([128, 256], BF16), x6e=([128, 256], BF16), bt=([128, 128], BF16),
            xhi=([128, 4, 128], BF16), xlo=([128, 4, 128], BF16), t=([128, 4, 128], F32),
            mt=([128, 4, 128], BF16), y=([128, 256], F32)).items()} for n_ in range(NSM)]
        ps = lambda name: es.enter_context(nc.psum_tensor(name, [128, 1024], F32))
        big = [ps("big0"), ps("big1")]
        aux0, aux1 = ps("aux0"), ps("aux1")
        bigi = [0]

        def nextbig():
            i = bigi[0] % 2
            bigi[0] += 1
            return big[i], [("big%d" % i, b) for b in range((TT + 511) // 512)]

        wi = [0]

        def nextw():
            i = wi[0] % NW
            wi[0] += 1
            return wbuf[i], "wbuf%d" % i

        for k in vec_shapes:
            DMA("sp", vec[k][:], vec_d[k][:], [], ["v_" + k], "setup")
        DMA("sp", flag[:], flag_d[:], [], ["flag"], "setup")
        DMA("pool", wdt[:], w_in[:, c.odt:c.odt + H].rearrange("(k p) h -> p k h", p=128), [], ["wdt"], "setup_p")

        def mask_const(t, tok, fill, pat, cm, op):
            S.op("pool", lambda e: e.affine_select(out=t[:], in_=t[:], pattern=pat, compare_op=op, fill=fill,
                                                   base=0, channel_multiplier=cm), [tok], [tok])

        S.op("pool", lambda e: e.memset(identb[:], 1.0), [], ["identb"])
        mask_const(identb, "identb", 0.0, [[-1, 128]], 1, ALU.is_equal)
        S.op("pool", lambda e: e.memset(identf[:], 1.0), [], ["identf"])
        mask_const(identf, "identf", 0.0, [[-1, 128]], 1, ALU.is_equal)
        S.op("pool", lambda e: e.memset(onesb[:], 1.0), [], ["onesb"])
        S.op("pool", lambda e: e.memset(trib[:], 1.0), [], ["trib"])
        mask_const(trib, "trib", 0.0, [[1, 128]], -1, ALU.is_ge)
        S.op("pool", lambda e: e.memset(negm[:], 0.0), [], ["negm"])
        mask_const(negm, "negm", -30000.0, [[0, 4], [1, 128]], -1, ALU.is_ge)
        S.op("pool", lambda e: e.memset(prev[:], 0.0), [], ["prev"])
        S.op("pool", lambda e: e.memset(hcar[:], 0.0), [], ["hcar"])
        S.op("pool", lambda e: e.memset(car_s[:], 0.0), [], ["car_s"])
        S.op("pool", lambda e: e.memset(car_l[:], 0.0), [], ["car_l"])

        def log1p_small(out_, e_, tmpw, tmpq, tok_o, tok_e, tok_w, tok_q):
            TS("dve", tmpw, e_, 2.0, None, ALU.add, None, [tok_e], [tok_w])
            S.op("dve", lambda e: e.reciprocal(out=tmpw, in_=tmpw), [tok_w], [tok_w])
            TTn("dve", tmpw, tmpw, e_, ALU.mult, [tok_w, tok_e], [tok_w])
            TTn("dve", out_, tmpw, tmpw, ALU.mult, [tok_w], [tok_o])
            TS("dve", tmpq, out_, 1.0 / 11.0, None, ALU.mult, None, [tok_o], [tok_q])
            for cst in (1.0 / 9.0, 1.0 / 7.0, 1.0 / 5.0, 1.0 / 3.0):
                STT(tmpq, tmpq, cst, out_, ALU.add, ALU.mult, [tok_q, tok_o], [tok_q])
            TS("dve", tmpq, tmpq, 1.0, None, ALU.add, None, [tok_q], [tok_q])
            TTn("dve", tmpq, tmpq, tmpw, ALU.mult, [tok_q, tok_w], [tok_q])
            TS("dve", out_, tmpq, 2.0, None, ALU.mult, None, [tok_q], [tok_o])

        def softplus(out_, x_, ta, tb, tcc, tok_o, tok_x, tok_a, tok_b, tok_c):
            TS("dve", ta, x_, -1.0, None, ALU.mult, None, [tok_x], [tok_a])
            TTn("dve", ta, ta, x_, ALU.max, [tok_a, tok_x], [tok_a])
            ACT(ta, ta, AF.Exp, [tok_a], [tok_a], scale=-1.0)
            log1p_small(tb, ta, tcc, out_, tok_b, tok_a, tok_c, tok_o)
            TS("dve", ta, x_, 0.0, None, ALU.max, None, [tok_x], [tok_a])
            TTn("dve", out_, ta, tb, ALU.add, [tok_a, tok_b], [tok_o])

        t0s = dtf["t0"][:, 0, 0:NBLK] if NBLK <= H else None
        assert NBLK <= H
        t1s, t2s, t3s, t4s = (dtf[k][:, 0, 0:NBLK] for k in ("t1", "t2", "dt", "adt"))
        TS("dve", t0s, vec["lam"][:], -1.0, None, ALU.mult, None, ["v_lam"], ["dt_t0"])
        softplus(t1s, t0s, t2s, t3s, t4s, "dt_t1", "dt_t0", "dt_t2", "dt_dt", "dt_adt")
        TS("dve", clru[:], t1s, -8.0, None, ALU.mult, None, ["dt_t1"], ["clru"])
        ACT(a_bc[:], vec["alog"][:], AF.Exp, ["v_alog"], ["a_bc"])
        TS("dve", a_bc[:], a_bc[:], -1.0, None, ALU.mult, None, ["a_bc"], ["a_bc"])

        def load_w_chunk(src_ap, nk=None):
            wb, tok = nextw()
            nk = KD if nk is None else nk
            DMA("pool", wb[:, 0:nk, :], src_ap.rearrange("(k p) m -> p k m", p=128), [], [tok], tok)
            return wb, tok

        def proj_chunk(col0):
            wb, wtok = load_w_chunk(w_in[:, col0:col0 + 128])
            bt, btok = nextbig()
            for hf in range(NH):
                for k in range(KD):
                    MM(bt[:, hf * HW:(hf + 1) * HW], wb[:, k, :], hT[:, k, hf * HW:(hf + 1) * HW], k == 0, k == KD - 1,
                       [wtok, ("hT", hf)], [btok[hf * HW // 512]])
            return bt, btok

        def conv_chunk(bt, btok, car, cidx, cwn, cbn, xpad, xptok, acc, acctok, car_tok):
            cw, cb = vec[cwn], vec[cbn]
            CP("pool", xpad[:, 0:3], car[:, cidx, :], [car_tok], [xptok])
            CP("act", xpad[:, 3:3 + TT], bt[:, 0:TT], btok + [xptok], [xptok])
            CP("pool", car[:, cidx, :], xpad[:, TT:TT + 3], [xptok], [car_tok])
            ACT(acc[:, 0:TT], xpad[:, 0:TT], AF.Identity, [xptok, "v_" + cwn, "v_" + cbn], [acctok],
                bias=cb[:, cidx:cidx + 1], scale=cw[:, 4 * cidx:4 * cidx + 1])
            for k in (1, 2, 3):
                STT(acc[:, 0:TT], xpad[:, k:k + TT], cw[:, 4 * cidx + k:4 * cidx + k + 1], acc[:, 0:TT], ALU.mult, ALU.add,
                    [xptok, acctok, "v_" + cwn], [acctok])

        def rstd_from(ps_ap, pstok, n, dst, dtok):
            ACT(dst, ps_ap, AF.Sqrt, pstok + ["eps"], [dtok], bias=eps_t[:, 0:1], scale=1.0 / n)
            S.op("dve", lambda e: e.reciprocal(out=dst, in_=dst), [dtok], [dtok])

        auxt = lambda nm: [(nm, b) for b in range((TT + 511) // 512)]
        eps_t = sb("eps_t", [128, 1], F32)
        S.op("pool", lambda e: e.memset(eps_t[:], EPS), [], ["eps"])

        acttoks = [("act", f) for f in range(c.FC // 128)]

        def phaseA(xsrc, t0, main, last_prefix=False):
            S.label = 'A0'
            for j in range(NQ):
                DMA("sp", xt, xsrc[t0 + j * 128:t0 + (j + 1) * 128, :], [], ["ovl"] + acttoks + ALT, "ovl")
                ACT(xn, xt, AF.Square, ["ovl"], ["ovlb", "ssq%d" % j] + ALT, accum=ssq_x[:, j:j + 1])
                ACT(ssq_x[:, j:j + 1], ssq_x[:, j:j + 1], AF.Sqrt, ["ssq%d" % j, "eps"], ["ssq%d" % j], bias=eps_t[:, 0:1], scale=1.0 / D)
                S.op("dve", lambda e, j=j: e.reciprocal(out=ssq_x[:, j:j + 1], in_=ssq_x[:, j:j + 1]), ["ssq%d" % j], ["ssq%d" % j])
                TS("dve", xn, xt, ssq_x[:, j:j + 1], None, ALU.mult, None, ["ovl", "ovlb", "ssq%d" % j], ["ovlb"])
                pt, ptok = nextbig()
                ptb = pt[:].bitcast(BF16)
                for k in range(KD):
                    TR(ptb[:, k * 128:(k + 1) * 128], xn[:, k * 128:(k + 1) * 128], identb[:], ["ovlb", "identb"], [ptok[(k * 64) // 512]])
                TTn("dve", hT[:, :, j * 128:(j + 1) * 128], ptb[:, 0:KD * 128].rearrange("p (k m) -> p k m", m=128),
                    bc(vec["g_pre"][:], 2, [128, KD, 128]), ALU.mult, ptok + ["v_g_pre"], [("hT", (j * 128) // HW)])
            stop(1)
            S.label = 'A1'
            a0v = aux0[:, 0:QH].rearrange("p (q h) -> p q h", h=H)
            a1v = aux0[:, 512:512 + QH].rearrange("p (q h) -> p q h", h=H)
            for q in range(NQ):
                for k in range(KD):
                    MM(a0v[:, q, :], hT[:, k, q * 128:(q + 1) * 128], wdt[:, k, :], k == 0, k == KD - 1,
                       [("hT", (q * 128) // HW), "wdt"], [("aux0", 0)])
            TTn("dve", dtf["t0"][:], a0v, bc(vec["dtb"][:], 1, [128, NQ, H]), ALU.add, [("aux0", 0), "v_dtb"], ["dt_t0"])
            softplus(dtf["dt"][:], dtf["t0"][:], dtf["t1"][:], dtf["t2"][:], dtf["adt"][:], "dt_dt", "dt_t0", "dt_t1", "dt_t2", "dt_adt")
            TTn("dve", dtf["adt"][:], dtf["dt"][:], bc(a_bc[:], 1, [128, NQ, H]), ALU.mult, ["dt_dt", "a_bc"], ["dt_adt"])
            CP("dve", adt_hi[:], dtf["adt"][:], ["dt_adt"], ["adt_hi"])
            TTn("dve", adt_lo[:], dtf["adt"][:], adt_hi[:], ALU.subtract, ["dt_adt", "adt_hi"], ["adt_lo"])
            for q in range(NQ):
                MM(a0v[:, q, :], trib[:], adt_hi[:, q, :], True, False, ["trib", "adt_hi"], [("aux0", 0)])
                MM(a0v[:, q, :], trib[:], adt_lo[:, q, :], False, True, ["trib", "adt_lo"], [("aux0", 0)])
                MM(a1v[:, q, :], onesb[:], adt_hi[:, q, :], True, False, ["onesb", "adt_hi"], [("aux0", 1)])
                MM(a1v[:, q, :], onesb[:], adt_lo[:, q, :], False, True, ["onesb", "adt_lo"], [("aux0", 1)])
            CP("act", dtf["acs"][:], a0v, [("aux0", 0)], ["dt_acs"])
            if main and not getattr(cfg, "NOEA", 0):
                ACT(dtf["ea"][:], dtf["acs"][:], AF.Exp, ["dt_acs"], ["dt_ea"])
            ACT(dtf["cdb"][:], a1v, AF.Exp, [("aux0", 1)], ["dt_cdb"])
            TTn("dve", dtf["t0"][:], a1v, dtf["acs"][:], ALU.subtract, [("aux0", 1), "dt_acs"], ["dt_t0"])
            ACT(dtf["t0"][:], dtf["t0"][:], AF.Exp, ["dt_t0"], ["dt_t0"])
            TTn("dve", dtf["dtde"][:], dtf["t0"][:], dtf["dt"][:], ALU.mult, ["dt_t0", "dt_dt"], ["dt_dtde"])

            stop(2)
            S.label = 'LRU' + ('m' if main else 'p')
            xpad, xl, rr, ii, a2, hl, ge = (f4[i] for i in range(7))
            xlb, sqb = h2[0], h2[1]
            for blk in range(NBLK):
                bt, btok = proj_chunk(c.oxl + blk * 128)
                conv_chunk(bt, btok, car_l, blk, "cwl", "cbl", xpad, "f4_0", xl, "f4_1", "car_l")
                CP("act", xlb[:], xl[:, 0:TT], ["f4_1"], ["h2_0"])
                lwtok = "lwb%d" % (blk % 2)
                DMA("pool", lwb[:, blk % 2, :], lw_d[blk], [], [lwtok], lwtok)
                for (wofs, bn, dst, dtok) in ((0, "lba", rr, "f4_2"), (128, "lbx", ii, "f4_3")):
                    gt, gtok = nextbig()
                    for hf in range(NH):
                        MM(gt[:, hf * HW:(hf + 1) * HW], lwb[:, blk % 2, wofs:wofs + 128], xlb[:, hf * HW:(hf + 1) * HW], True, True, [lwtok, "h2_0"], [gtok[hf * HW // 512]])
                    ACT(dst[:, 0:TT], gt[:, 0:TT], AF.Sigmoid, gtok + ["v_" + bn], [dtok], bias=vec[bn][:, blk:blk + 1])
                ACT(rr[:, 0:TT], rr[:, 0:TT], AF.Exp, ["f4_2", "clru"], ["f4_2"], scale=clru[:, blk:blk + 1])
                ACT(a2[:, 0:TT], rr[:, 0:TT], AF.Square, ["f4_2"], ["f4_4"])
                ACT(a2[:, 0:TT], a2[:, 0:TT], AF.Sqrt, ["f4_4"], ["f4_4"], bias=1.0, scale=-1.0)
                TTn("dve", ii[:, 0:TT], ii[:, 0:TT], xl[:, 0:TT], ALU.mult, ["f4_3", "f4_1"], ["f4_3"])
                TTn("dve", ii[:, 0:TT], ii[:, 0:TT], a2[:, 0:TT], ALU.mult, ["f4_3", "f4_4"], ["f4_3"])
                S.op("dve", lambda e, blk=blk: e.tensor_tensor_scan(out=hl[:, 0:TT], data0=rr[:, 0:TT], data1=ii[:, 0:TT],
                                                                     initial=hcar[:, blk:blk + 1], op0=ALU.mult, op1=ALU.add),
                     ["f4_2", "f4_3", "hcar"], ["f4_5"], cost=(130.0 + 2 * TT) / 0.96)
                CP("pool", hcar[:, blk:blk + 1], hl[:, TT - 1:TT], ["f4_5"], ["hcar"])
                if main:
                    gt, gtok = proj_chunk(c.ogate + blk * 128)
                    gx = f4[7]
                    CP("act", gx[:, 0:TT], gt[:, 0:TT], gtok, ["f4_7"])
                    TTn("dve", ge[:, 0:TT], gx[:, 0:TT], gx[:, 0:TT], ALU.mult, ["f4_7"], ["f4_6"])
                    TS("dve", ge[:, 0:TT], ge[:, 0:TT], 0.044715, 1.0, ALU.mult, ALU.add, ["f4_6"], ["f4_6"])
                    TTn("dve", ge[:, 0:TT], ge[:, 0:TT], gx[:, 0:TT], ALU.mult, ["f4_6", "f4_7"], ["f4_6"])
                    ACT(ge[:, 0:TT], ge[:, 0:TT], AF.Sigmoid, ["f4_6"], ["f4_6"], scale=1.5957691216057308)
                    TTn("dve", ge[:, 0:TT], ge[:, 0:TT], gx[:, 0:TT], ALU.mult, ["f4_6", "f4_7"], ["f4_6"])
                    TTn("dve", ge[:, 0:TT], ge[:, 0:TT], hl[:, 0:TT], ALU.mult, ["f4_6", "f4_5"], ["f4_6"])
                    ACT(sqb[:], ge[:, 0:TT], AF.Square, ["f4_6"], ["h2_1"])
                    CP("act", vT[:, KS + blk, :], ge[:, 0:TT], ["f4_6"], [("vT", KS + blk)])
                    for hf in range(NH):
                        MM(aux1[:, hf * HW:(hf + 1) * HW], onesb[:], sqb[:, hf * HW:(hf + 1) * HW], blk == 0, blk == NBLK - 1,
                           ["onesb", "h2_1"], [("aux1", hf * HW // 512)])
            if main:
                rl = f4[7]
                rstd_from(aux1[:, 0:TT], auxt("aux1"), c.DL, rl[:, 0:TT], "f4_7")
                for blk in range(NBLK):
                    STT(vT[:, KS + blk, :], vT[:, KS + blk, :], vec["lnorm"][:, blk:blk + 1], rl[:, 0:TT], ALU.mult, ALU.mult,
                        [("vT", KS + blk), "f4_7", "v_lnorm"], [("vT", KS + blk)])

            stop(3)
            S.label = 'SSD' + ('m' if main else 'p')
            zs = (f4[4], f4[5])
            yT = (f4[6], f4[7])
            sqg = h2[2]
            ps_acsb = aux0[:, 0:512]
            ps_xs = aux0[:, 512:768]
            ps_y = aux0[:, 768:1024]
            ps_st = aux1[:, 0:256]
            ps_yo = aux1[:, 256:512]
            ps_sc = aux1[:, 512:640]
            ps_yT = aux1[:, 640:896]
            ps_bt = aux1[:, 896:960].bitcast(BF16)
            for g in range(G):
                hs = slice(g * 4, g * 4 + 4)
                pg = prev[:, g * 256:(g + 1) * 256]
                if ALT and g % 2 == 1:
                    xsT, BT, CT = alt_xs, alt_b, alt_c
                    xstok, btk, ctk = ("alt_xs0", "alt_xs1"), "alt_b", "alt_c"
                    altw = ["ovl", "ovlb"] + acttoks
                else:
                    xsT, BT, CT = (f4[2], f4[3]), h2[0], h2[1]
                    xstok, btk, ctk = ("f4_2", "f4_3"), "h2_0", "h2_1"
                    altw = []
                for i in range(2):
                    cidx = g * 2 + i
                    bt, btok = proj_chunk(c.oxs + cidx * 128)
                    conv_chunk(bt, btok, car_s, cidx, "cws", "cbs", f4[0], "f4_0", f4[1], "f4_1", "car_s")
                    ACT(xsT[i][:, 0:TT], f4[1][:, 0:TT], AF.Silu, ["f4_1"], [xstok[i]] + altw)
                for (cidx, dst, dtok) in ((KS + g, BT, btk), (KS + G + g, CT, ctk)):
                    if cidx >= KS + G and not main and c.NMAIN > 0 and not last_prefix:
                        continue
                    bt, btok = proj_chunk(c.oxs + cidx * 128)
                    conv_chunk(bt, btok, car_s, cidx, "cws", "cbs", f4[0], "f4_0", f4[1], "f4_1", "car_s")
                    ACT(dst[:, 0:TT], f4[1][:, 0:TT], AF.Silu, ["f4_1"], [dtok] + altw)
                if main:
                    for i in range(2):
                        bt, btok = proj_chunk(g * 256 + i * 128)
                        ACT(zs[i][:, 0:TT], bt[:, 0:TT], AF.Silu, btok, ["f4_%d" % (4 + i)])
                stop(10)
                for q in range(NQ):
                    qs = slice(q * 128, (q + 1) * 128)
                    sm = smsets[q % NSM]
                    smp = "sm%d_" % (q % NSM)
                    for i in range(2):
                        TR(ps_xs[:, i * 128:(i + 1) * 128], xsT[i][:, qs], identf[:], [xstok[i], "identf"], [("aux0", 1)])
                    CP("act", sm["xs"][:], ps_xs, [("aux0", 1)], [smp + "xs"])
                    xs3 = sm["xs"][:].rearrange("p (h d) -> p h d", d=64)
                    TTn("dve", sm["x6e"][:].rearrange("p (h d) -> p h d", d=64), xs3, bc(dtf["dtde"][:, q, hs], 2, [128, 4, 64]),
                        ALU.mult, [smp + "xs", "dt_dtde"], [smp + "x6e"])
                    if main:
                        TTn("dve", sm["y"][:].rearrange("p (h d) -> p h d", d=64), xs3, bc(vec["dsk"][:, hs], 2, [128, 4, 64]),
                            ALU.mult, [smp + "xs", "v_dsk"], [smp + "y"])
                    TR(ps_bt, BT[:, qs], identb[:], [btk, "identb"], [("aux1", 1)])
                    CP("act", sm["bt"][:], ps_bt, [("aux1", 1)], [smp + "bt"])
                    MM(ps_st, sm["bt"][:], sm["x6e"][:], True, True, [smp + "bt", smp + "x6e"], [("aux1", 0)])
                    if main:
                        CP("act", prevb[:, q, :], pg, [("prev", g)], [("prevb", q)])
                    TTn("dve", pg.rearrange("p (h d) -> p h d", d=64), pg.rearrange("p (h d) -> p h d", d=64),
                        bc(dtf["cdb"][:, q, hs], 2, [128, 4, 64]), ALU.mult, [("prev", g), "dt_cdb"], [("prev", g)])
                    TTn("dve", pg, pg, ps_st, ALU.add, [("prev", g), ("aux1", 0)], [("prev", g)])
                    if not main:
                        continue
                    TTn("dve", sm["x6"][:].rearrange("p (h d) -> p h d", d=64), xs3, bc(dtf["dt"][:, q, hs], 2, [128, 4, 64]),
                        ALU.mult, [smp + "xs", "dt_dt"], [smp + "x6"])
                    TTn("dve", sm["xhi"][:], bc(adt_hi[:, q, hs], 2, [128, 4, 128]), bc(trib[:], 1, [128, 4, 128]), ALU.mult,
                        ["adt_hi", "trib"], [smp + "xhi"])
                    TTn("dve", sm["xlo"][:], bc(adt_lo[:, q, hs], 2, [128, 4, 128]), bc(trib[:], 1, [128, 4, 128]), ALU.mult,
                        ["adt_lo", "trib"], [smp + "xlo"])
                    MM(ps_acsb, onesb[:], sm["xhi"][:].rearrange("p h l -> p (h l)"), True, False, ["onesb", smp + "xhi"], [("aux0", 0)])
                    MM(ps_acsb, onesb[:], sm["xlo"][:].rearrange("p h l -> p (h l)"), False, False, ["onesb", smp + "xlo"], [("aux0", 0)])
                    MM(ps_acsb, identb[:], negm[:].rearrange("p h l -> p (h l)"), False, True, ["identb", "negm"], [("aux0", 0)])
                    MM(ps_sc, BT[:, qs], CT[:, qs], True, True, [btk, ctk], [("aux1", 1)])
                    TTn("dve", sm["t"][:], ps_acsb.rearrange("p (h l) -> p h l", l=128), bc(dtf["acs"][:, q, hs], 2, [128, 4, 128]),
                        ALU.subtract, [("aux0", 0), "dt_acs"], [smp + "t"])
                    ACT(sm["t"][:], sm["t"][:], AF.Exp, [smp + "t"], [smp + "t"])
                    TTn("dve", sm["mt"][:], sm["t"][:], bc(ps_sc, 1, [128, 4, 128]), ALU.mult, [smp + "t", ("aux1", 1)], [smp + "mt"])
                    for h in range(4):
                        MM(ps_y[:, h * 64:(h + 1) * 64], sm["mt"][:, h, :], sm["x6"][:, h * 64:(h + 1) * 64], True, True,
                           [smp + "mt", smp + "x6"], [("aux0", 1)])
                    MM(ps_yo, CT[:, qs], prevb[:, q, :], True, True, [ctk, ("prevb", q)], [("aux1", 0)])
                    y3 = sm["y"][:].rearrange("p (h d) -> p h d", d=64)
                    tmp = sm["t"][:].rearrange("p h l -> p (h l)")[:, 0:256]
                    TTn("dve", tmp.rearrange("p (h d) -> p h d", d=64), ps_yo.rearrange("p (h d) -> p h d", d=64), bc(dtf["ea"][:, q, hs], 2, [128, 4, 64]), ALU.mult,
                        [("aux1", 0), "dt_ea", smp + "t"], [smp + "t"])
                    TTn("dve", sm["y"][:], sm["y"][:], ps_y, ALU.add, [smp + "y", ("aux0", 1)], [smp + "y"])
                    TTn("dve", sm["y"][:], sm["y"][:], tmp, ALU.add, [smp + "y", smp + "t"], [smp + "y"])
                    for i in range(2):
                        TR(ps_yT[:, i * 128:(i + 1) * 128], sm["y"][:, i * 128:(i + 1) * 128], identf[:], [smp + "y", "identf"], [("aux1", 1)])
                    for i in range(2):
                        CP("act", yT[i][:, qs], ps_yT[:, i * 128:(i + 1) * 128], [("aux1", 1)], ["f4_%d" % (6 + i)])
                stop(11)
                if not main:
                    continue
                gt, gtok = nextbig()
                for i in range(2):
                    TTn("dve", yT[i][:, 0:TT], yT[i][:, 0:TT], zs[i][:, 0:TT], ALU.mult, ["f4_%d" % (6 + i), "f4_%d" % (4 + i)], ["f4_%d" % (6 + i)])
                    ACT(sqg[:], yT[i][:, 0:TT], AF.Square, ["f4_%d" % (6 + i)], ["h2_2"])
                    for hf in range(NH):
                        MM(gt[:, hf * HW:(hf + 1) * HW], onesb[:], sqg[:, hf * HW:(hf + 1) * HW], i == 0, i == 1, ["onesb", "h2_2"], [gtok[hf * HW // 512]])
                rstd_from(gt[:, 0:TT], gtok, 256, f4[0][:, 0:TT], "f4_0")
                for i in range(2):
                    STT(vT[:, g * 2 + i, :], yT[i][:, 0:TT], vec["snorm"][:, g * 2 + i:g * 2 + i + 1], f4[0][:, 0:TT], ALU.mult, ALU.mult,
                        ["f4_%d" % (6 + i), "f4_0", "v_snorm"], [("vT", g * 2 + i)])
            stop(9)

        def phaseB(t0):
            stop(4)
            S.label = 'B1'
            vtoks = [("vT", k) for k in range(KM)]
            for cc in range(KD):
                bt, btok = nextbig()
                nkk = (KM + KD - 1) // KD
                wl = []
                for part in range(nkk):
                    k0 = part * KD
                    nk = min(KD, KM - k0)
                    wl.append((load_w_chunk(w_out[k0 * 128:(k0 + nk) * 128, cc * 128:(cc + 1) * 128], nk), k0, nk))
                for hf in range(NH):
                    for (wb, wtok), k0, nk in wl:
                        for k in range(nk):
                            MM(bt[:, hf * HW:(hf + 1) * HW], wb[:, k, :], vT[:, k0 + k, hf * HW:(hf + 1) * HW], k0 + k == 0, k0 + k == KM - 1,
                               [wtok] + vtoks, [btok[hf * HW // 512]])
                st, stok = f4[cc % 2], "f4_%d" % (cc % 2)
                sq, sqtok = h2[cc % 2], "h2_%d" % (cc % 2)
                CP("act", st[:, 0:TT], bt[:, 0:TT], btok, [stok])
                ACT(sq[:], bt[:, 0:TT], AF.Square, btok, [sqtok])
                for hf in range(NH):
                    MM(aux0[:, hf * HW:(hf + 1) * HW], onesb[:], sq[:, hf * HW:(hf + 1) * HW], cc == 0, cc == KD - 1, ["onesb", sqtok], [("aux0", hf * HW // 512)])
                DMA("sp", msT[cc], st[:, 0:TT], [stok], [("msT", cc)], stok + "s")
            r1 = f4[7]
            rstd_from(aux0[:, 0:TT], auxt("aux0"), D, r1[:, 0:TT], "f4_7")
            stop(5)
            S.label = 'B2'
            for cc in range(KD):
                ms, mstok = f4[cc % 2], "f4_%d" % (cc % 2)
                xb, xbtok = f4[2 + cc % 2], "f4_%d" % (2 + cc % 2)
                sq, sqtok = h2[cc % 2], "h2_%d" % (cc % 2)
                DMA("sp", ms[:, 0:TT], msT[cc], [("msT", cc)], [mstok], mstok)
                DMA("sp", xb[:, 0:TT].rearrange("p (j m) -> p j m", m=128),
                    xm[t0:t0 + TT, cc * 128:(cc + 1) * 128].rearrange("(j p) m -> p j m", p=128), [], [xbtok], xbtok)
                bt, btok = nextbig()
                for j in range(NQ):
                    TR(bt[:, j * 128:(j + 1) * 128], xb[:, j * 128:(j + 1) * 128], identf[:], [xbtok, "identf"], [btok[(j * 128) // 512]])
                STT(ms[:, 0:TT], ms[:, 0:TT], vec["g_post"][:, cc:cc + 1], r1[:, 0:TT], ALU.mult, ALU.mult, [mstok, "f4_7", "v_g_post"], [mstok])
                TTn("dve", ms[:, 0:TT], ms[:, 0:TT], bt[:, 0:TT], ALU.add, [mstok] + btok, [mstok])
                DMA("sp", x1s[cc], ms[:, 0:TT], [mstok], [("x1s", cc)], mstok + "s")
                ACT(sq[:], ms[:, 0:TT], AF.Square, [mstok], [sqtok])
                for hf in range(NH):
                    MM(aux1[:, hf * HW:(hf + 1) * HW], onesb[:], sq[:, hf * HW:(hf + 1) * HW], cc == 0, cc == KD - 1, ["onesb", sqtok], [("aux1", hf * HW // 512)])
            r2 = f4[6]
            rstd_from(aux1[:, 0:TT], auxt("aux1"), D, r2[:, 0:TT], "f4_6")
            stop(6)
            S.label = 'B3'
            for cc in range(KD):
                xb, xbtok = f4[cc % 2], "f4_%d" % (cc % 2)
                DMA("sp", xb[:, 0:TT], x1s[cc], [("x1s", cc)], [xbtok], xbtok)
                STT(hT[:, cc, :], xb[:, 0:TT], vec["g_mlp"][:, cc:cc + 1], r2[:, 0:TT], ALU.mult, ALU.mult, [xbtok, "f4_6", "v_g_mlp"],
                    [("hT", hf) for hf in range(NH)])
            stop(7)
            S.label = 'MLP'
            KB = c.FC // 128
            htoks = [("hT", hf) for hf in range(NH)]
            for fb in range(c.DFF // c.FC):
                for f in range(KB):
                    wb, wtok = load_w_chunk(w1[:, fb * c.FC + f * 128:fb * c.FC + (f + 1) * 128])
                    bt, btok = nextbig()
                    for hf in range(NH):
                        for k in range(KD):
                            MM(bt[:, hf * HW:(hf + 1) * HW], wb[:, k, :], hT[:, k, hf * HW:(hf + 1) * HW], k == 0, k == KD - 1, [wtok] + htoks, [btok[hf * HW // 512]])
                    rl, rltok = f4[2 + f % 2], "f4_%d" % (2 + f % 2)
                    ACT(rl[:, 0:TT], bt[:, 0:TT], AF.Relu, btok, [rltok])
                    TTn("dve", actT[:, f, :], rl[:, 0:TT], rl[:, 0:TT], ALU.mult, [rltok], [("act", f), "ovl", "ovlb"] + ALT)
                for cc in range(KD):
                    wb, wtok = load_w_chunk(w2[fb * c.FC:(fb + 1) * c.FC, cc * 128:(cc + 1) * 128], KB)
                    bt, btok = nextbig()
                    for hf in range(NH):
                        for k in range(KB):
                            MM(bt[:, hf * HW:(hf + 1) * HW], wb[:, k, :], actT[:, k, hf * HW:(hf + 1) * HW], k == 0, k == KB - 1,
                               [wtok, ("act", k)], [btok[hf * HW // 512]])
                    if fb == 0:
                        CP("act", accT[:, cc, :], bt[:, 0:TT], btok + vtoks, [("acc", cc)] + vtoks)
                    else:
                        TTn("dve", accT[:, cc, :], accT[:, cc, :], bt[:, 0:TT], ALU.add, btok + [("acc", cc)], [("acc", cc)])
            stop(8)
            S.label = 'B4'
            for cc in range(KD):
                sq, sqtok = h2[cc % 2], "h2_%d" % (cc % 2)
                ACT(sq[:], accT[:, cc, :], AF.Square, [("acc", cc)] + vtoks, [sqtok])
                for hf in range(NH):
                    MM(aux0[:, hf * HW:(hf + 1) * HW], onesb[:], sq[:, hf * HW:(hf + 1) * HW], cc == 0, cc == KD - 1, ["onesb", sqtok], [("aux0", hf * HW // 512)])
            r3 = f4[7]
            rstd_from(aux0[:, 0:TT], auxt("aux0"), D, r3[:, 0:TT], "f4_7")
            for cc in range(KD):
                xb, xbtok = f4[cc % 2], "f4_%d" % (cc % 2)
                ob, obtok = f4[2 + cc % 2], "f4_%d" % (2 + cc % 2)
                os_, ostok = f4[4 + cc % 2], "f4_%d" % (4 + cc % 2)
                DMA("sp", xb[:, 0:TT], x1s[cc], [("x1s", cc)], [xbtok], xbtok)
                STT(ob[:, 0:TT], accT[:, cc, :], vec["g_pmlp"][:, cc:cc + 1], r3[:, 0:TT], ALU.mult, ALU.mult, [("acc", cc), "f4_7", "v_g_pmlp"] + vtoks, [obtok])
                TTn("dve", ob[:, 0:TT], ob[:, 0:TT], xb[:, 0:TT], ALU.add, [obtok, xbtok], [obtok])
                bt, btok = nextbig()
                for j in range(NQ):
                    TR(bt[:, j * 128:(j + 1) * 128], ob[:, j * 128:(j + 1) * 128], identf[:], [obtok, "identf"], [btok[(j * 128) // 512]])
                CP("act", os_[:, 0:TT], bt[:, 0:TT], btok, [ostok])
                DMA("sp", out[t0:t0 + TT, cc * 128:(cc + 1) * 128].rearrange("(j p) m -> p j m", p=128),
                    os_[:, 0:TT].rearrange("p (j m) -> p j m", m=128), [ostok], [("out", t0, cc)], ostok + "o")
                S.out_tokens.append(("out", t0, cc))

        def _program():
            for ti in range(c.NPRE):
                phaseA(xp, ti * TT, False, last_prefix=(ti == c.NPRE - 1))
            alltok = [("prev", g) for g in range(G)]
            TS("dve", prev[:], prev[:], flag[:, 0:1], None, ALU.mult, None, alltok + ["flag"], alltok)
            TS("dve", hcar[:], hcar[:], flag[:, 0:1], None, ALU.mult, None, ["hcar", "flag"], ["hcar"])
            for ti in range(c.NMAIN):
                phaseA(xm, ti * TT, True)
                phaseB(ti * TT)

        S.out_tokens = []

        class _Stop(Exception):
            pass

        def stop(level):
            if getattr(cfg, "STOP", 0) == level:
                raise _Stop()

        try:
            _program()
        except _Stop:
            pass
        S.op("dve", lambda e: e.engine_nop(), S.out_tokens, [])
        S.emit(reorder=getattr(cfg, "REORDER", True))
        nc._sched = S
    return nc


def _vec_layouts(cfg, p):
    c = cfg
    pm = lambda v, n: np.ascontiguousarray(np.asarray(v, np.float32).reshape(n, 128).T)
    bcast = lambda v: np.ascontiguousarray(np.broadcast_to(np.asarray(v, np.float32)[None, :], (128, len(v))))
    cw = lambda w, n: np.ascontiguousarray(np.asarray(w, np.float32).reshape(4, n, 128).transpose(2, 1, 0).reshape(128, n * 4))
    return dict(
        g_pre=pm(p["pre_mix_norm"], c.KD), g_post=pm(p["post_mix_norm"], c.KD), g_mlp=pm(p["pre_mlp_norm"], c.KD),
        g_pmlp=pm(p["post_mlp_norm"], c.KD),
        cws=cw(p["ssd_conv_w"], c.KX), cbs=pm(p["ssd_conv_b"], c.KX), cwl=cw(p["lru_conv_w"], c.NBLK), cbl=pm(p["lru_conv_b"], c.NBLK),
        lba=pm(p["lru_b_a"], c.NBLK), lbx=pm(p["lru_b_x"], c.NBLK), lam=pm(p["lru_lambda"], c.NBLK), lnorm=pm(p["lru_norm"], c.NBLK),
        snorm=pm(p["ssd_norm"], c.KS), dtb=bcast(p["ssd_dt_bias"]), alog=bcast(p["ssd_a_log"]), dsk=bcast(p["ssd_d"]))


def make_in_maps(cfg, inputs, n_batch, n_half):
    c = cfg
    p = {k: np.asarray(v)[0] for k, v in inputs.items() if k != "x"}
    x = np.asarray(inputs["x"], np.float32)
    vl = _vec_layouts(c, p)
    half = c.NMAIN * c.TT
    shared = dict(w_in=np.ascontiguousarray(p["w_in"], np.float32), w_out=np.ascontiguousarray(p["w_out"], np.float32),
                  w1=np.ascontiguousarray(p["w_mlp_in"], np.float32), w2=np.ascontiguousarray(p["w_mlp_out"], np.float32),
                  lw=np.ascontiguousarray(np.concatenate([np.asarray(p["lru_w_a"], np.float32), np.asarray(p["lru_w_x"], np.float32)], axis=2)), **vl)
    maps = []
    for b in range(n_batch):
        for s in range(n_half):
            m = dict(shared)
            m["xm"] = np.ascontiguousarray(x[b, s * half:(s + 1) * half])
            pre = np.zeros((max(c.NPRE, 1) * c.TT, c.D), np.float32)
            if s > 0:
                pre[:] = x[b, (s - 1) * half:s * half][-pre.shape[0]:]
            m["xp"] = pre
            m["flag"] = np.full((128, 1), 1.0 if s > 0 else 0.0, np.float32)
            maps.append(m)
    return maps


def kernel(**inputs):
    cfg = Cfg()
    nc = build(cfg)
    maps = make_in_maps(cfg, inputs, 4, 2)
    res = run_bass_kernel_spmd(nc, maps, core_ids=list(range(8)))
    x = np.asarray(inputs["x"])
    outp = np.empty(x.shape, np.float32)
    half = cfg.NMAIN * cfg.TT
    i = 0
    for b in range(4):
        for s in range(2):
            outp[b, s * half:(s + 1) * half] = np.asarray(res.results[i]["out"], np.float32)
            i += 1
    return outp
```

```python
import contextlib
import numpy as np
import concourse.bass as bass
import concourse.mybir as mybir
from concourse.bass_utils import run_bass_kernel_spmd

F32 = mybir.dt.float32
BF16 = mybir.dt.bfloat16
AF = mybir.ActivationFunctionType
ALU = mybir.AluOpType
EPS = 1e-6


class Cfg:
    def __init__(self, D=2048, H=32, G=8, DL=2048, DFF=8192, TT=1024, NMAIN=2, NPRE=2, FC=1024):
        self.D, self.H, self.G, self.DL, self.DFF = D, H, G, DL, DFF
        self.TT, self.NMAIN, self.NPRE, self.FC = TT, NMAIN, NPRE, FC
        self.P, self.N = 64, 128
        self.DS = H * 64
        self.HG = H // G
        assert self.HG == 4
        self.KD = D // 128
        self.KS = self.DS // 128
        self.DXBC = self.DS + 2 * G * 128
        self.KX = self.DXBC // 128
        self.DIN = self.DS + self.DXBC + H + 2 * DL
        self.NBLK = DL // 128
        self.DMIX = self.DS + DL
        self.KM = self.DMIX // 128
        self.KF = DFF // 128
        self.NQ = TT // 128
        self.HW = min(512, TT)
        self.NH = TT // self.HW
        self.oxs = self.DS
        self.oB = 2 * self.DS
        self.oC = 2 * self.DS + G * 128
        self.odt = self.DS + self.DXBC
        self.ogate = self.odt + H
        self.oxl = self.ogate + DL


TUNE = dict(win=256, poolx=1.0, lat=250.0, dmaiss=1500.0, actx=1.2, dvex=1.3, pex=1.1, bw=250.0)


class Sched:
    ENGS = ("pe", "act", "dve", "pool", "sp")

    def __init__(self, nc):
        self.nc = nc
        self.ops = []
        self.last_w = {}
        self.readers = {}
        self.dma_cnt = {}

    def op(self, eng, fn, reads=(), writes=(), dma_key=None, cost=300.0, nbytes=0):
        idx = len(self.ops)
        deps = set()
        for t in reads:
            w = self.last_w.get(t)
            if w is not None:
                deps.add(w)
        for t in writes:
            w = self.last_w.get(t)
            if w is not None:
                deps.add(w)
            deps.update(self.readers.get(t, ()))
        deps.discard(idx)
        o = dict(eng=eng, fn=fn, deps=deps, dma=dma_key is not None, key=dma_key, ms=False, cost=cost, nbytes=nbytes, label=getattr(self, 'label', ''))
        if dma_key is not None:
            n = self.dma_cnt.get(dma_key, 0) + 1
            self.dma_cnt[dma_key] = n
            o["dval"] = 16 * n
        self.ops.append(o)
        for t in reads:
            self.readers.setdefault(t, []).append(idx)
        for t in writes:
            self.last_w[t] = idx
            self.readers[t] = []
        return idx

    def schedule(self, window=None):
        ops = self.ops
        window = window or TUNE["win"]
        mult = dict(pool=TUNE["poolx"], act=TUNE["actx"], dve=TUNE["dvex"], pe=TUNE["pex"], sp=1.0)
        pend = {e: [] for e in self.ENGS}
        for i, o in enumerate(ops):
            pend[o["eng"]].append(i)
        head = {e: 0 for e in self.ENGS}
        tfree = {e: 0.0 for e in self.ENGS}
        done = [None] * len(ops)
        sched = [False] * len(ops)
        order = {e: [] for e in self.ENGS}
        pipe_free = 0.0
        tf_prev = {}
        remaining = len(ops)
        while remaining:
            progressed = False
            for e in sorted(self.ENGS, key=lambda x: tfree[x]):
                lst = pend[e]
                while head[e] < len(lst) and sched[lst[head[e]]]:
                    head[e] += 1
                if head[e] >= len(lst):
                    continue
                best, best_t = None, None
                seen = 0
                j = head[e]
                while j < len(lst) and seen < window:
                    i = lst[j]
                    j += 1
                    if sched[i]:
                        continue
                    seen += 1
                    t = tfree[e]
                    ok = True
                    for d in ops[i]["deps"]:
                        dt_ = done[d]
                        if dt_ is None:
                            ok = False
                            break
                        if dt_ > t:
                            t = dt_
                    if ok and (best is None or t < best_t - 1e-9):
                        best, best_t = i, t
                        if t <= tfree[e] + 1e-9:
                            break
                if best is None:
                    continue
                o = ops[best]
                if o["dma"]:
                    issue = TUNE["dmaiss"] if e == "pool" else 120.0
                    xs_ = max(best_t + issue, pipe_free)
                    pipe_free = xs_ + o["nbytes"] / TUNE["bw"]
                    done[best] = pipe_free + 2000.0
                    tfree[e] = best_t + issue
                else:
                    tfree[e] = best_t + o["cost"] * mult[e]
                    tf_prev[best] = tfree[e]
                    done[best] = tfree[e] + (TUNE["lat"] if e != "pe" else 0.0)
                sched[best] = True
                o['t0'] = best_t
                o['crit'] = max(list(o['deps']) + ([order[e][-1]] if order[e] else []), key=lambda d_: (done[d_] if (ops[d_]['eng'] != e or ops[d_]['dma']) else tf_prev.get(d_, done[d_])), default=None)
                o['t1'] = done[best]
                order[e].append(best)
                remaining -= 1
                progressed = True
                break
            assert progressed, "list scheduler stuck"
        self.makespan = max(d for d in done if d is not None)
        return order

    def emit(self, reorder=True):
        nc, ops = self.nc, self.ops
        if reorder:
            order = self.schedule()
        else:
            order = {e: [i for i, o in enumerate(ops) if o["eng"] == e] for e in self.ENGS}
        pos = {}
        for e in self.ENGS:
            for k, i in enumerate(order[e]):
                pos[i] = k
        for i, o in enumerate(ops):
            latest = {}
            eff = []
            for d in o["deps"]:
                p = ops[d]
                if p["dma"]:
                    eff.append(d)
                    continue
                if p["eng"] == "pe" and o["eng"] == "pe" and not o["dma"]:
                    assert pos[d] < pos[i]
                    continue
                if p["eng"] == o["eng"]:
                    assert pos[d] < pos[i]
                if p["eng"] not in latest or pos[d] > pos[latest[p["eng"]]]:
                    latest[p["eng"]] = d
            o["eff"] = eff + list(latest.values())
            for d in latest.values():
                ops[d]["ms"] = True
        KSEM = 8
        cnt = {e: 0 for e in self.ENGS}
        for e in self.ENGS:
            for i in order[e]:
                o = ops[i]
                if o["ms"]:
                    o["msi"] = cnt[e]
                    cnt[e] += 1
        self.ms_counts = cnt
        with contextlib.ExitStack() as es:
            esem = {e: [es.enter_context(nc.semaphore("S_%s%d" % (e, i))) for i in range(KSEM)] for e in self.ENGS if cnt[e] > 0}
            dsem = {}
            for i, (k, n) in enumerate(self.dma_cnt.items()):
                r = 1 if k in ("setup", "setup_p") else max(1, (n * 16 + 479) // 480)
                dsem[k] = [es.enter_context(nc.semaphore("D_%d_%d" % (i, j))) for j in range(r)]
            block = es.enter_context(nc.Block())

            def semval(p):
                if p["dma"]:
                    lst = dsem[p["key"]]
                    if p["key"] in ("setup", "setup_p"):
                        return lst[0], 16 * self.dma_cnt[p["key"]]
                    n = p["dval"] // 16 - 1
                    return lst[n % len(lst)], 16 * (n // len(lst) + 1)
                i = p["msi"]
                return esem[p["eng"]][i % KSEM], i // KSEM + 1

            def run(engname, eng):
                waited = {}
                for oi in order[engname]:
                    o = ops[oi]
                    need = {}
                    for d in o["eff"]:
                        p = ops[d]
                        s, v = semval(p)
                        if need.get(id(s), (None, 0))[1] < v:
                            need[id(s)] = (s, v)
                    for s, v in need.values():
                        if waited.get(id(s), 0) < v:
                            eng.wait_ge(s, v)
                            waited[id(s)] = v
                    ins = o["fn"](eng)
                    if o["dma"]:
                        ins.then_inc(semval(o)[0], 16)
                    elif o["ms"]:
                        ins.then_inc(semval(o)[0], 1)

            block.tensor(lambda e: run("pe", e))
            block.scalar(lambda e: run("act", e))
            block.vector(lambda e: run("dve", e))
            block.gpsimd(lambda e: run("pool", e))
            block.sync(lambda e: run("sp", e))


def build(cfg):
    c = cfg
    D, H, G, TT, KD, KS, KX, KM, NQ, NH, HW, NBLK = c.D, c.H, c.G, c.TT, c.KD, c.KS, c.KX, c.KM, c.NQ, c.NH, c.HW, c.NBLK
    nc = bass.Bass("TRN2", target_bir_lowering=False)
    din = lambda name, shape: nc.dram_tensor(name, list(shape), F32, kind="ExternalInput").ap()
    xm = din("xm", [c.NMAIN * TT, D])
    xp = din("xp", [max(c.NPRE, 1) * TT, D])
    flag_d = din("flag", [128, 1])
    w_in = din("w_in", [D, c.DIN])
    w_out = din("w_out", [c.DMIX, D])
    w1 = din("w1", [D, c.DFF])
    w2 = din("w2", [c.DFF, D])
    lw_d = din("lw", [NBLK, 128, 256])
    vec_d = {}
    vec_shapes = dict(g_pre=[128, KD], g_post=[128, KD], g_mlp=[128, KD], g_pmlp=[128, KD],
                      cws=[128, KX * 4], cbs=[128, KX], cwl=[128, NBLK * 4], cbl=[128, NBLK],
                      lba=[128, NBLK], lbx=[128, NBLK], lam=[128, NBLK], lnorm=[128, NBLK],
                      snorm=[128, KS], dtb=[128, H], alog=[128, H], dsk=[128, H])
    for k, shp in vec_shapes.items():
        vec_d[k] = din(k, shp)
    out = nc.dram_tensor("out", [c.NMAIN * TT, D], F32, kind="ExternalOutput").ap()
    msT = nc.dram_tensor("msT", [KD, 128, TT], F32).ap()
    x1s = nc.dram_tensor("x1s", [KD, 128, TT], F32).ap()

    es = contextlib.ExitStack()
    with es:
        sb = lambda name, shape, dt: es.enter_context(nc.sbuf_tensor(name, list(shape), dt))
        S = Sched(nc)
        if getattr(cfg, "PAD", 0):
            sb("pad", [128, cfg.PAD // 4], F32)
        def ACT(out_, in_, func, r, w, bias=None, scale=None, accum=None):
            kw = {}
            if bias is not None:
                kw["bias"] = bias
            if scale is not None:
                kw["scale"] = scale
            if accum is not None:
                kw["accum_out"] = accum
            S.op("act", lambda e: e.activation(out=out_, in_=in_, func=func, **kw), r, w, cost=(224.0 + in_.free_size()) / 1.2 + (90.0 if accum is not None else 0.0))

        def TTn(eng, out_, in0, in1, op, r, w):
            S.op(eng, lambda e: e.tensor_tensor(out=out_, in0=in0, in1=in1, op=op), r, w, cost=(130.0 + out_.free_size()) / 0.96)

        def TS(eng, out_, in0, s1, s2, op0, op1, r, w):
            if s2 is None:
                S.op(eng, lambda e: e.tensor_scalar(out=out_, in0=in0, scalar1=s1, scalar2=None, op0=op0), r, w, cost=(130.0 + out_.free_size()) / 0.96)
            else:
                S.op(eng, lambda e: e.tensor_scalar(out=out_, in0=in0, scalar1=s1, scalar2=s2, op0=op0, op1=op1), r, w, cost=(130.0 + out_.free_size()) / 0.96)

        def STT(out_, in0, scalar, in1, op0, op1, r, w):
            S.op("dve", lambda e: e.scalar_tensor_tensor(out=out_, in0=in0, scalar=scalar, in1=in1, op0=op0, op1=op1), r, w, cost=(130.0 + out_.free_size()) / 0.96)

        def CP(eng, out_, in_, r, w):
            if eng == "act":
                S.op("act", lambda e: e.activation(out=out_, in_=in_, func=AF.Copy), r, w, cost=(224.0 + in_.free_size()) / 1.2)
            else:
                S.op(eng, lambda e: e.tensor_copy(out=out_, in_=in_), r, w, cost=(130.0 + out_.free_size()) / 0.96)

        def MM(out_, lhsT, rhs, start, stop, r, w):
            S.op("pe", lambda e: e.matmul(out_, lhsT=lhsT, rhs=rhs, start=start, stop=stop), r, w, cost=max(64.0, rhs.free_size()) / 2.4 + 12.0)

        def TR(out_, in_, ident, r, w):
            S.op("pe", lambda e: e.transpose(out_, in_, ident), r, w, cost=110.0)

        def DMA(q, out_, in_, r, w, key):
            S.op(q, lambda e: e.dma_start(out=out_, in_=in_), r, w, dma_key=key, nbytes=max(out_.nbytes(), in_.nbytes()))

        def bc(ap, axis, shape):
            return ap.unsqueeze(axis).to_broadcast(list(shape))

        identb = sb("identb", [128, 128], BF16)
        identf = sb("identf", [128, 128], F32)
        onesb = sb("onesb", [128, 128], BF16)
        trib = sb("trib", [128, 128], BF16)
        negm = sb("negm", [128, 4, 128], BF16)
        vec = {k: sb("v_" + k, shp, F32) for k, shp in vec_shapes.items()}
        flag = sb("flag_sb", [128, 1], F32)
        clru = sb("clru", [128, NBLK], F32)
        a_bc = sb("a_bc", [128, H], F32)
        wdt = sb("wdt", [128, KD, H], BF16)
        lwb = sb("lwb", [128, 2, 256], BF16)
        prev = sb("prev", [128, G * 256], F32)
        prevb = sb("prevb", [128, NQ, 256], BF16)
        hcar = sb("hcar", [128, NBLK], F32)
        car_s = sb("car_s", [128, KX, 3], F32)
        car_l = sb("car_l", [128, NBLK, 3], F32)
        QH = NQ * H
        dtf = {k: sb("dt_" + k, [128, NQ, H], F32) for k in ("dt", "adt", "acs", "ea", "cdb", "dtde", "t0", "t1", "t2")}
        adt_hi = sb("adt_hi", [128, NQ, H], BF16)
        adt_lo = sb("adt_lo", [128, NQ, H], BF16)
        ssq_x = sb("ssq_x", [128, NQ], F32)
        RAK = max(KD, (KM + 1) // 2)
        RA = sb("RA", [128, RAK * TT], F32)
        vT = RA[:].bitcast(BF16).rearrange("p (k t) -> p k t", t=TT)
        accT = RA[:].rearrange("p (k t) -> p k t", t=TT)
        hT = sb("hT", [128, KD, TT], BF16)
        OVW = max(c.FC // 128 * TT // 2, D + D // 2)
        ovl = sb("ovl", [128, OVW], F32)
        xt = ovl[:, 0:D]
        xn = ovl[:, D:D + D // 2].bitcast(BF16)
        actT = ovl[:, 0:c.FC // 128 * TT // 2].bitcast(BF16).rearrange("p (k t) -> p k t", t=TT)
        ALT = []
        if getattr(cfg, "ALTSSD", False) and OVW >= 3 * (TT + 4):
            ALT = ["alt_xs0", "alt_xs1", "alt_b", "alt_c"]
            alt_xs = (ovl[:, 0:TT + 4], ovl[:, TT + 4:2 * (TT + 4)])
            alt_b = ovl[:, 2 * (TT + 4):2 * (TT + 4) + TT // 2].bitcast(BF16)
            alt_c = ovl[:, 2 * (TT + 4) + TT // 2:2 * (TT + 4) + TT].bitcast(BF16)
        NW = 3
        wbuf = [sb("wbuf%d" % i, [128, max(KD, c.FC // 128), 128], BF16) for i in range(NW)]
        NF = 8
        f4 = [sb("f4_%d" % i, [128, TT + 4], F32) for i in range(NF)]
        h2 = [sb("h2_%d" % i, [128, TT], BF16) for i in range(3)]
        NSM = 2
        smsets = [{k: sb("sm%d_%s" % (n_, k), shp, dt) for k, (shp, dt) in dict(
            xs=([128, 256], F32), x6=([128, 256], BF16), x6e=([128, 256], BF16), bt=([128, 128], BF16),
            xhi=([128, 4, 128], BF16), xlo=([128, 4, 128], BF16), t=([128, 4, 128], F32),
            mt=([128, 4, 128], BF16), y=([128, 256], F32)).items()} for n_ in range(NSM)]
        ps = lambda name: es.enter_context(nc.psum_tensor(name, [128, 1024], F32))
        big = [ps("big0"), ps("big1")]
        aux0, aux1 = ps("aux0"), ps("aux1")
        bigi = [0]

        def nextbig():
            i = bigi[0] % 2
            bigi[0] += 1
            return big[i], [("big%d" % i, b) for b in range((TT + 511) // 512)]

        wi = [0]

        def nextw():
            i = wi[0] % NW
            wi[0] += 1
            return wbuf[i], "wbuf%d" % i

        for k in vec_shapes:
            DMA("sp", vec[k][:], vec_d[k][:], [], ["v_" + k], "setup")
        DMA("sp", flag[:], flag_d[:], [], ["flag"], "setup")
        DMA("pool", wdt[:], w_in[:, c.odt:c.odt + H].rearrange("(k p) h -> p k h", p=128), [], ["wdt"], "setup_p")

        def mask_const(t, tok, fill, pat, cm, op):
            S.op("pool", lambda e: e.affine_select(out=t[:], in_=t[:], pattern=pat, compare_op=op, fill=fill,
                                                   base=0, channel_multiplier=cm), [tok], [tok])

        S.op("pool", lambda e: e.memset(identb[:], 1.0), [], ["identb"])
        mask_const(identb, "identb", 0.0, [[-1, 128]], 1, ALU.is_equal)
        S.op("pool", lambda e: e.memset(identf[:], 1.0), [], ["identf"])
        mask_const(identf, "identf", 0.0, [[-1, 128]], 1, ALU.is_equal)
        S.op("pool", lambda e: e.memset(onesb[:], 1.0), [], ["onesb"])
        S.op("pool", lambda e: e.memset(trib[:], 1.0), [], ["trib"])
        mask_const(trib, "trib", 0.0, [[1, 128]], -1, ALU.is_ge)
        S.op("pool", lambda e: e.memset(negm[:], 0.0), [], ["negm"])
        mask_const(negm, "negm", -30000.0, [[0, 4], [1, 128]], -1, ALU.is_ge)
        S.op("pool", lambda e: e.memset(prev[:], 0.0), [], ["prev"])
        S.op("pool", lambda e: e.memset(hcar[:], 0.0), [], ["hcar"])
        S.op("pool", lambda e: e.memset(car_s[:], 0.0), [], ["car_s"])
        S.op("pool", lambda e: e.memset(car_l[:], 0.0), [], ["car_l"])

        def log1p_small(out_, e_, tmpw, tmpq, tok_o, tok_e, tok_w, tok_q):
            TS("dve", tmpw, e_, 2.0, None, ALU.add, None, [tok_e], [tok_w])
            S.op("dve", lambda e: e.reciprocal(out=tmpw, in_=tmpw), [tok_w], [tok_w])
            TTn("dve", tmpw, tmpw, e_, ALU.mult, [tok_w, tok_e], [tok_w])
            TTn("dve", out_, tmpw, tmpw, ALU.mult, [tok_w], [tok_o])
            TS("dve", tmpq, out_, 1.0 / 11.0, None, ALU.mult, None, [tok_o], [tok_q])
            for cst in (1.0 / 9.0, 1.0 / 7.0, 1.0 / 5.0, 1.0 / 3.0):
                STT(tmpq, tmpq, cst, out_, ALU.add, ALU.mult, [tok_q, tok_o], [tok_q])
            TS("dve", tmpq, tmpq, 1.0, None, ALU.add, None, [tok_q], [tok_q])
            TTn("dve", tmpq, tmpq, tmpw, ALU.mult, [tok_q, tok_w], [tok_q])
            TS("dve", out_, tmpq, 2.0, None, ALU.mult, None, [tok_q], [tok_o])

        def softplus(out_, x_, ta, tb, tcc, tok_o, tok_x, tok_a, tok_b, tok_c):
            TS("dve", ta, x_, -1.0, None, ALU.mult, None, [tok_x], [tok_a])
            TTn("dve", ta, ta, x_, ALU.max, [tok_a, tok_x], [tok_a])
            ACT(ta, ta, AF.Exp, [tok_a], [tok_a], scale=-1.0)
            log1p_small(tb, ta, tcc, out_, tok_b, tok_a, tok_c, tok_o)
            TS("dve", ta, x_, 0.0, None, ALU.max, None, [tok_x], [tok_a])
            TTn("dve", out_, ta, tb, ALU.add, [tok_a, tok_b], [tok_o])

        t0s = dtf["t0"][:, 0, 0:NBLK] if NBLK <= H else None
        assert NBLK <= H
        t1s, t2s, t3s, t4s = (dtf[k][:, 0, 0:NBLK] for k in ("t1", "t2", "dt", "adt"))
        TS("dve", t0s, vec["lam"][:], -1.0, None, ALU.mult, None, ["v_lam"], ["dt_t0"])
        softplus(t1s, t0s, t2s, t3s, t4s, "dt_t1", "dt_t0", "dt_t2", "dt_dt", "dt_adt")
        TS("dve", clru[:], t1s, -8.0, None, ALU.mult, None, ["dt_t1"], ["clru"])
        ACT(a_bc[:], vec["alog"][:], AF.Exp, ["v_alog"], ["a_bc"])
        TS("dve", a_bc[:], a_bc[:], -1.0, None, ALU.mult, None, ["a_bc"], ["a_bc"])

        def load_w_chunk(src_ap, nk=None):
            wb, tok = nextw()
            nk = KD if nk is None else nk
            DMA("pool", wb[:, 0:nk, :], src_ap.rearrange("(k p) m -> p k m", p=128), [], [tok], tok)
            return wb, tok

        def proj_chunk(col0):
            wb, wtok = load_w_chunk(w_in[:, col0:col0 + 128])
            bt, btok = nextbig()
            for hf in range(NH):
                for k in range(KD):
                    MM(bt[:, hf * HW:(hf + 1) * HW], wb[:, k, :], hT[:, k, hf * HW:(hf + 1) * HW], k == 0, k == KD - 1,
                       [wtok, ("hT", hf)], [btok[hf * HW // 512]])
            return bt, btok

        def conv_chunk(bt, btok, car, cidx, cwn, cbn, xpad, xptok, acc, acctok, car_tok):
            cw, cb = vec[cwn], vec[cbn]
            CP("pool", xpad[:, 0:3], car[:, cidx, :], [car_tok], [xptok])
            CP("act", xpad[:, 3:3 + TT], bt[:, 0:TT], btok + [xptok], [xptok])
            CP("pool", car[:, cidx, :], xpad[:, TT:TT + 3], [xptok], [car_tok])
            ACT(acc[:, 0:TT], xpad[:, 0:TT], AF.Identity, [xptok, "v_" + cwn, "v_" + cbn], [acctok],
                bias=cb[:, cidx:cidx + 1], scale=cw[:, 4 * cidx:4 * cidx + 1])
            for k in (1, 2, 3):
                STT(acc[:, 0:TT], xpad[:, k:k + TT], cw[:, 4 * cidx + k:4 * cidx + k + 1], acc[:, 0:TT], ALU.mult, ALU.add,
                    [xptok, acctok, "v_" + cwn], [acctok])

        def rstd_from(ps_ap, pstok, n, dst, dtok):
            ACT(dst, ps_ap, AF.Sqrt, pstok + ["eps"], [dtok], bias=eps_t[:, 0:1], scale=1.0 / n)
            S.op("dve", lambda e: e.reciprocal(out=dst, in_=dst), [dtok], [dtok])

        auxt = lambda nm: [(nm, b) for b in range((TT + 511) // 512)]
        eps_t = sb("eps_t", [128, 1], F32)
        S.op("pool", lambda e: e.memset(eps_t[:], EPS), [], ["eps"])

        acttoks = [("act", f) for f in range(c.FC // 128)]

        def phaseA(xsrc, t0, main, last_prefix=False):
            S.label = 'A0'
            for j in range(NQ):
                DMA("sp", xt, xsrc[t0 + j * 128:t0 + (j + 1) * 128, :], [], ["ovl"] + acttoks + ALT, "ovl")
                ACT(xn, xt, AF.Square, ["ovl"], ["ovlb", "ssq%d" % j] + ALT, accum=ssq_x[:, j:j + 1])
                ACT(ssq_x[:, j:j + 1], ssq_x[:, j:j + 1], AF.Sqrt, ["ssq%d" % j, "eps"], ["ssq%d" % j], bias=eps_t[:, 0:1], scale=1.0 / D)
                S.op("dve", lambda e, j=j: e.reciprocal(out=ssq_x[:, j:j + 1], in_=ssq_x[:, j:j + 1]), ["ssq%d" % j], ["ssq%d" % j])
                ACT(xn, xt, AF.Identity, ["ovl", "ovlb", "ssq%d" % j], ["ovlb"], scale=ssq_x[:, j:j + 1])
                pt, ptok = nextbig()
                ptb = pt[:].bitcast(BF16)
                for k in range(KD):
                    TR(ptb[:, k * 128:(k + 1) * 128], xn[:, k * 128:(k + 1) * 128], identb[:], ["ovlb", "identb"], [ptok[(k * 64) // 512]])
                TTn("dve", hT[:, :, j * 128:(j + 1) * 128], ptb[:, 0:KD * 128].rearrange("p (k m) -> p k m", m=128),
                    bc(vec["g_pre"][:], 2, [128, KD, 128]), ALU.mult, ptok + ["v_g_pre"], [("hT", (j * 128) // HW)])
            stop(1)
            S.label = 'A1'
            a0v = aux0[:, 0:QH].rearrange("p (q h) -> p q h", h=H)
            a1v = aux0[:, 512:512 + QH].rearrange("p (q h) -> p q h", h=H)
            for q in range(NQ):
                for k in range(KD):
                    MM(a0v[:, q, :], hT[:, k, q * 128:(q + 1) * 128], wdt[:, k, :], k == 0, k == KD - 1,
                       [("hT", (q * 128) // HW), "wdt"], [("aux0", 0)])
            TTn("dve", dtf["t0"][:], a0v, bc(vec["dtb"][:], 1, [128, NQ, H]), ALU.add, [("aux0", 0), "v_dtb"], ["dt_t0"])
            softplus(dtf["dt"][:], dtf["t0"][:], dtf["t1"][:], dtf["t2"][:], dtf["adt"][:], "dt_dt", "dt_t0", "dt_t1", "dt_t2", "dt_adt")
            TTn("dve", dtf["adt"][:], dtf["dt"][:], bc(a_bc[:], 1, [128, NQ, H]), ALU.mult, ["dt_dt", "a_bc"], ["dt_adt"])
            CP("dve", adt_hi[:], dtf["adt"][:], ["dt_adt"], ["adt_hi"])
            TTn("dve", adt_lo[:], dtf["adt"][:], adt_hi[:], ALU.subtract, ["dt_adt", "adt_hi"], ["adt_lo"])
            for q in range(NQ):
                MM(a0v[:, q, :], trib[:], adt_hi[:, q, :], True, False, ["trib", "adt_hi"], [("aux0", 0)])
                MM(a0v[:, q, :], trib[:], adt_lo[:, q, :], False, True, ["trib", "adt_lo"], [("aux0", 0)])
                MM(a1v[:, q, :], onesb[:], adt_hi[:, q, :], True, False, ["onesb", "adt_hi"], [("aux0", 1)])
                MM(a1v[:, q, :], onesb[:], adt_lo[:, q, :], False, True, ["onesb", "adt_lo"], [("aux0", 1)])
            CP("act", dtf["acs"][:], a0v, [("aux0", 0)], ["dt_acs"])
            if main and not getattr(cfg, "NOEA", 0):
                ACT(dtf["ea"][:], dtf["acs"][:], AF.Exp, ["dt_acs"], ["dt_ea"])
            ACT(dtf["cdb"][:], a1v, AF.Exp, [("aux0", 1)], ["dt_cdb"])
            TTn("dve", dtf["t0"][:], a1v, dtf["acs"][:], ALU.subtract, [("aux0", 1), "dt_acs"], ["dt_t0"])
            ACT(dtf["t0"][:], dtf["t0"][:], AF.Exp, ["dt_t0"], ["dt_t0"])
            TTn("dve", dtf["dtde"][:], dtf["t0"][:], dtf["dt"][:], ALU.mult, ["dt_t0", "dt_dt"], ["dt_dtde"])

            stop(2)
            S.label = 'LRU' + ('m' if main else 'p')
            xpad, xl, rr, ii, a2, hl, ge = (f4[i] for i in range(7))
            xlb, sqb = h2[0], h2[1]
            for blk in range(NBLK):
                bt, btok = proj_chunk(c.oxl + blk * 128)
                conv_chunk(bt, btok, car_l, blk, "cwl", "cbl", xpad, "f4_0", xl, "f4_1", "car_l")
                CP("act", xlb[:], xl[:, 0:TT], ["f4_1"], ["h2_0"])
                lwtok = "lwb%d" % (blk % 2)
                DMA("pool", lwb[:, blk % 2, :], lw_d[blk], [], [lwtok], lwtok)
                for (wofs, bn, dst, dtok) in ((0, "lba", rr, "f4_2"), (128, "lbx", ii, "f4_3")):
                    gt, gtok = nextbig()
                    for hf in range(NH):
                        MM(gt[:, hf * HW:(hf + 1) * HW], lwb[:, blk % 2, wofs:wofs + 128], xlb[:, hf * HW:(hf + 1) * HW], True, True, [lwtok, "h2_0"], [gtok[hf * HW // 512]])
                    ACT(dst[:, 0:TT], gt[:, 0:TT], AF.Sigmoid, gtok + ["v_" + bn], [dtok], bias=vec[bn][:, blk:blk + 1])
                ACT(rr[:, 0:TT], rr[:, 0:TT], AF.Exp, ["f4_2", "clru"], ["f4_2"], scale=clru[:, blk:blk + 1])
                ACT(a2[:, 0:TT], rr[:, 0:TT], AF.Square, ["f4_2"], ["f4_4"])
                ACT(a2[:, 0:TT], a2[:, 0:TT], AF.Sqrt, ["f4_4"], ["f4_4"], bias=1.0, scale=-1.0)
                TTn("dve", ii[:, 0:TT], ii[:, 0:TT], xl[:, 0:TT], ALU.mult, ["f4_3", "f4_1"], ["f4_3"])
                TTn("dve", ii[:, 0:TT], ii[:, 0:TT], a2[:, 0:TT], ALU.mult, ["f4_3", "f4_4"], ["f4_3"])
                S.op("dve", lambda e, blk=blk: e.tensor_tensor_scan(out=hl[:, 0:TT], data0=rr[:, 0:TT], data1=ii[:, 0:TT],
                                                                     initial=hcar[:, blk:blk + 1], op0=ALU.mult, op1=ALU.add),
                     ["f4_2", "f4_3", "hcar"], ["f4_5"], cost=(130.0 + 2 * TT) / 0.96)
                CP("pool", hcar[:, blk:blk + 1], hl[:, TT - 1:TT], ["f4_5"], ["hcar"])
                if main:
                    gt, gtok = proj_chunk(c.ogate + blk * 128)
                    gx = f4[7]
                    CP("act", gx[:, 0:TT], gt[:, 0:TT], gtok, ["f4_7"])
                    TTn("dve", ge[:, 0:TT], gx[:, 0:TT], gx[:, 0:TT], ALU.mult, ["f4_7"], ["f4_6"])
                    TS("dve", ge[:, 0:TT], ge[:, 0:TT], 0.044715, 1.0, ALU.mult, ALU.add, ["f4_6"], ["f4_6"])
                    TTn("dve", ge[:, 0:TT], ge[:, 0:TT], gx[:, 0:TT], ALU.mult, ["f4_6", "f4_7"], ["f4_6"])
                    ACT(ge[:, 0:TT], ge[:, 0:TT], AF.Sigmoid, ["f4_6"], ["f4_6"], scale=1.5957691216057308)
                    TTn("dve", ge[:, 0:TT], ge[:, 0:TT], gx[:, 0:TT], ALU.mult, ["f4_6", "f4_7"], ["f4_6"])
                    TTn("dve", ge[:, 0:TT], ge[:, 0:TT], hl[:, 0:TT], ALU.mult, ["f4_6", "f4_5"], ["f4_6"])
                    ACT(sqb[:], ge[:, 0:TT], AF.Square, ["f4_6"], ["h2_1"])
                    CP("act", vT[:, KS + blk, :], ge[:, 0:TT], ["f4_6"], [("vT", KS + blk)])
                    for hf in range(NH):
                        MM(aux1[:, hf * HW:(hf + 1) * HW], onesb[:], sqb[:, hf * HW:(hf + 1) * HW], blk == 0, blk == NBLK - 1,
                           ["onesb", "h2_1"], [("aux1", hf * HW // 512)])
            if main:
                rl = f4[7]
                rstd_from(aux1[:, 0:TT], auxt("aux1"), c.DL, rl[:, 0:TT], "f4_7")
                for blk in range(NBLK):
                    STT(vT[:, KS + blk, :], vT[:, KS + blk, :], vec["lnorm"][:, blk:blk + 1], rl[:, 0:TT], ALU.mult, ALU.mult,
                        [("vT", KS + blk), "f4_7", "v_lnorm"], [("vT", KS + blk)])

            stop(3)
            S.label = 'SSD' + ('m' if main else 'p')
            zs = (f4[4], f4[5])
            yT = (f4[6], f4[7])
            sqg = h2[2]
            ps_acsb = aux0[:, 0:512]
            ps_xs = aux0[:, 512:768]
            ps_y = aux0[:, 768:1024]
            ps_st = aux1[:, 0:256]
            ps_yo = aux1[:, 256:512]
            ps_sc = aux1[:, 512:640]
            ps_yT = aux1[:, 640:896]
            ps_bt = aux1[:, 896:960].bitcast(BF16)
            for g in range(G):
                hs = slice(g * 4, g * 4 + 4)
                pg = prev[:, g * 256:(g + 1) * 256]
                if ALT and g % 2 == 1:
                    xsT, BT, CT = alt_xs, alt_b, alt_c
                    xstok, btk, ctk = ("alt_xs0", "alt_xs1"), "alt_b", "alt_c"
                    altw = ["ovl", "ovlb"] + acttoks
                else:
                    xsT, BT, CT = (f4[2], f4[3]), h2[0], h2[1]
                    xstok, btk, ctk = ("f4_2", "f4_3"), "h2_0", "h2_1"
                    altw = []
                for i in range(2):
                    cidx = g * 2 + i
                    bt, btok = proj_chunk(c.oxs + cidx * 128)
                    conv_chunk(bt, btok, car_s, cidx, "cws", "cbs", f4[0], "f4_0", f4[1], "f4_1", "car_s")
                    ACT(xsT[i][:, 0:TT], f4[1][:, 0:TT], AF.Silu, ["f4_1"], [xstok[i]] + altw)
                for (cidx, dst, dtok) in ((KS + g, BT, btk), (KS + G + g, CT, ctk)):
                    if cidx >= KS + G and not main and c.NMAIN > 0 and not last_prefix:
                        continue
                    bt, btok = proj_chunk(c.oxs + cidx * 128)
                    conv_chunk(bt, btok, car_s, cidx, "cws", "cbs", f4[0], "f4_0", f4[1], "f4_1", "car_s")
                    ACT(dst[:, 0:TT], f4[1][:, 0:TT], AF.Silu, ["f4_1"], [dtok] + altw)
                if main:
                    for i in range(2):
                        bt, btok = proj_chunk(g * 256 + i * 128)
                        ACT(zs[i][:, 0:TT], bt[:, 0:TT], AF.Silu, btok, ["f4_%d" % (4 + i)])
                stop(10)
                for q in range(NQ):
                    qs = slice(q * 128, (q + 1) * 128)
                    sm = smsets[q % NSM]
                    smp = "sm%d_" % (q % NSM)
                    for i in range(2):
                        TR(ps_xs[:, i * 128:(i + 1) * 128], xsT[i][:, qs], identf[:], [xstok[i], "identf"], [("aux0", 1)])
                    CP("act", sm["xs"][:], ps_xs, [("aux0", 1)], [smp + "xs"])
                    xs3 = sm["xs"][:].rearrange("p (h d) -> p h d", d=64)
                    TTn("dve", sm["x6e"][:].rearrange("p (h d) -> p h d", d=64), xs3, bc(dtf["dtde"][:, q, hs], 2, [128, 4, 64]),
                        ALU.mult, [smp + "xs", "dt_dtde"], [smp + "x6e"])
                    if main:
                        TTn("dve", sm["y"][:].rearrange("p (h d) -> p h d", d=64), xs3, bc(vec["dsk"][:, hs], 2, [128, 4, 64]),
                            ALU.mult, [smp + "xs", "v_dsk"], [smp + "y"])
                    TR(ps_bt, BT[:, qs], identb[:], [btk, "identb"], [("aux1", 1)])
                    CP("act", sm["bt"][:], ps_bt, [("aux1", 1)], [smp + "bt"])
                    MM(ps_st, sm["bt"][:], sm["x6e"][:], True, True, [smp + "bt", smp + "x6e"], [("aux1", 0)])
                    if main:
                        CP("act", prevb[:, q, :], pg, [("prev", g)], [("prevb", q)])
                    TTn("dve", pg.rearrange("p (h d) -> p h d", d=64), pg.rearrange("p (h d) -> p h d", d=64),
                        bc(dtf["cdb"][:, q, hs], 2, [128, 4, 64]), ALU.mult, [("prev", g), "dt_cdb"], [("prev", g)])
                    TTn("dve", pg, pg, ps_st, ALU.add, [("prev", g), ("aux1", 0)], [("prev", g)])
                    if not main:
                        continue
                    TTn("dve", sm["x6"][:].rearrange("p (h d) -> p h d", d=64), xs3, bc(dtf["dt"][:, q, hs], 2, [128, 4, 64]),
                        ALU.mult, [smp + "xs", "dt_dt"], [smp + "x6"])
                    TTn("dve", sm["xhi"][:], bc(adt_hi[:, q, hs], 2, [128, 4, 128]), bc(trib[:], 1, [128, 4, 128]), ALU.mult,
                        ["adt_hi", "trib"], [smp + "xhi"])
                    TTn("dve", sm["xlo"][:], bc(adt_lo[:, q, hs], 2, [128, 4, 128]), bc(trib[:], 1, [128, 4, 128]), ALU.mult,
                        ["adt_lo", "trib"], [smp + "xlo"])
                    MM(ps_acsb, onesb[:], sm["xhi"][:].rearrange("p h l -> p (h l)"), True, False, ["onesb", smp + "xhi"], [("aux0", 0)])
                    MM(ps_acsb, onesb[:], sm["xlo"][:].rearrange("p h l -> p (h l)"), False, False, ["onesb", smp + "xlo"], [("aux0", 0)])
                    MM(ps_acsb, identb[:], negm[:].rearrange("p h l -> p (h l)"), False, True, ["identb", "negm"], [("aux0", 0)])
                    MM(ps_sc, BT[:, qs], CT[:, qs], True, True, [btk, ctk], [("aux1", 1)])
                    TTn("dve", sm["t"][:], ps_acsb.rearrange("p (h l) -> p h l", l=128), bc(dtf["acs"][:, q, hs], 2, [128, 4, 128]),
                        ALU.subtract, [("aux0", 0), "dt_acs"], [smp + "t"])
                    ACT(sm["t"][:], sm["t"][:], AF.Exp, [smp + "t"], [smp + "t"])
                    TTn("dve", sm["mt"][:], sm["t"][:], bc(ps_sc, 1, [128, 4, 128]), ALU.mult, [smp + "t", ("aux1", 1)], [smp + "mt"])
                    for h in range(4):
                        MM(ps_y[:, h * 64:(h + 1) * 64], sm["mt"][:, h, :], sm["x6"][:, h * 64:(h + 1) * 64], True, True,
                           [smp + "mt", smp + "x6"], [("aux0", 1)])
                    MM(ps_yo, CT[:, qs], prevb[:, q, :], True, True, [ctk, ("prevb", q)], [("aux1", 0)])
                    y3 = sm["y"][:].rearrange("p (h d) -> p h d", d=64)
                    tmp = sm["t"][:].rearrange("p h l -> p (h l)")[:, 0:256]
                    TTn("dve", tmp.rearrange("p (h d) -> p h d", d=64), ps_yo.rearrange("p (h d) -> p h d", d=64), bc(dtf["ea"][:, q, hs], 2, [128, 4, 64]), ALU.mult,
                        [("aux1", 0), "dt_ea", smp + "t"], [smp + "t"])
                    TTn("dve", sm["y"][:], sm["y"][:], ps_y, ALU.add, [smp + "y", ("aux0", 1)], [smp + "y"])
                    TTn("dve", sm["y"][:], sm["y"][:], tmp, ALU.add, [smp + "y", smp + "t"], [smp + "y"])
                    for i in range(2):
                        TR(ps_yT[:, i * 128:(i + 1) * 128], sm["y"][:, i * 128:(i + 1) * 128], identf[:], [smp + "y", "identf"], [("aux1", 1)])
                    for i in range(2):
                        CP("act", yT[i][:, qs], ps_yT[:, i * 128:(i + 1) * 128], [("aux1", 1)], ["f4_%d" % (6 + i)])
                stop(11)
                if not main:
                    continue
                gt, gtok = nextbig()
                for i in range(2):
                    TTn("dve", yT[i][:, 0:TT], yT[i][:, 0:TT], zs[i][:, 0:TT], ALU.mult, ["f4_%d" % (6 + i), "f4_%d" % (4 + i)], ["f4_%d" % (6 + i)])
                    ACT(sqg[:], yT[i][:, 0:TT], AF.Square, ["f4_%d" % (6 + i)], ["h2_2"])
                    for hf in range(NH):
                        MM(gt[:, hf * HW:(hf + 1) * HW], onesb[:], sqg[:, hf * HW:(hf + 1) * HW], i == 0, i == 1, ["onesb", "h2_2"], [gtok[hf * HW // 512]])
                rstd_from(gt[:, 0:TT], gtok, 256, f4[0][:, 0:TT], "f4_0")
                for i in range(2):
                    STT(vT[:, g * 2 + i, :], yT[i][:, 0:TT], vec["snorm"][:, g * 2 + i:g * 2 + i + 1], f4[0][:, 0:TT], ALU.mult, ALU.mult,
                        ["f4_%d" % (6 + i), "f4_0", "v_snorm"], [("vT", g * 2 + i)])
            stop(9)

        def phaseB(t0):
            stop(4)
            S.label = 'B1'
            vtoks = [("vT", k) for k in range(KM)]
            for cc in range(KD):
                bt, btok = nextbig()
                nkk = (KM + KD - 1) // KD
                wl = []
                for part in range(nkk):
                    k0 = part * KD
                    nk = min(KD, KM - k0)
                    wl.append((load_w_chunk(w_out[k0 * 128:(k0 + nk) * 128, cc * 128:(cc + 1) * 128], nk), k0, nk))
                for hf in range(NH):
                    for (wb, wtok), k0, nk in wl:
                        for k in range(nk):
                            MM(bt[:, hf * HW:(hf + 1) * HW], wb[:, k, :], vT[:, k0 + k, hf * HW:(hf + 1) * HW], k0 + k == 0, k0 + k == KM - 1,
                               [wtok] + vtoks, [btok[hf * HW // 512]])
                st, stok = f4[cc % 2], "f4_%d" % (cc % 2)
                sq, sqtok = h2[cc % 2], "h2_%d" % (cc % 2)
                CP("act", st[:, 0:TT], bt[:, 0:TT], btok, [stok])
                ACT(sq[:], bt[:, 0:TT], AF.Square, btok, [sqtok])
                for hf in range(NH):
                    MM(aux0[:, hf * HW:(hf + 1) * HW], onesb[:], sq[:, hf * HW:(hf + 1) * HW], cc == 0, cc == KD - 1, ["onesb", sqtok], [("aux0", hf * HW // 512)])
                DMA("sp", msT[cc], st[:, 0:TT], [stok], [("msT", cc)], stok + "s")
            r1 = f4[7]
            rstd_from(aux0[:, 0:TT], auxt("aux0"), D, r1[:, 0:TT], "f4_7")
            stop(5)
            S.label = 'B2'
            for cc in range(KD):
                ms, mstok = f4[cc % 2], "f4_%d" % (cc % 2)
                xb, xbtok = f4[2 + cc % 2], "f4_%d" % (2 + cc % 2)
                sq, sqtok = h2[cc % 2], "h2_%d" % (cc % 2)
                DMA("sp", ms[:, 0:TT], msT[cc], [("msT", cc)], [mstok], mstok)
                DMA("sp", xb[:, 0:TT].rearrange("p (j m) -> p j m", m=128),
                    xm[t0:t0 + TT, cc * 128:(cc + 1) * 128].rearrange("(j p) m -> p j m", p=128), [], [xbtok], xbtok)
                bt, btok = nextbig()
                for j in range(NQ):
                    TR(bt[:, j * 128:(j + 1) * 128], xb[:, j * 128:(j + 1) * 128], identf[:], [xbtok, "identf"], [btok[(j * 128) // 512]])
                STT(ms[:, 0:TT], ms[:, 0:TT], vec["g_post"][:, cc:cc + 1], r1[:, 0:TT], ALU.mult, ALU.mult, [mstok, "f4_7", "v_g_post"], [mstok])
                TTn("dve", ms[:, 0:TT], ms[:, 0:TT], bt[:, 0:TT], ALU.add, [mstok] + btok, [mstok])
                DMA("sp", x1s[cc], ms[:, 0:TT], [mstok], [("x1s", cc)], mstok + "s")
                ACT(sq[:], ms[:, 0:TT], AF.Square, [mstok], [sqtok])
                for hf in range(NH):
                    MM(aux1[:, hf * HW:(hf + 1) * HW], onesb[:], sq[:, hf * HW:(hf + 1) * HW], cc == 0, cc == KD - 1, ["onesb", sqtok], [("aux1", hf * HW // 512)])
            r2 = f4[6]
            rstd_from(aux1[:, 0:TT], auxt("aux1"), D, r2[:, 0:TT], "f4_6")
            stop(6)
            S.label = 'B3'
            for cc in range(KD):
                xb, xbtok = f4[cc % 2], "f4_%d" % (cc % 2)
                DMA("sp", xb[:, 0:TT], x1s[cc], [("x1s", cc)], [xbtok], xbtok)
                STT(hT[:, cc, :], xb[:, 0:TT], vec["g_mlp"][:, cc:cc + 1], r2[:, 0:TT], ALU.mult, ALU.mult, [xbtok, "f4_6", "v_g_mlp"],
                    [("hT", hf) for hf in range(NH)])
            stop(7)
            S.label = 'MLP'
            KB = c.FC // 128
            htoks = [("hT", hf) for hf in range(NH)]
            for fb in range(c.DFF // c.FC):
                for f in range(KB):
                    wb, wtok = load_w_chunk(w1[:, fb * c.FC + f * 128:fb * c.FC + (f + 1) * 128])
                    bt, btok = nextbig()
                    for hf in range(NH):
                        for k in range(KD):
                            MM(bt[:, hf * HW:(hf + 1) * HW], wb[:, k, :], hT[:, k, hf * HW:(hf + 1) * HW], k == 0, k == KD - 1, [wtok] + htoks, [btok[hf * HW // 512]])
                    rl, rltok = f4[2 + f % 2], "f4_%d" % (2 + f % 2)
                    ACT(rl[:, 0:TT], bt[:, 0:TT], AF.Relu, btok, [rltok])
                    TTn("dve", actT[:, f, :], rl[:, 0:TT], rl[:, 0:TT], ALU.mult, [rltok], [("act", f), "ovl", "ovlb"] + ALT)
                for cc in range(KD):
                    wb, wtok = load_w_chunk(w2[fb * c.FC:(fb + 1) * c.FC, cc * 128:(cc + 1) * 128], KB)
                    bt, btok = nextbig()
                    for hf in range(NH):
                        for k in range(KB):
                            MM(bt[:, hf * HW:(hf + 1) * HW], wb[:, k, :], actT[:, k, hf * HW:(hf + 1) * HW], k == 0, k == KB - 1,
                               [wtok, ("act", k)], [btok[hf * HW // 512]])
                    if fb == 0:
                        CP("act", accT[:, cc, :], bt[:, 0:TT], btok + vtoks, [("acc", cc)] + vtoks)
                    else:
                        TTn("dve", accT[:, cc, :], accT[:, cc, :], bt[:, 0:TT], ALU.add, btok + [("acc", cc)], [("acc", cc)])
            stop(8)
            S.label = 'B4'
            for cc in range(KD):
                sq, sqtok = h2[cc % 2], "h2_%d" % (cc % 2)
                ACT(sq[:], accT[:, cc, :], AF.Square, [("acc", cc)] + vtoks, [sqtok])
                for hf in range(NH):
                    MM(aux0[:, hf * HW:(hf + 1) * HW], onesb[:], sq[:, hf * HW:(hf + 1) * HW], cc == 0, cc == KD - 1, ["onesb", sqtok], [("aux0", hf * HW // 512)])
            r3 = f4[7]
            rstd_from(aux0[:, 0:TT], auxt("aux0"), D, r3[:, 0:TT], "f4_7")
            for cc in range(KD):
                xb, xbtok = f4[cc % 2], "f4_%d" % (cc % 2)
                ob, obtok = f4[2 + cc % 2], "f4_%d" % (2 + cc % 2)
                os_, ostok = f4[4 + cc % 2], "f4_%d" % (4 + cc % 2)
                DMA("sp", xb[:, 0:TT], x1s[cc], [("x1s", cc)], [xbtok], xbtok)
                STT(ob[:, 0:TT], accT[:, cc, :], vec["g_pmlp"][:, cc:cc + 1], r3[:, 0:TT], ALU.mult, ALU.mult, [("acc", cc), "f4_7", "v_g_pmlp"] + vtoks, [obtok])
                TTn("dve", ob[:, 0:TT], ob[:, 0:TT], xb[:, 0:TT], ALU.add, [obtok, xbtok], [obtok])
                bt, btok = nextbig()
                for j in range(NQ):
                    TR(bt[:, j * 128:(j + 1) * 128], ob[:, j * 128:(j + 1) * 128], identf[:], [obtok, "identf"], [btok[(j * 128) // 512]])
                CP("act", os_[:, 0:TT], bt[:, 0:TT], btok, [ostok])
                DMA("sp", out[t0:t0 + TT, cc * 128:(cc + 1) * 128].rearrange("(j p) m -> p j m", p=128),
                    os_[:, 0:TT].rearrange("p (j m) -> p j m", m=128), [ostok], [("out", t0, cc)], ostok + "o")
                S.out_tokens.append(("out", t0, cc))

        def _program():
            for ti in range(c.NPRE):
                phaseA(xp, ti * TT, False, last_prefix=(ti == c.NPRE - 1))
            alltok = [("prev", g) for g in range(G)]
            TS("dve", prev[:], prev[:], flag[:, 0:1], None, ALU.mult, None, alltok + ["flag"], alltok)
            TS("dve", hcar[:], hcar[:], flag[:, 0:1], None, ALU.mult, None, ["hcar", "flag"], ["hcar"])
            for ti in range(c.NMAIN):
                phaseA(xm, ti * TT, True)
                phaseB(ti * TT)

        S.out_tokens = []

        class _Stop(Exception):
            pass

        def stop(level):
            if getattr(cfg, "STOP", 0) == level:
                raise _Stop()

        try:
            _program()
        except _Stop:
            pass
        S.op("dve", lambda e: e.engine_nop(), S.out_tokens, [])
        S.emit(reorder=getattr(cfg, "REORDER", True))
        nc._sched = S
    return nc


def _vec_layouts(cfg, p):
    c = cfg
    pm = lambda v, n: np.ascontiguousarray(np.asarray(v, np.float32).reshape(n, 128).T)
    bcast = lambda v: np.ascontiguousarray(np.broadcast_to(np.asarray(v, np.float32)[None, :], (128, len(v))))
    cw = lambda w, n: np.ascontiguousarray(np.asarray(w, np.float32).reshape(4, n, 128).transpose(2, 1, 0).reshape(128, n * 4))
    return dict(
        g_pre=pm(p["pre_mix_norm"], c.KD), g_post=pm(p["post_mix_norm"], c.KD), g_mlp=pm(p["pre_mlp_norm"], c.KD),
        g_pmlp=pm(p["post_mlp_norm"], c.KD),
        cws=cw(p["ssd_conv_w"], c.KX), cbs=pm(p["ssd_conv_b"], c.KX), cwl=cw(p["lru_conv_w"], c.NBLK), cbl=pm(p["lru_conv_b"], c.NBLK),
        lba=pm(p["lru_b_a"], c.NBLK), lbx=pm(p["lru_b_x"], c.NBLK), lam=pm(p["lru_lambda"], c.NBLK), lnorm=pm(p["lru_norm"], c.NBLK),
        snorm=pm(p["ssd_norm"], c.KS), dtb=bcast(p["ssd_dt_bias"]), alog=bcast(p["ssd_a_log"]), dsk=bcast(p["ssd_d"]))


def make_in_maps(cfg, inputs, n_batch, n_half):
    c = cfg
    p = {k: np.asarray(v)[0] for k, v in inputs.items() if k != "x"}
    x = np.asarray(inputs["x"], np.float32)
    vl = _vec_layouts(c, p)
    half = c.NMAIN * c.TT
    shared = dict(w_in=np.ascontiguousarray(p["w_in"], np.float32), w_out=np.ascontiguousarray(p["w_out"], np.float32),
                  w1=np.ascontiguousarray(p["w_mlp_in"], np.float32), w2=np.ascontiguousarray(p["w_mlp_out"], np.float32),
                  lw=np.ascontiguousarray(np.concatenate([np.asarray(p["lru_w_a"], np.float32), np.asarray(p["lru_w_x"], np.float32)], axis=2)), **vl)
    maps = []
    for b in range(n_batch):
        for s in range(n_half):
            m = dict(shared)
            m["xm"] = np.ascontiguousarray(x[b, s * half:(s + 1) * half])
            pre = np.zeros((max(c.NPRE, 1) * c.TT, c.D), np.float32)
            if s > 0:
                pre[:] = x[b, (s - 1) * half:s * half][-pre.shape[0]:]
            m["xp"] = pre
            m["flag"] = np.full((128, 1), 1.0 if s > 0 else 0.0, np.float32)
            maps.append(m)
    return maps


def kernel(**inputs):
    cfg = Cfg()
    nc = build(cfg)
    maps = make_in_maps(cfg, inputs, 4, 2)
    res = run_bass_kernel_spmd(nc, maps, core_ids=list(range(8)))
    x = np.asarray(inputs["x"])
    outp = np.empty(x.shape, np.float32)
    half = cfg.NMAIN * cfg.TT
    i = 0
    for b in range(4):
        for s in range(2):
            outp[b, s * half:(s + 1) * half] = np.asarray(res.results[i]["out"], np.float32)
            i += 1
    return outp
```
